# Optimizing a Trainium2 kernel written in Bass

```python
import math
import jax, jax.numpy as jnp
from jax import lax
import numpy as np

D_MODEL = 2048
BATCH = 4
SEQ = 2048
DEPTH = 1

SSM_WIDTH = D_MODEL // 2
SSM_GROUP = 16
SSM_GROUPS = SSM_WIDTH // SSM_GROUP
SSM_STATE = 64
DT_MIN = 0.001
DT_MAX = 0.1
POOL_WIDTH = D_MODEL // 2
POOL_WINDOWS = (2, 4, 8, 16)
POOL_GROUP = POOL_WIDTH // len(POOL_WINDOWS)
N_BRANCH = 2
IN_WIDTH = SSM_WIDTH + POOL_WIDTH + N_BRANCH * D_MODEL
D_FF = 4 * D_MODEL
N_MOD = 6
LN_EPS = 1e-5
ALPHA = (2.0 * DEPTH) ** 0.25
BETA = (8.0 * DEPTH) ** -0.25

kernel_name = 'hybrid_s5_pool_gated_postnorm_block'


def _layernorm(x, gain=None, bias=None):
    xf = x.astype(jnp.float32)
    mu = jnp.mean(xf, axis=-1, keepdims=True)
    var = jnp.mean(jnp.square(xf - mu), axis=-1, keepdims=True)
    y = (xf - mu) * lax.rsqrt(var + LN_EPS)
    if gain is not None:
        y = y * gain.astype(jnp.float32) + bias.astype(jnp.float32)
    return y.astype(x.dtype)


def _modulate(x, shift, scale):
    return _layernorm(x) * (1.0 + scale) + shift


def _s5_branch(u, lam_re, lam_im, log_dt, b_re, b_im, c_re, c_im, d_skip, w_val, w_gate):
    f32 = jnp.float32
    bsz, seq, _ = u.shape
    uf = u.astype(f32).reshape(bsz, seq, SSM_GROUPS, SSM_GROUP)
    lr = lam_re.astype(f32)
    li = lam_im.astype(f32)
    dt = jnp.exp(log_dt.astype(f32))[:, None]
    mag = jnp.exp(lr * dt)
    ang = li * dt
    ab_re = mag * jnp.cos(ang)
    ab_im = mag * jnp.sin(ang)
    num_re = ab_re - 1.0
    num_im = ab_im
    den = lr * lr + li * li
    f_re = (num_re * lr + num_im * li) / den
    f_im = (num_im * lr - num_re * li) / den
    br = b_re.astype(f32)
    bi = b_im.astype(f32)
    bb_re = f_re[..., None] * br - f_im[..., None] * bi
    bb_im = f_re[..., None] * bi + f_im[..., None] * br
    bu_re = jnp.einsum('bsgh,gph->bsgp', uf, bb_re)
    bu_im = jnp.einsum('bsgh,gph->bsgp', uf, bb_im)
    a_re = jnp.broadcast_to(ab_re, bu_re.shape)
    a_im = jnp.broadcast_to(ab_im, bu_im.shape)

    def combine(e1, e2):
        a1r, a1i, b1r, b1i = e1
        a2r, a2i, b2r, b2i = e2
        return (a2r * a1r - a2i * a1i,
                a2r * a1i + a2i * a1r,
                a2r * b1r - a2i * b1i + b2r,
                a2r * b1i + a2i * b1r + b2i)

    _, _, xs_re, xs_im = lax.associative_scan(combine, (a_re, a_im, bu_re, bu_im), axis=1)
    y = (jnp.einsum('bsgp,ghp->bsgh', xs_re, c_re.astype(f32))
         - jnp.einsum('bsgp,ghp->bsgh', xs_im, c_im.astype(f32))
         + d_skip.astype(f32).reshape(SSM_GROUPS, SSM_GROUP) * uf)
    y = y.reshape(bsz, seq, SSM_WIDTH).astype(u.dtype)
    z = jax.nn.gelu(y)
    return (z @ w_val) * jax.nn.sigmoid(z @ w_gate)


def _pool_branch(u, w_pool, pool_scale, w_pool_out):
    f32 = jnp.float32
    seq = u.shape[1]
    uf = u.astype(f32)
    cs = jnp.pad(jnp.cumsum(uf, axis=1), ((0, 0), (1, 0), (0, 0)))
    pos = jnp.arange(seq, dtype=f32)
    outs = []
    for gi, w in enumerate(POOL_WINDOWS):
        lo_c, hi_c = gi * POOL_GROUP, (gi + 1) * POOL_GROUP
        csg = cs[..., lo_c:hi_c]
        hi = csg[:, 1:]
        lo = jnp.pad(csg, ((0, 0), (w - 1, 0), (0, 0)))[:, :seq]
        count = jnp.minimum(pos + 1.0, float(w))[None, :, None]
        pooled = (hi - lo) / count - uf[..., lo_c:hi_c]
        outs.append(jnp.einsum('bsc,cd->bsd', pooled, w_pool[gi].astype(f32)))
    y = jnp.concatenate(outs, axis=-1) * pool_scale.astype(f32)
    return y.astype(u.dtype) @ w_pool_out


def _hybrid_mixer(h, w_in, lam_re, lam_im, log_dt, b_re, b_im, c_re, c_im, d_skip,
                  w_glu_val, w_glu_gate, w_pool, pool_scale, w_pool_out, w_out):
    proj = h @ w_in
    u_ssm = proj[..., :SSM_WIDTH]
    u_pool = proj[..., SSM_WIDTH:SSM_WIDTH + POOL_WIDTH]
    g_a = proj[..., SSM_WIDTH + POOL_WIDTH:SSM_WIDTH + POOL_WIDTH + D_MODEL]
    g_b = proj[..., SSM_WIDTH + POOL_WIDTH + D_MODEL:]
    y_a = _s5_branch(u_ssm, lam_re, lam_im, log_dt, b_re, b_im, c_re, c_im, d_skip,
                     w_glu_val, w_glu_gate)
    y_b = _pool_branch(u_pool, w_pool, pool_scale, w_pool_out)
    merged = jax.nn.sigmoid(g_a) * y_a + jax.nn.sigmoid(g_b) * y_b
    return merged @ w_out


def _sq_relu_mlp(h, w_ff1, w_ff2):
    return jnp.square(jax.nn.relu(h @ w_ff1)) @ w_ff2


def setup_inputs(seed: int = 0) -> dict:
    key = jax.random.key(seed)
    ks = jax.random.split(key, 26)
    f32 = jnp.float32
    L = DEPTH

    def nrm(k, shape, scale):
        return jax.random.normal(k, shape, f32) * scale

    n_idx = jnp.arange(SSM_STATE, dtype=f32)
    return {
        'x': nrm(ks[0], (BATCH, SEQ, D_MODEL), 1.0),
        'c': nrm(ks[1], (BATCH, D_MODEL), 1.0),
        'w_ada': nrm(ks[2], (L, D_MODEL, N_MOD * D_MODEL), 0.5 * D_MODEL ** -0.5),
        'b_ada': nrm(ks[3], (L, N_MOD * D_MODEL), 0.02),
        'w_in': nrm(ks[4], (L, D_MODEL, IN_WIDTH), D_MODEL ** -0.5),
        'lam_re': -0.5 + nrm(ks[5], (L, SSM_GROUPS, SSM_STATE), 0.01),
        'lam_im': math.pi * n_idx + nrm(ks[6], (L, SSM_GROUPS, SSM_STATE), 0.01),
        'log_dt': jax.random.uniform(ks[7], (L, SSM_GROUPS), f32,
                                     math.log(DT_MIN), math.log(DT_MAX)),
        'ssm_b_re': nrm(ks[8], (L, SSM_GROUPS, SSM_STATE, SSM_GROUP), (2.0 * SSM_GROUP) ** -0.5),
        'ssm_b_im': nrm(ks[9], (L, SSM_GROUPS, SSM_STATE, SSM_GROUP), (2.0 * SSM_GROUP) ** -0.5),
        'ssm_c_re': nrm(ks[10], (L, SSM_GROUPS, SSM_GROUP, SSM_STATE), (2.0 * SSM_STATE) ** -0.5),
        'ssm_c_im': nrm(ks[11], (L, SSM_GROUPS, SSM_GROUP, SSM_STATE), (2.0 * SSM_STATE) ** -0.5),
        'ssm_d': nrm(ks[12], (L, SSM_WIDTH), 1.0),
        'w_glu_val': nrm(ks[13], (L, SSM_WIDTH, D_MODEL), BETA * SSM_WIDTH ** -0.5),
        'w_glu_gate': nrm(ks[14], (L, SSM_WIDTH, D_MODEL), SSM_WIDTH ** -0.5),
        'w_pool': nrm(ks[15], (L, len(POOL_WINDOWS), POOL_GROUP, POOL_GROUP), POOL_GROUP ** -0.5),
        'pool_scale': 1.0 + nrm(ks[16], (L, POOL_WIDTH), 0.1),
        'w_pool_out': nrm(ks[17], (L, POOL_WIDTH, D_MODEL), BETA * POOL_WIDTH ** -0.5),
        'w_out': nrm(ks[18], (L, D_MODEL, D_MODEL), BETA * D_MODEL ** -0.5),
        'ln1_g': 1.0 + nrm(ks[19], (L, D_MODEL), 0.02),
        'ln1_b': nrm(ks[20], (L, D_MODEL), 0.02),
        'w_ff1': nrm(ks[21], (L, D_MODEL, D_FF), D_MODEL ** -0.5),
        'w_ff2': nrm(ks[22], (L, D_FF, D_MODEL), BETA * D_FF ** -0.5),
        'ln2_g': 1.0 + nrm(ks[23], (L, D_MODEL), 0.02),
        'ln2_b': nrm(ks[24], (L, D_MODEL), 0.02),
    }


def reference(x, c, w_ada, b_ada, w_in, lam_re, lam_im, log_dt, ssm_b_re, ssm_b_im,
              ssm_c_re, ssm_c_im, ssm_d, w_glu_val, w_glu_gate, w_pool, pool_scale,
              w_pool_out, w_out, ln1_g, ln1_b, w_ff1, w_ff2, ln2_g, ln2_b):
    c_act = jax.nn.silu(c)
    for l in range(DEPTH):
        mod = (c_act @ w_ada[l] + b_ada[l])[:, None, :]
        sh1, sc1, g1, sh2, sc2, g2 = jnp.split(mod, N_MOD, axis=-1)
        h = _modulate(x, sh1, sc1)
        y = _hybrid_mixer(h, w_in[l], lam_re[l], lam_im[l], log_dt[l], ssm_b_re[l], ssm_b_im[l],
                          ssm_c_re[l], ssm_c_im[l], ssm_d[l], w_glu_val[l], w_glu_gate[l],
                          w_pool[l], pool_scale[l], w_pool_out[l], w_out[l])
        x = _layernorm(ALPHA * x + g1 * y, ln1_g[l], ln1_b[l])
        h = _modulate(x, sh2, sc2)
        y = _sq_relu_mlp(h, w_ff1[l], w_ff2[l])
        x = _layernorm(ALPHA * x + g2 * y, ln2_g[l], ln2_b[l])
    return x
```

```python
import numpy as np
from contextlib import ExitStack
import concourse.bass as bass
import concourse.mybir as mybir
from concourse.bass_utils import run_bass_kernel_spmd

F32 = mybir.dt.float32
BF16 = mybir.dt.bfloat16
I32 = mybir.dt.int32
ALU = mybir.AluOpType
AF = mybir.ActivationFunctionType
AX = mybir.AxisListType

D = 2048
T = 1024
NCH = 128
L = 8
DFF = 8192
ALPHA = float(2.0 ** 0.25)
EPS = 1e-5
TWO_PI = float(2.0 * np.pi)
SIN_SCALE = TWO_PI * 0.9999995
GELU_C = 1.5957691216057308

ENGS = ("pe", "act", "dve", "pool", "sp")
CENGS = ("pe", "act", "dve", "pool")


class Op:
    __slots__ = ("kind", "eng", "idx", "sem", "val")

    def __init__(self, kind, eng=None, idx=None, sem=None, val=None):
        self.kind = kind
        self.eng = eng
        self.idx = idx
        self.sem = sem
        self.val = val


class K:
    def __init__(self, nc, es, n_dma_sems=28):
        self.nc = nc
        self.es = es
        self.prog = {e: [] for e in ENGS}
        self.csem = {e: es.enter_context(nc.semaphore("cs_" + e)) for e in CENGS}
        self.cnt = {e: 0 for e in ENGS}
        self.waited = {e: {p: 0 for p in ENGS} for e in ENGS}
        self.dsems = [es.enter_context(nc.semaphore("ds%d" % i)) for i in range(n_dma_sems)]
        self.dval = [0] * n_dma_sems
        self.dlast = [None] * n_dma_sems
        self.dnext = 0
        self.dnext_sw = 0
        self.n_hw = n_dma_sems - 6
        self.dwaited = {e: [0] * n_dma_sems for e in ENGS}
        self.nsb = 0

    def sb(self, shape, dtype, name=None):
        self.nsb += 1
        return self.es.enter_context(self.nc.sbuf_tensor("s_" + (name or ("sb%d" % self.nsb)), list(shape), dtype))

    def ps(self, shape, dtype, name=None):
        self.nsb += 1
        return self.es.enter_context(self.nc.psum_tensor(name or ("ps%d" % self.nsb), list(shape), dtype))

    def _waits(self, e, deps):
        out = []
        for d in deps:
            if d is None:
                continue
            if isinstance(d, (list, tuple)):
                out += self._waits(e, d)
                continue
            if d.kind == "c":
                if self.waited[e][d.eng] < d.idx:
                    self.waited[e][d.eng] = d.idx
                    out.append((self.csem[d.eng], d.idx))
            else:
                if self.dwaited[e][d.sem] < d.val:
                    self.dwaited[e][d.sem] = d.val
                    out.append((self.dsems[d.sem], d.val))
        return out

    def op(self, e, fn, deps=(), signal=True):
        ws = self._waits(e, deps)
        o = None
        sem = None
        if signal:
            self.cnt[e] += 1
            o = Op("c", eng=e, idx=self.cnt[e])
            sem = self.csem[e]

        def run(engine, ws=ws, fn=fn, sem=sem):
            for (s, v) in ws:
                engine.wait_ge(s, v)
            ins = fn(engine)
            if sem is not None:
                ins.then_inc(sem, 1)

        self.prog[e].append(run)
        return o

    def dma(self, q, out, in_, deps=(), **kw):
        if q == "pool":
            slot = self.n_hw + self.dnext_sw
            self.dnext_sw = (self.dnext_sw + 1) % (len(self.dsems) - self.n_hw)
        else:
            slot = self.dnext
            self.dnext = (self.dnext + 1) % self.n_hw
        deps = list(deps)
        if self.dlast[slot] is not None:
            deps.append(self.dlast[slot])
        ws = self._waits(q, deps)
        self.dval[slot] += 16
        o = Op("d", sem=slot, val=self.dval[slot])
        self.dlast[slot] = o
        s = self.dsems[slot]

        def run(engine, ws=ws, s=s, out=out, in_=in_, kw=kw):
            for (ss, v) in ws:
                engine.wait_ge(ss, v)
            engine.dma_start(out=out, in_=in_, **kw).then_inc(s, 16)

        self.prog[q].append(run)
        return o

    def wait_all(self, e, deps):
        ws = self._waits(e, deps)

        def run(engine, ws=ws):
            for (s, v) in ws:
                engine.wait_ge(s, v)

        self.prog[e].append(run)

    def last(self, e):
        return Op("c", eng=e, idx=self.cnt[e]) if self.cnt[e] > 0 else None

    def barrier(self):
        deps = [self.last(e) for e in CENGS] + [o for o in self.dlast if o is not None]
        for e in ENGS:
            self.wait_all(e, deps)

    def build(self):
        nc = self.nc
        with nc.Block() as block:
            @block.tensor
            def _(eng):
                for f in self.prog["pe"]:
                    f(eng)

            @block.scalar
            def _(eng):
                for f in self.prog["act"]:
                    f(eng)

            @block.vector
            def _(eng):
                for f in self.prog["dve"]:
                    f(eng)

            @block.gpsimd
            def _(eng):
                for f in self.prog["pool"]:
                    f(eng)

            @block.sync
            def _(eng):
                for f in self.prog["sp"]:
                    f(eng)


class Banks:
    def __init__(self, k):
        self.k = k
        self.t = [k.ps([128, 512], F32, name="bank%d" % i) for i in range(8)]
        self.free_op = [None] * 8
        self.nxt = 0

    def get(self):
        i = self.nxt
        self.nxt = (self.nxt + 1) % 8
        return i, self.free_op[i]

    def rel(self, i, op):
        self.free_op[i] = op

    def f32(self, i):
        return self.t[i][:]

    def bf(self, i):
        return self.t[i].bitcast(BF16)[:]


class WStream:
    def __init__(self, k, nbuf):
        self.k = k
        self.bufs = [k.sb([128, 16, 512], BF16, name="wb%d" % i) for i in range(nbuf)]
        self.rel_op = [None] * nbuf
        self.nxt = 0

    def load(self, parts):
        i = self.nxt
        self.nxt = (self.nxt + 1) % len(self.bufs)
        buf = self.bufs[i]
        ops = []
        for (dst_fn, src) in parts:
            ops.append(self.k.dma("pool", dst_fn(buf), src, deps=[self.rel_op[i]]))
        return buf, ops, i

    def release(self, i, op):
        self.rel_op[i] = op


def wsrc(w, r0, nrows, c0, ncols):
    return w[r0:r0 + nrows, c0:c0 + ncols].rearrange("(kc p) n -> p kc n", p=128)


def build(debug=None):
    nc = bass.Bass("TRN2", target_bir_lowering=False)

    def din(name, shape, dt=F32):
        return nc.dram_tensor(name, list(shape), dt, kind="ExternalInput").ap()

    x_own = din("x_own", [T, D])
    x_prev = din("x_prev", [T, D])
    c_row = din("c_row", [D])
    flag_d = din("flag", [128, 1])
    w_ada = din("w_ada", [D, 6 * D])
    b_ada = din("b_ada", [1, 6 * D])
    w_in = din("w_in", [D, 6144])
    lam_re = din("lam_re", [64, 64])
    lam_im = din("lam_im", [64, 64])
    log_dt = din("log_dt", [64])
    sb_re = din("ssm_b_re", [64, 64, 16])
    sb_im = din("ssm_b_im", [64, 64, 16])
    sc_re = din("ssm_c_re", [64, 16, 64])
    sc_im = din("ssm_c_im", [64, 16, 64])
    ssm_d = din("ssm_d", [1024])
    w_val = din("w_glu_val", [1024, D])
    w_gate = din("w_glu_gate", [1024, D])
    w_pool = din("w_pool", [1024, 256])
    pool_scale = din("pool_scale", [1024])
    w_pool_out = din("w_pool_out", [1024, D])
    w_out = din("w_out", [D, D])
    ln1_g = din("ln1_g", [1, D])
    ln1_b = din("ln1_b", [1, D])
    w_ff1 = din("w_ff1", [D, DFF])
    w_ff2 = din("w_ff2", [DFF, D])
    ln2_g = din("ln2_g", [1, D])
    ln2_b = din("ln2_b", [1, D])
    identf_d = din("identf", [128, 128])
    maskT_d = din("maskT", [128, 128])
    kvec_d = din("kvec", [128, 24])
    iota_d = din("iota", [128, 129])
    rc_d = din("rc", [4, T])
    out_d = nc.dram_tensor("out", [T, D], F32, kind="ExternalOutput").ap()
    dbg_d = None
    if debug is not None:
        dbg_d = nc.dram_tensor("dbg", [128, 16384], F32, kind="ExternalOutput").ap()

    es = ExitStack()
    with es:
        k = K(nc, es)
        banks = Banks(k)
        ws = WStream(k, 2)

        HT = k.sb([128, 16, 1024], BF16, name="HT")
        BIG = k.sb([128, 45056], BF16, name="BIG")
        RB = k.sb([128, 17408], BF16, name="RB")
        identf = k.sb([128, 128], F32, name="identf")
        identb = k.sb([128, 128], BF16, name="identb")
        maskT = k.sb([128, 128], F32, name="maskT")
        kvec = k.sb([128, 24], F32, name="kvec")
        iota = k.sb([128, 129], F32, name="iota")
        flag = k.sb([128, 1], F32, name="flag")
        crep = k.sb([128, 16, 128], BF16, name="crep")
        modA = k.sb([128, 4, 16], F32, name="modA")
        sc1p = k.sb([128, 16], F32, name="sc1p")
        G2 = k.sb([128, 16], F32, name="G2")
        B2 = k.sb([128, 16], F32, name="B2")
        lnA = k.sb([128, 2, 16], F32, name="lnA")
        pscale = k.sb([128, 8], F32, name="pscale")
        th8n = k.sb([128, 32], F32, name="th8n")
        m8 = k.sb([128, 32], F32, name="m8")
        tail = k.sb([128, 16, 16], BF16, name="tail")
        xinit = k.sb([128, 2, 32], F32, name="xinit")
        stat = k.sb([128, 4, 6], F32, name="stat")
        mv = k.sb([128, 8, 2], F32, name="mv")
        rstd = k.sb([128, 8], F32, name="rstd")
        nmr = k.sb([128, 8], F32, name="nmr")
        sdv = k.sb([128, 8], F32, name="sdv")

        def carve(reg, off, n, dt):
            v = reg[:, off // 2:(off + n) // 2]
            return v.bitcast(dt) if dt != BF16 else v

        KB = 1024
        MinT = carve(BIG, 40 * KB, 16 * KB, BF16).rearrange("p (g r m) -> p g r m", g=32, r=2)
        Tsb = carve(BIG, 56 * KB, 16 * KB, BF16).rearrange("p (g n) -> p g n", g=64)
        Cre = carve(BIG, 72 * KB, 8 * KB, BF16).rearrange("p (g n) -> p g n", g=32)
        Cimn = carve(BIG, 80 * KB, 8 * KB, BF16).rearrange("p (g n) -> p g n", g=32)
        TC = carve(RB, 0, 16512, F32).rearrange("p (g c) -> p g c", g=32)
        TS = carve(RB, 16512, 16512, F32).rearrange("p (g c) -> p g c", g=32)

        dbg_ops = []

        def dump(ap2d, col0, deps):
            n = ap2d.shape[1]
            dbg_ops.append(k.dma("sp", dbg_d[:, col0:col0 + n], ap2d, deps=deps))

        def finish():
            k.barrier()
            k.wait_all("sp", dbg_ops)
            k.build()

        kd = lambda *a_, **kw_: k.dma(*a_, allow_slow_non_contiguous=True, **kw_)
        d_id = kd("sp", identf[:], identf_d)
        d_mk = kd("sp", maskT[:], maskT_d)
        d_kv = kd("sp", kvec[:], kvec_d)
        d_io = kd("sp", iota[:], iota_d)
        d_fl = kd("sp", flag[:], flag_d)
        cT = carve(BIG, 0, 64, F32)
        lr = carve(BIG, 128, 128, F32)
        li = carve(BIG, 256, 128, F32)
        ldt = carve(BIG, 384, 128, F32)
        lll = carve(BIG, 128, 384, F32)
        bre = carve(BIG, 1 * KB, 2 * KB, F32).rearrange("p (g h) -> p g h", g=32)
        bim = carve(BIG, 3 * KB, 2 * KB, F32).rearrange("p (g h) -> p g h", g=32)
        dvec = carve(BIG, 9 * KB, 256, F32)
        Cq_re = carve(BIG, 18 * KB, 2 * KB, F32).rearrange("p (g h) -> p g h", g=32)
        Cq_im = carve(BIG, 20 * KB, 2 * KB, F32).rearrange("p (g h) -> p g h", g=32)
        bA = [carve(RB, 0, 8 * KB, F32), carve(RB, 8 * KB, 8 * KB, F32)]
        cA = [carve(RB, 16 * KB, 8 * KB, F32), carve(RB, 24 * KB, 8 * KB, F32)]
        lamA = [carve(RB, 32 * KB + 512 * i, 512, F32) for i in range(3)]
        ld2 = carve(RB, 32 * KB + 1536, 8, F32)
        vst = [carve(BIG, 35 * KB + 512 * i, 512, F32) for i in range(4)]
        dsm = carve(BIG, 37 * KB, 64, F32)
        dst_ = carve(BIG, 37 * KB + 64, 512, F32)
        d_bA = [kd("sp", bA[0][0:32, :], sb_re.rearrange("(gq g2) p h -> gq (g2 p h)", g2=2)),
                kd("sp", bA[1][0:32, :], sb_im.rearrange("(gq g2) p h -> gq (g2 p h)", g2=2))]
        d_cA = [kd("sp", cA[0][0:32, :], sc_re.rearrange("(gq g2) o p -> gq (g2 o p)", g2=2)),
                kd("sp", cA[1][0:32, :], sc_im.rearrange("(gq g2) o p -> gq (g2 o p)", g2=2))]
        d_lam = [kd("sp", lamA[0][0:32, :], lam_re.rearrange("(gq g2) p -> gq (g2 p)", g2=2)),
                 kd("sp", lamA[1][0:32, :], lam_im.rearrange("(gq g2) p -> gq (g2 p)", g2=2)),
                 kd("sp", ld2[0:32, :], log_dt.rearrange("(gq g2) -> gq g2", g2=2))]
        d_v = [kd("sp", vst[0][0:16, :], c_row.rearrange("(kc p) -> kc p", p=128)),
               kd("sp", vst[1][0:16, :], ln1_g[0].rearrange("(kc p) -> kc p", p=128)),
               kd("sp", vst[2][0:16, :], ln1_b[0].rearrange("(kc p) -> kc p", p=128)),
               kd("sp", vst[3][0:8, :], pool_scale.rearrange("(kc p) -> kc p", p=128))]
        d_ds = kd("sp", dsm[0:64, :], ssm_d.rearrange("(g h) -> g h", h=16))
        o_ldA = k.op("dve", lambda e: e.tensor_copy(out=lamA[2][0:32, :].rearrange("q (a p) -> q a p", a=2), in_=ld2[0:32, :].unsqueeze(2).broadcast_to([32, 2, 64])), [d_lam[2]])
        o_dst = k.op("dve", lambda e: e.tensor_copy(out=dst_[0:64, :].rearrange("q (i h) -> q i h", i=8), in_=dsm[0:64, :].unsqueeze(1).broadcast_to([64, 8, 16])), [d_ds])

        def tr32(bk, col0, ncol, in_ap, npart, deps, sig):
            return k.op("pe", lambda e: e.transpose(out=banks.f32(bk)[:, col0:col0 + ncol], in_=in_ap, identity=identf[0:npart, 0:npart]), deps, signal=sig)

        bk, fr_ = banks.get()
        tr32(bk, 0, 32, lamA[0][0:32, :], 32, [fr_, d_id, d_lam[0]], False)
        tr32(bk, 32, 32, lamA[1][0:32, :], 32, [d_lam[1]], False)
        last = tr32(bk, 64, 32, lamA[2][0:32, :], 32, [o_ldA], True)
        d_lr = k.op("dve", lambda e, bk=bk: e.tensor_copy(out=lll, in_=banks.f32(bk)[:, 0:96]), [last])
        banks.rel(bk, d_lr)
        d_li = d_ld0 = d_ld1 = d_lr
        bk, fr_ = banks.get()
        tr32(bk, 0, 16, vst[0][0:16, :], 16, [fr_, d_v[0]], False)
        tr32(bk, 16, 16, vst[1][0:16, :], 16, [d_v[1]], False)
        tr32(bk, 32, 16, vst[2][0:16, :], 16, [d_v[2]], False)
        tr32(bk, 48, 8, vst[3][0:8, :], 8, [d_v[3]], False)
        last = tr32(bk, 64, 64, dst_[0:64, :], 64, [o_dst], True)
        d_c = k.op("dve", lambda e, bk=bk: e.tensor_copy(out=cT, in_=banks.f32(bk)[:, 0:16]), [last])
        d_lnA0 = k.op("dve", lambda e, bk=bk: e.tensor_copy(out=lnA[:].rearrange("p a b -> p (a b)"), in_=banks.f32(bk)[:, 16:48]), [last])
        d_lnA1 = d_lnA0
        d_ps = k.op("dve", lambda e, bk=bk: e.tensor_copy(out=pscale[:], in_=banks.f32(bk)[:, 48:56]), [last])
        o_dv = k.op("dve", lambda e, bk=bk: e.tensor_copy(out=dvec, in_=banks.f32(bk)[:, 64:128]), [last])
        banks.rel(bk, o_dv)
        d_dv = [o_dv]
        d_bc = []
        HT32 = HT[:].rearrange("p a b -> p (a b)").bitcast(F32)
        for si_, (src, dstv, dd_, is_c) in enumerate( ((bA[0], bre, d_bA[0], False), (bA[1], bim, d_bA[1], False), (cA[0], Cq_re, d_cA[0], True), (cA[1], Cq_im, d_cA[1], True))):
            st2 = HT32[:, si_ * 2048:(si_ + 1) * 2048]
            if is_c:
                pin = src[0:32, :].rearrange("q (g o p) -> q o g p", g=2, o=16)
            else:
                pin = src[0:32, :].rearrange("q (g p h) -> q h g p", g=2, p=64)
            o_pm = k.op("dve", lambda e, st2=st2, pin=pin: e.tensor_copy(out=st2[0:32, :].rearrange("q (h g p) -> q h g p", h=16, g=2), in_=pin), [dd_])
            bk, fr_ = banks.get()
            last = None
            for h in range(16):
                inp = st2[0:32, h * 128:(h + 1) * 128]
                last = tr32(bk, h * 32, 32, inp, 32, [fr_, o_pm] if h == 0 else [], h == 15)
            o = k.op("dve", lambda e, bk=bk, dstv=dstv: e.tensor_copy(out=dstv, in_=banks.f32(bk).rearrange("p (h g) -> p g h", h=16)), [last])
            banks.rel(bk, o)
            d_bc.append(o)
        d_br, d_bi = d_bc[0], d_bc[1]
        d_cst = [d_bc[2], d_bc[3]]

        o_idb = k.op("dve", lambda e: e.tensor_copy(out=identb[:], in_=identf[:]), [d_id])

        cact = carve(BIG, 9 * KB + 256, 64, F32)
        o_silu = k.op("act", lambda e: e.activation(out=cact, in_=cT, func=AF.Silu), [d_c])
        o_crep = k.op("dve", lambda e: e.tensor_copy(out=crep[:], in_=cact.unsqueeze(2).broadcast_to([128, 16, 128])), [o_silu])

        ADA = k.sb([128, 3, 512], F32, name="ADA")
        bada = ADA[:, 0, :]
        modbc = ADA[:, 1, :]
        dtmp = ADA[:, 2, :]
        state = {"bada_free": None, "modbc_free": None}

        def ada_block(n0, dst=None):
            dst = modbc if dst is None else dst
            buf, dops, bi = ws.load([(lambda b: b[:], wsrc(w_ada, 0, D, n0, 512))])
            d_b = k.dma("sp", bada, b_ada[:, n0:n0 + 512].broadcast_to([128, 512]), deps=[state["bada_free"]])
            bk, fr = banks.get()
            last = None
            for kc in range(16):
                last = k.op("pe", lambda e, kc=kc, bk=bk, buf=buf: e.matmul(banks.f32(bk), lhsT=crep[:, kc, :], rhs=buf[:, kc, :],
                                                                            start=(kc == 0), stop=(kc == 15)),
                            [fr, o_crep] + dops if kc == 0 else (), signal=(kc == 15))
            ws.release(bi, last)
            o = k.op("dve", lambda e, bk=bk, dst=dst: e.tensor_tensor(out=dst, in0=banks.f32(bk), in1=bada, op=ALU.add),
                     [last, d_b, state["modbc_free"]])
            banks.rel(bk, o)
            state["bada_free"] = o
            return o

        def ada_vec_a(vi, dst_fn, blks=range(4)):
            outs = []
            for blk in blks:
                o = ada_block(vi * D + blk * 512)
                o2 = k.op("dve", lambda e: e.tensor_tensor(out=dtmp.rearrange("p (a b) -> p a b", a=4),
                                                          in0=modbc.rearrange("p (a b) -> p a b", a=4),
                                                          in1=identf[:].unsqueeze(1).broadcast_to([128, 4, 128]), op=ALU.mult), [o, d_id])
                o3 = k.op("dve", lambda e, blk=blk: e.tensor_reduce(out=dst_fn(blk), in_=dtmp.rearrange("p (a b) -> p a b", a=4),
                                                                    axis=AX.X, op=ALU.add), [o2])
                state["modbc_free"] = o3
                outs.append(o3)
            return outs

        o_sh1 = []
        o_sc1 = []
        pending = []
        for blk_ in range(4):
            pending.append(lambda blk_=blk_: o_sh1.extend(ada_vec_a(0, lambda blk: modA[:, 0, 4 * blk:4 * blk + 4], blks=(blk_,))))
        for blk_ in range(4):
            pending.append(lambda blk_=blk_: o_sc1.extend(ada_vec_a(1, lambda blk: modA[:, 1, 4 * blk:4 * blk + 4], blks=(blk_,))))
        tickc = [0]

        def tick():
            tickc[0] += 1
            if tickc[0] % 10 == 0 and pending:
                pending.pop(0)()

        pending.pop(0)()
        pending.pop(0)()


        SM = lambda i: carve(BIG, 16 * KB + 128 * i, 128, F32)
        dt_t, a_t, th_t, mag1, cos1, sin1, abm1, abim, den, rden, fre, fim, tA, tB, tC, thn = [SM(i) for i in range(16)]
        Cq_re = carve(BIG, 18 * KB, 2 * KB, F32).rearrange("p (g h) -> p g h", g=32)
        Cq_im = carve(BIG, 20 * KB, 2 * KB, F32).rearrange("p (g h) -> p g h", g=32)
        bb_re = carve(BIG, 22 * KB, 2 * KB, F32).rearrange("p (g h) -> p g h", g=32)
        bb_im = carve(BIG, 24 * KB, 2 * KB, F32).rearrange("p (g h) -> p g h", g=32)
        LT = lambda i: carve(BIG, 26 * KB + 1024 * i, 1024, F32).rearrange("p (g k) -> p g k", g=32)
        LTi = carve(BIG, 26 * KB + 1024 * 7, 1024, I32).rearrange("p (g k) -> p g k", g=32)
        smi = carve(BIG, 34 * KB, 128, I32)
        smf = carve(BIG, 34 * KB + 128, 128, F32)
        smr = carve(BIG, 34 * KB + 256, 128, F32)
        HTb = HT[:].rearrange("p a b -> p (a b)")
        Rm_re = HTb[:, 0:4096].rearrange("p (g n) -> p g n", g=32)
        Rm_im = HTb[:, 4096:8192].rearrange("p (g n) -> p g n", g=32)
        Rp_re = HTb[:, 8192:12288].rearrange("p (g n) -> p g n", g=32)
        Rp_im = HTb[:, 12288:16384].rearrange("p (g n) -> p g n", g=32)
        P1 = carve(RB, 0, 16 * KB, F32).rearrange("p (g k h) -> p g k h", g=32, k=8)
        P2 = carve(RB, 16 * KB, 16 * KB, F32).rearrange("p (g k h) -> p g k h", g=32, k=8)

        ch = {"dve": None, "act": None}

        def dv(fn, deps=()):
            o = k.op("dve", fn, list(deps) + [ch["dve"], ch["act"]])
            ch["dve"] = o
            tick()
            return o

        def ac(fn, deps=()):
            o = k.op("act", fn, list(deps) + [ch["dve"], ch["act"]])
            ch["act"] = o
            return o

        def sincos(sin_out, cos_out, turns, ti, tf, fr, shape_fn=lambda v: v):
            for (dst, off) in ((sin_out, 0.0), (cos_out, 0.25)):
                if off != 0.0:
                    dv(lambda e: e.tensor_scalar(out=fr, in0=turns, scalar1=off, scalar2=None, op0=ALU.add))
                    src = fr
                else:
                    src = turns
                dv(lambda e, src=src: e.tensor_copy(out=ti, in_=src))
                dv(lambda e: e.tensor_copy(out=tf, in_=ti))
                dv(lambda e, src=src: e.tensor_tensor(out=fr, in0=src, in1=tf, op=ALU.subtract))
                ac(lambda e, dst=dst: e.activation(out=dst, in_=fr, func=AF.Sin, scale=SIN_SCALE))

        ac(lambda e: e.activation(out=dt_t, in_=ldt, func=AF.Exp), [d_ld0, d_ld1])
        dv(lambda e: e.tensor_tensor(out=a_t, in0=lr, in1=dt_t, op=ALU.mult), [d_lr])
        dv(lambda e: e.tensor_tensor(out=th_t, in0=li, in1=dt_t, op=ALU.mult), [d_li])
        dv(lambda e: e.tensor_scalar(out=thn, in0=th_t, scalar1=1.0 / TWO_PI, scalar2=None, op0=ALU.mult))
        ac(lambda e: e.activation(out=mag1, in_=a_t, func=AF.Exp))
        sincos(sin1, cos1, thn, smi, smf, smr)
        dv(lambda e: e.tensor_tensor(out=abim, in0=mag1, in1=sin1, op=ALU.mult))
        dv(lambda e: e.tensor_tensor(out=abm1, in0=mag1, in1=cos1, op=ALU.mult))
        dv(lambda e: e.tensor_scalar(out=abm1, in0=abm1, scalar1=-1.0, scalar2=None, op0=ALU.add))
        dv(lambda e: e.tensor_tensor(out=tA, in0=lr, in1=lr, op=ALU.mult))
        dv(lambda e: e.tensor_tensor(out=tB, in0=li, in1=li, op=ALU.mult))
        dv(lambda e: e.tensor_tensor(out=den, in0=tA, in1=tB, op=ALU.add))
        dv(lambda e: e.reciprocal(out=rden, in_=den))
        dv(lambda e: e.tensor_tensor(out=tA, in0=abm1, in1=lr, op=ALU.mult))
        dv(lambda e: e.tensor_tensor(out=tB, in0=abim, in1=li, op=ALU.mult))
        dv(lambda e: e.tensor_tensor(out=tA, in0=tA, in1=tB, op=ALU.add))
        dv(lambda e: e.tensor_tensor(out=fre, in0=tA, in1=rden, op=ALU.mult))
        dv(lambda e: e.tensor_tensor(out=tA, in0=abim, in1=lr, op=ALU.mult))
        dv(lambda e: e.tensor_tensor(out=tB, in0=abm1, in1=li, op=ALU.mult))
        dv(lambda e: e.tensor_tensor(out=tA, in0=tA, in1=tB, op=ALU.subtract))
        dv(lambda e: e.tensor_tensor(out=fim, in0=tA, in1=rden, op=ALU.mult))
        dv(lambda e: e.tensor_scalar(out=th8n[:], in0=thn, scalar1=8.0, scalar2=None, op0=ALU.mult))
        ac(lambda e: e.activation(out=m8[:], in_=a_t, func=AF.Exp, scale=8.0))
        bc16 = lambda v: v.unsqueeze(2).broadcast_to([128, 32, 16])
        T1 = carve(RB, 0, 2 * KB, F32).rearrange("p (g h) -> p g h", g=32)
        T2 = carve(RB, 2 * KB, 2 * KB, F32).rearrange("p (g h) -> p g h", g=32)
        dv(lambda e: e.tensor_tensor(out=T1, in0=bre, in1=bc16(fre), op=ALU.mult), [d_br])
        dv(lambda e: e.tensor_tensor(out=T2, in0=bim, in1=bc16(fim), op=ALU.mult), [d_bi])
        dv(lambda e: e.tensor_tensor(out=bb_re, in0=T1, in1=T2, op=ALU.subtract))
        dv(lambda e: e.tensor_tensor(out=T1, in0=bim, in1=bc16(fre), op=ALU.mult))
        dv(lambda e: e.tensor_tensor(out=T2, in0=bre, in1=bc16(fim), op=ALU.mult))
        dv(lambda e: e.tensor_tensor(out=bb_im, in0=T1, in1=T2, op=ALU.add))
        def lamtab(ks):
            kv = kvec[:, ks * 8:ks * 8 + 8].unsqueeze(1).broadcast_to([128, 32, 8])
            b8 = lambda v: v.unsqueeze(2).broadcast_to([128, 32, 8])
            dv(lambda e: e.tensor_tensor(out=LT(2), in0=b8(thn), in1=kv, op=ALU.mult), [d_kv])
            dv(lambda e: e.tensor_tensor(out=LT(3), in0=b8(a_t), in1=kv, op=ALU.mult))
            ac(lambda e: e.activation(out=LT(3), in_=LT(3), func=AF.Exp))
            sincos(LT(4), LT(5), LT(2), LTi, LT(6), LT(0))
            dv(lambda e: e.tensor_tensor(out=LT(0), in0=LT(3), in1=LT(5), op=ALU.mult))
            dv(lambda e: e.tensor_tensor(out=LT(1), in0=LT(3), in1=LT(4), op=ALU.mult))

        bk16 = lambda v: v.unsqueeze(3).broadcast_to([128, 32, 8, 16])
        bj8 = lambda v: v.unsqueeze(2).broadcast_to([128, 32, 8, 16])
        v4 = lambda v: v.rearrange("p g (k h) -> p g k h", k=8)

        def cprod(out_re, out_im, Xre, Xim, neg_im=False):
            dv(lambda e: e.tensor_tensor(out=P1, in0=bj8(Xre), in1=bk16(LT(0)), op=ALU.mult))
            dv(lambda e: e.tensor_tensor(out=P2, in0=bj8(Xim), in1=bk16(LT(1)), op=ALU.mult))
            dv(lambda e: e.tensor_tensor(out=v4(out_re), in0=P1, in1=P2, op=ALU.subtract))
            dv(lambda e: e.tensor_tensor(out=P1, in0=bj8(Xre), in1=bk16(LT(1)), op=ALU.mult))
            dv(lambda e: e.tensor_tensor(out=P2, in0=bj8(Xim), in1=bk16(LT(0)), op=ALU.mult))
            if neg_im:
                dv(lambda e: e.scalar_tensor_tensor(out=out_im, in0=P1.rearrange("p g k h -> p g (k h)"), scalar=-1.0,
                                                    in1=P2.rearrange("p g k h -> p g (k h)"), op0=ALU.mult, op1=ALU.subtract))
            else:
                dv(lambda e: e.tensor_tensor(out=v4(out_im), in0=P1, in1=P2, op=ALU.add))

        lamtab(0)
        ch["dve"] = [ch["dve"]] + d_cst
        cprod(Cre, Cimn, Cq_re, Cq_im, neg_im=True)
        lamtab(1)
        cprod(Rm_re, Rm_im, bb_re, bb_im)
        lamtab(2)
        o_rp = cprod(Rp_re, Rp_im, bb_re, bb_im)
        o_params = ch["dve"]
        while pending:
            pending.pop(0)()
        o_sc1p = k.op("dve", lambda e: e.tensor_scalar(out=sc1p[:], in0=modA[:, 1, :], scalar1=1.0, scalar2=None, op0=ALU.add), o_sc1)
        if debug == "b1":
            finish()
            return nc

        for r, Rm in enumerate((Rm_re, Rm_im)):
            for g8 in range(4):
                bk, fr_ = banks.get()
                last = None
                for gi in range(8):
                    gq = g8 * 8 + gi
                    last = k.op("pe", lambda e, gq=gq, gi=gi, bk=bk, Rm=Rm: e.transpose(out=banks.bf(bk)[:, gi * 128:(gi + 1) * 128],
                                                                                      in_=Rm[:, gq, :], identity=identb[:]),
                                [fr_, o_params, o_idb] if gi == 0 else (), signal=(gi == 7))
                o = k.op("act", lambda e, g8=g8, r=r, bk=bk: e.activation(out=MinT[:, g8 * 8:(g8 + 1) * 8, r, :],
                                                                      in_=banks.bf(bk).rearrange("p (g m) -> p g m", g=8), func=AF.Copy), [last])
                banks.rel(bk, o)
        o_minT = k.last("act")
        if debug == "b2":
            finish()
            return nc

        Ttmp = carve(BIG, 10 * KB, 2 * KB, F32).rearrange("p (a b) -> p a b", a=4)
        o_T = []
        tfree = None
        for g4 in range(16):
            gqb, g2 = g4 // 2, g4 % 2
            pr = slice(64 * g2, 64 * g2 + 64)
            bk, fr_ = banks.get()
            last = None
            for gi in range(4):
                gq = gqb * 4 + gi
                k.op("pe", lambda e, gq=gq, pr=pr, gi=gi, bk=bk: e.matmul(banks.f32(bk)[:, gi * 128:(gi + 1) * 128], lhsT=Rp_re[pr, gq, :], rhs=Cre[pr, gq, :],
                                                                            start=True, stop=False),
                     [fr_, o_params] if gi == 0 else (), signal=False)
                last = k.op("pe", lambda e, gq=gq, pr=pr, gi=gi, bk=bk: e.matmul(banks.f32(bk)[:, gi * 128:(gi + 1) * 128], lhsT=Rp_im[pr, gq, :], rhs=Cimn[pr, gq, :],
                                                                                   start=False, stop=True), signal=(gi == 3))
            o = k.op("dve", lambda e, bk=bk: e.tensor_tensor(out=Ttmp, in0=banks.f32(bk).rearrange("p (a b) -> p a b", a=4),
                                                            in1=maskT[:].unsqueeze(1).broadcast_to([128, 4, 128]), op=ALU.mult), [last, d_mk, tfree])
            banks.rel(bk, o)
            for gi in range(4):
                g = 2 * (gqb * 4 + gi) + g2
                tfree = k.op("dve", lambda e, g=g, gi=gi: e.scalar_tensor_tensor(out=Tsb[:, g, :], in0=identf[:], scalar=dvec[:, g:g + 1], in1=Ttmp[:, gi, :],
                                                                                op0=ALU.mult, op1=ALU.add), [o] + d_dv)
            o_T.append(tfree)
        if debug == "b3":
            finish()
            return nc
        k.barrier()
        TT = lambda i: carve(BIG, i * 16512, 16512, F32).rearrange("p (g c) -> p g c", g=32)
        TT2 = HT[:].rearrange("p a b -> p (a b)")[:, 0:8256].bitcast(F32).rearrange("p (g c) -> p g c", g=32)
        TTi = carve(BIG, 0, 16512, I32).rearrange("p (g c) -> p g c", g=32)
        ch["dve"] = k.last("dve"); ch["act"] = k.last("act")
        dv(lambda e: e.tensor_tensor(out=TT(1), in0=th8n[:].unsqueeze(2).broadcast_to([128, 32, 129]),
                                     in1=iota[:].unsqueeze(1).broadcast_to([128, 32, 129]), op=ALU.mult), [d_io])
        for (dst, off) in ((TS, 0.0), (TC, 0.25)):
            if off != 0.0:
                dv(lambda e: e.tensor_scalar(out=TT(1), in0=TT(1), scalar1=off, scalar2=None, op0=ALU.add))
            dv(lambda e: e.tensor_copy(out=TTi, in_=TT(1)))
            dv(lambda e: e.tensor_copy(out=TT2, in_=TTi))
            dv(lambda e: e.tensor_tensor(out=TT2, in0=TT(1),
                                         in1=TT2, op=ALU.subtract))
            ac(lambda e, dst=dst: e.activation(out=dst, in_=TT2, func=AF.Sin, scale=SIN_SCALE))
        k.barrier()

        if debug == "s5mats":
            dd = carve(BIG, 0, 32 * KB, F32)
            o1 = k.op("dve", lambda e: e.tensor_copy(out=dd[:, 0:4096], in_=MinT.rearrange("p g r m -> p (g r m)")[:, 0:4096]))
            dump(dd[:, 0:4096], 0, [o1])
            o2 = k.op("dve", lambda e: e.tensor_copy(out=dd[:, 4096:8192], in_=Tsb.rearrange("p g n -> p (g n)")[:, 0:4096]))
            dump(dd[:, 4096:8192], 4096, [o2])
            k.barrier()
            o3 = k.op("dve", lambda e: e.tensor_copy(out=dd[:, 0:4096], in_=Cre.rearrange("p g n -> p (g n)")))
            dump(dd[:, 0:4096], 8192, [o3])
            o4 = k.op("dve", lambda e: e.tensor_copy(out=dd[:, 4096:8192], in_=Cimn.rearrange("p g n -> p (g n)")))
            dump(dd[:, 4096:8192], 12288, [o4])
            finish()
            return nc
        if debug == "tables":
            dump(TC.rearrange("p g c -> p (g c)"), 0, [])
            dump(TS.rearrange("p g c -> p (g c)"), 4128, [])
            dump(m8[:], 8256, [])
            dump(th8n[:], 8288, [])
            finish()
            return nc


        def ln_stats(j, src_fn, deps):
            o = None
            for q in range(4):
                o = k.op("dve", lambda e, q=q: e.bn_stats(out=stat[:, q, :], in_=src_fn(q)), deps if q == 0 else ())
            o = k.op("dve", lambda e: e.bn_aggr(out=mv[:, j, :], in_=stat[:].rearrange("p a b -> p (a b)")), [o])
            o = k.op("act", lambda e: e.activation(out=sdv[:, j:j + 1], in_=mv[:, j, 1:2], func=AF.Sqrt, bias=EPS), [o])
            o = k.op("dve", lambda e: e.reciprocal(out=rstd[:, j:j + 1], in_=sdv[:, j:j + 1]), [o])
            o = k.op("dve", lambda e: e.tensor_scalar(out=nmr[:, j:j + 1], in0=mv[:, j, 0:1], scalar1=rstd[:, j:j + 1], scalar2=-1.0,
                                                     op0=ALU.mult, op1=ALU.mult), [o])
            return o

        def transposes_to_hT(nbf, dst_fn, scaleA, biasA, deps):
            evs = []
            for q in range(4):
                bk, fr_ = banks.get()
                last = None
                for kk in range(4):
                    kc = 4 * q + kk
                    last = k.op("pe", lambda e, kc=kc, kk=kk, bk=bk: e.transpose(out=banks.bf(bk)[:, kk * 128:(kk + 1) * 128],
                                                                              in_=nbf[:, kc * 128:(kc + 1) * 128], identity=identb[:]),
                                [fr_] + list(deps) if kk == 0 else (), signal=(kk == 3))
                o = None
                for kk in range(4):
                    kc = 4 * q + kk
                    if q % 2 == 0:
                        o = k.op("act", lambda e, kc=kc, kk=kk, bk=bk: e.activation(out=dst_fn(kc), in_=banks.bf(bk)[:, kk * 128:(kk + 1) * 128], func=AF.Identity,
                                                                                 scale=scaleA[:, kc:kc + 1], bias=biasA[:, kc:kc + 1]), [last])
                    else:
                        o = k.op("dve", lambda e, kc=kc, kk=kk, bk=bk: e.tensor_scalar(out=dst_fn(kc), in0=banks.bf(bk)[:, kk * 128:(kk + 1) * 128],
                                                                                    scalar1=scaleA[:, kc:kc + 1], scalar2=biasA[:, kc:kc + 1],
                                                                                    op0=ALU.mult, op1=ALU.add), [last])
                    evs.append(o)
                banks.rel(bk, evs[-1])
            return evs

        SCR = lambda off, n, dt: carve(BIG, off, n, dt)
        xt = [SCR(16 * KB, 8 * KB, F32), SCR(24 * KB, 8 * KB, F32)]
        nbf = [SCR(32 * KB, 4 * KB, BF16), SCR(36 * KB, 4 * KB, BF16)]
        xt_free = [None, None]
        nbf_free = [None, None]

        wu = []
        wu_d = []
        wu_i = []
        for n2 in range(2):
            buf, dops, bi = ws.load([(lambda b: b[:], wsrc(w_in, 0, D, n2 * 512, 512))])
            wu.append(buf); wu_d.append(dops); wu_i.append(bi)

        def u_matmuls(hT_fn, Udst, j, deps):
            evs = []
            last = None
            for n2 in range(2):
                bk, fr_ = banks.get()
                for kc in range(16):
                    last = k.op("pe", lambda e, kc=kc, bk=bk, n2=n2: e.matmul(banks.f32(bk), lhsT=hT_fn(kc), rhs=wu[n2][:, kc, :], start=(kc == 0), stop=(kc == 15)),
                                [fr_] + list(deps) + wu_d[n2] if kc == 0 else (), signal=(kc == 15))
                eng = "act" if n2 == 0 else "dve"
                if eng == "act":
                    o = k.op("act", lambda e, bk=bk, n2=n2: e.activation(out=Udst[:, 32 * n2:32 * n2 + 32, j, :], in_=banks.f32(bk).rearrange("p (g h) -> p g h", g=32), func=AF.Copy), [last])
                else:
                    o = k.op("dve", lambda e, bk=bk, n2=n2: e.tensor_copy(out=Udst[:, 32 * n2:32 * n2 + 32, j, :], in_=banks.f32(bk).rearrange("p (g h) -> p g h", g=32)), [last])
                banks.rel(bk, o)
                evs.append(o)
            return evs, last

        HTb2 = HT[:].rearrange("p a b -> p (a b)")
        Uprev = HTb2[:, 0:8192].rearrange("p (g i h) -> p g i h", g=64, i=8)
        hpT = [HTb2[:, 8192:10240].rearrange("p (a b) -> p a b", a=16), HTb2[:, 10240:12288].rearrange("p (a b) -> p a b", a=16)]
        hpT_free = [None, None]
        xpv = x_prev.rearrange("(c j) d -> c j d", j=8)
        xov = x_own.rearrange("(c j) d -> c j d", j=8)
        u_evs = []
        tail_ops = []
        def ln_part(xv, j, extra):
            b2 = j % 2
            d_x = k.dma("sp", xt[b2], xv[:, j, :], deps=[xt_free[b2]] + list(extra))
            o_st = ln_stats(j, lambda q, b2=b2: xt[b2][:, q * 512:(q + 1) * 512], [d_x])
            o_n = k.op("dve", lambda e, b2=b2, j=j: e.tensor_scalar(out=nbf[b2], in0=xt[b2], scalar1=rstd[:, j:j + 1], scalar2=nmr[:, j:j + 1],
                                                                 op0=ALU.mult, op1=ALU.add), [o_st, nbf_free[b2]])
            xt_free[b2] = o_n
            return o_n

        o_ns = {0: ln_part(xpv, 0, [o_T[-1]])}
        for j in range(8):
            b2 = j % 2
            if j + 1 < 8:
                o_ns[j + 1] = ln_part(xpv, j + 1, [o_T[-1]])
            o_n = o_ns[j]
            evs = transposes_to_hT(nbf[b2], lambda kc, b2=b2: hpT[b2][:, kc, :], sc1p, modA[:, 0, :], [o_n, hpT_free[b2], o_sc1p] + o_sh1)
            nbf_free[b2] = k.last("pe")
            o_tl = k.op("act", lambda e, b2=b2, j=j: e.activation(out=tail[:, :, 2 * j:2 * j + 2], in_=hpT[b2][:, :, 126:128], func=AF.Copy), evs)
            tail_ops.append(o_tl)
            ue, lastpe = u_matmuls(lambda kc, b2=b2: hpT[b2][:, kc, :], Uprev, j, evs)
            hpT_free[b2] = [lastpe, o_tl]
            u_evs += ue

        sofs = {"o": 0}

        def salloc(n, base):
            o = sofs["o"]
            sofs["o"] += (n + 63) // 64 * 64
            return base + o

        def s5_temps(base):
            sofs["o"] = 0
            t = {}
            t["UgT"] = carve(BIG, salloc(2048, base), 2048, BF16).rearrange("p (g c) -> p g c", g=8)
            for nm in ("Vre", "Vim", "t1", "t2"):
                t[nm] = carve(BIG, salloc(2048, base), 2048, F32).rearrange("p (g c) -> p g c", g=4)
            for nm in ("Wre", "Wim"):
                t[nm] = carve(BIG, salloc(2064, base), 2064, F32).rearrange("p (g c) -> p g c", g=4)
            for nm in ("Xre", "Xim"):
                t[nm] = carve(BIG, salloc(1024, base), 1024, BF16).rearrange("p (g c) -> p g c", g=4)
            for nm in ("ysb", "sg", "ztm"):
                t[nm] = salloc(2048, base)
            t["end"] = sofs["o"]
            return t

        def s5_states(U, tm, b, start_deps, own):
            gqs = slice(4 * b, 4 * b + 4)
            bk, fr_ = banks.get()
            last = None
            for gi in range(8):
                g = 8 * b + gi
                last = k.op("pe", lambda e, g=g, gi=gi, bk=bk: e.transpose(out=banks.bf(bk)[:, gi * 128:(gi + 1) * 128],
                                                                        in_=U[:, g].rearrange("p i h -> p (i h)"), identity=identb[:]),
                            [fr_] + list(start_deps) if gi == 0 else (), signal=(gi == 7))
            o_ug = k.op("act", lambda e, bk=bk: e.activation(out=tm["UgT"].rearrange("p g c -> p (g c)"), in_=banks.bf(bk), func=AF.Copy), [last] + list(start_deps))
            banks.rel(bk, o_ug)
            bre_, fr1 = banks.get()
            bim_, fr2 = banks.get()
            lastS = None
            for ri, bkk, frr in ((0, bre_, fr1), (1, bim_, fr2)):
                first = True
                for gi in range(4):
                    for g2 in range(2):
                        lastS = k.op("pe", lambda e, gi=gi, g2=g2, ri=ri, bkk=bkk: e.matmul(banks.f32(bkk)[64 * g2:64 * g2 + 64, gi * 128:(gi + 1) * 128],
                                                                                            lhsT=MinT[:, 4 * b + gi, ri, 64 * g2:64 * g2 + 64], rhs=tm["UgT"][:, 2 * gi + g2, :],
                                                                                            start=True, stop=True),
                                     [frr, o_ug, o_minT] if first else (), signal=(gi == 3 and g2 == 1))
                        first = False
            Sre = banks.f32(bre_).rearrange("p (g c) -> p g c", g=4)
            Sim = banks.f32(bim_).rearrange("p (g c) -> p g c", g=4)
            tcd = TC[:, gqs, 1:129]
            tsd = TS[:, gqs, 1:129]
            c_ = {"o": [lastS] + list(start_deps)}

            def dvc(fn, extra=()):
                o = k.op("dve", fn, c_["o"] + list(extra))
                c_["o"] = [o]
                return o

            dvc(lambda e: e.tensor_tensor(out=tm["t1"], in0=Sre, in1=tcd, op=ALU.mult))
            dvc(lambda e: e.tensor_tensor(out=tm["t2"], in0=Sim, in1=tsd, op=ALU.mult))
            dvc(lambda e: e.tensor_tensor(out=tm["Vre"], in0=tm["t1"], in1=tm["t2"], op=ALU.add))
            dvc(lambda e: e.tensor_tensor(out=tm["t1"], in0=Sim, in1=tcd, op=ALU.mult))
            o_s = dvc(lambda e: e.tensor_tensor(out=tm["t2"], in0=Sre, in1=tsd, op=ALU.mult))
            banks.rel(bre_, o_s)
            banks.rel(bim_, o_s)
            dvc(lambda e: e.tensor_tensor(out=tm["Vim"], in0=tm["t1"], in1=tm["t2"], op=ALU.subtract))
            if own:
                dvc(lambda e: e.tensor_scalar(out=tm["Wre"][:, :, 0], in0=xinit[:, 0, gqs], scalar1=flag[:, 0:1], scalar2=None, op0=ALU.mult), [d_fl])
                dvc(lambda e: e.tensor_scalar(out=tm["Wim"][:, :, 0], in0=xinit[:, 1, gqs], scalar1=flag[:, 0:1], scalar2=None, op0=ALU.mult))
            else:
                dvc(lambda e: e.memset(tm["Wre"][:, :, 0], 0.0))
                dvc(lambda e: e.memset(tm["Wim"][:, :, 0], 0.0))
            for gi in range(4):
                gq = 4 * b + gi
                for nm_w, nm_v in (("Wre", "Vre"), ("Wim", "Vim")):
                    dvc(lambda e, gi=gi, gq=gq, nm_w=nm_w, nm_v=nm_v: e.tensor_tensor_scan(out=tm[nm_w][:, gi, 1:129], data0=m8[:, gq:gq + 1].broadcast_to([128, 128]),
                                                                                        data1=tm[nm_v][:, gi, :], initial=tm[nm_w][:, gi, 0:1], op0=ALU.mult, op1=ALU.add))
            if not own:
                tce = TC[:, gqs, 128]
                tse = TS[:, gqs, 128]
                s1 = tm["t1"][:, :, 0]
                s2 = tm["t2"][:, :, 0]
                dvc(lambda e: e.tensor_tensor(out=s1, in0=tce, in1=tm["Wre"][:, :, 128], op=ALU.mult))
                dvc(lambda e: e.tensor_tensor(out=s2, in0=tse, in1=tm["Wim"][:, :, 128], op=ALU.mult))
                dvc(lambda e: e.tensor_tensor(out=xinit[:, 0, gqs], in0=s1, in1=s2, op=ALU.subtract))
                dvc(lambda e: e.tensor_tensor(out=s1, in0=tce, in1=tm["Wim"][:, :, 128], op=ALU.mult))
                dvc(lambda e: e.tensor_tensor(out=s2, in0=tse, in1=tm["Wre"][:, :, 128], op=ALU.mult))
                return dvc(lambda e: e.tensor_tensor(out=xinit[:, 1, gqs], in0=s1, in1=s2, op=ALU.add))
            tcr = TC[:, gqs, 0:128]
            tsr = TS[:, gqs, 0:128]
            Wr = tm["Wre"][:, :, 0:128]
            Wi = tm["Wim"][:, :, 0:128]
            dvc(lambda e: e.tensor_tensor(out=tm["t1"], in0=tcr, in1=Wr, op=ALU.mult))
            dvc(lambda e: e.tensor_tensor(out=tm["t2"], in0=tsr, in1=Wi, op=ALU.mult))
            dvc(lambda e: e.tensor_tensor(out=tm["Xre"], in0=tm["t1"], in1=tm["t2"], op=ALU.subtract))
            dvc(lambda e: e.tensor_tensor(out=tm["t1"], in0=tcr, in1=Wi, op=ALU.mult))
            dvc(lambda e: e.tensor_tensor(out=tm["t2"], in0=tsr, in1=Wr, op=ALU.mult))
            return dvc(lambda e: e.tensor_tensor(out=tm["Xim"], in0=tm["t1"], in1=tm["t2"], op=ALU.add))

        tmP = s5_temps(0)
        prev = list(u_evs)
        for b in range(8):
            prev = [s5_states(Uprev, tmP, b, prev, own=False)]
        o_xinit = prev[0]

        if debug == "xinit":
            dump(xinit[:].rearrange("p a b -> p (a b)"), 0, [o_xinit])
            dd = carve(BIG, 16 * KB, 2 * KB, F32)
            o1 = k.op("dve", lambda e: e.tensor_copy(out=dd.rearrange("p (a b) -> p a b", a=16)[:, :, 0:16], in_=tail[:]), tail_ops + [o_xinit])
            dump(dd[:, 0:256], 64, [o1])
            finish()
            return nc
        k.barrier()

        U = carve(BIG, 0, 16 * KB, BF16).rearrange("p (g i h) -> p g i h", g=64, i=8)
        u_evs = []
        lastpe = None
        o_ns = {0: ln_part(xov, 0, [])}
        for j in range(8):
            b2 = j % 2
            if j + 1 < 8:
                o_ns[j + 1] = ln_part(xov, j + 1, [])
            o_n = o_ns[j]
            evs = transposes_to_hT(nbf[b2], lambda kc, j=j: HT[:, kc, j * 128:(j + 1) * 128], sc1p, modA[:, 0, :], [o_n])
            nbf_free[b2] = k.last("pe")
            ue, lastpe = u_matmuls(lambda kc, j=j: HT[:, kc, j * 128:(j + 1) * 128], U, j, evs)
            u_evs += ue
        for n2 in range(2):
            ws.release(wu_i[n2], lastpe)
        k.barrier()

        tm = s5_temps(16 * KB)
        assert 16 * KB + tm["end"] <= 40 * KB, tm["end"]
        zT = carve(BIG, 0, 16 * KB, BF16).rearrange("p (b n) -> p b n", b=8)
        ysb = carve(BIG, tm["ysb"], 2048, F32)
        sgb = carve(BIG, tm["sg"], 2048, F32)
        ztm = carve(BIG, tm["ztm"], 2048, BF16).rearrange("p (j q g2 h) -> p j q g2 h", j=8, q=4, g2=2)
        tmpC = tm["t1"].rearrange("p g c -> p (g c)")
        sqb = tm["t2"].rearrange("p g c -> p (g c)")
        prev = []
        zt_ops = []
        for b in range(8):
            o_x = s5_states(U, tm, b, prev, own=True)
            gl = []
            for g2 in range(2):
                pr = slice(64 * g2, 64 * g2 + 64)
                bA, frA = banks.get()
                lastA = None
                for gi in range(4):
                    k.op("pe", lambda e, gi=gi, pr=pr, bA=bA, b=b: e.matmul(banks.f32(bA)[:, gi * 128:(gi + 1) * 128], lhsT=tm["Xre"][pr, gi, :], rhs=Cre[pr, 4 * b + gi, :],
                                                                       start=True, stop=False), [frA, o_x] if gi == 0 else (), signal=False)
                    lastA = k.op("pe", lambda e, gi=gi, pr=pr, bA=bA, b=b: e.matmul(banks.f32(bA)[:, gi * 128:(gi + 1) * 128], lhsT=tm["Xim"][pr, gi, :], rhs=Cimn[pr, 4 * b + gi, :],
                                                                               start=False, stop=True), signal=(gi == 3))
                bC, frC = banks.get()
                lastC = None
                for gi in range(4):
                    lastC = k.op("pe", lambda e, gi=gi, g2=g2, bC=bC, b=b: e.matmul(banks.f32(bC)[:, gi * 128:(gi + 1) * 128], lhsT=tm["UgT"][:, 2 * gi + g2, :], rhs=Tsb[:, 8 * b + 2 * gi + g2, :],
                                                                               start=True, stop=True), [frC] + o_T if gi == 0 else (), signal=(gi == 3))
                o1 = k.op("act", lambda e, bC=bC: e.activation(out=tmpC, in_=banks.f32(bC), func=AF.Copy), [lastC, o_x] + gl)
                banks.rel(bC, o1)
                o2 = k.op("dve", lambda e, bA=bA: e.tensor_tensor(out=ysb, in0=banks.f32(bA), in1=tmpC, op=ALU.add), [lastA, o1] + gl)
                banks.rel(bA, o2)
                o3 = k.op("act", lambda e: e.activation(out=sqb, in_=ysb, func=AF.Square), [o2])
                o4 = k.op("dve", lambda e: e.tensor_scalar(out=sqb, in0=sqb, scalar1=0.044715, scalar2=1.0, op0=ALU.mult, op1=ALU.add), [o3])
                o5 = k.op("dve", lambda e: e.tensor_tensor(out=sqb, in0=sqb, in1=ysb, op=ALU.mult), [o4])
                o6 = k.op("act", lambda e: e.activation(out=sgb, in_=sqb, func=AF.Sigmoid, scale=GELU_C), [o5])
                o7 = k.op("dve", lambda e, g2=g2: e.tensor_tensor(out=ztm[:, :, :, g2, :].rearrange("p j q h -> p q j h"),
                                                                in0=sgb.rearrange("p (q j h) -> p q j h", q=4, j=8), in1=ysb.rearrange("p (q j h) -> p q j h", q=4, j=8),
                                                                op=ALU.mult), [o6] + zt_ops[-1:])
                gl = [o7]
            bk, fr_ = banks.get()
            last = None
            for j in range(8):
                last = k.op("pe", lambda e, j=j, bk=bk: e.transpose(out=banks.bf(bk)[:, j * 128:(j + 1) * 128], in_=ztm[:, j].rearrange("p q g h -> p (q g h)"), identity=identb[:]),
                            [fr_, o7] if j == 0 else (), signal=(j == 7))
            oz = k.op("act", lambda e, b=b, bk=bk: e.activation(out=zT[:, b, :], in_=banks.bf(bk), func=AF.Copy), [last])
            banks.rel(bk, oz)
            zt_ops.append(oz)
            prev = [o7, oz]

        if debug == "z":
            k.barrier()
            dd = carve(BIG, 40 * KB, 32 * KB, F32)
            o1 = k.op("dve", lambda e: e.tensor_copy(out=dd, in_=zT.rearrange("p b n -> p (b n)")))
            dump(dd, 0, [o1])
            finish()
            return nc
        k.barrier()


        upT = carve(BIG, 40 * KB, 33280, F32).rearrange("p (k j c) -> p k j c", k=8, j=8)
        pA = carve(BIG, 73 * KB, 4160, F32).rearrange("p (j c) -> p j c", j=8)
        pB = carve(BIG, 78 * KB, 4160, F32).rearrange("p (j c) -> p j c", j=8)
        rct = carve(BIG, 83 * KB, 4096, F32).rearrange("p (j c) -> p j c", j=8)
        pooledT = carve(RB, 0, 16 * KB, BF16).rearrange("p (k n) -> p k n", k=8)
        qsT = carve(BIG, 16 * KB, 16 * KB, BF16).rearrange("p (k n) -> p k n", k=8)
        o_ms = [k.op("dve", lambda e: e.memset(pA, 0.0)), k.op("dve", lambda e: e.memset(pB, 0.0))]
        up_ops = {}
        for mb in range(2):
            buf, dops, bi = ws.load([(lambda b_: b_[:], wsrc(w_in, 0, D, 1024 + mb * 512, 512))])
            last = None
            for m in range(4):
                k8 = mb * 4 + m
                ops = []
                for half in range(2):
                    bk, fr_ = banks.get()
                    for kc in range(16):
                        last = k.op("pe", lambda e, kc=kc, m=m, half=half, bk=bk, buf=buf: e.matmul(banks.f32(bk), lhsT=buf[:, kc, m * 128:(m + 1) * 128],
                                                                                                 rhs=HT[:, kc, half * 512:(half + 1) * 512], start=(kc == 0), stop=(kc == 15)),
                                    [fr_] + dops if kc == 0 else (), signal=(kc == 15))
                    o = k.op("act", lambda e, k8=k8, half=half, bk=bk: e.activation(out=upT[:, k8, 4 * half:4 * half + 4, 2:130], in_=banks.f32(bk).rearrange("p (j c) -> p j c", j=4),
                                                                                 func=AF.Copy), [last])
                    banks.rel(bk, o)
                    ops.append(o)
                bk, fr_ = banks.get()
                for kc in range(16):
                    last = k.op("pe", lambda e, kc=kc, m=m, bk=bk, buf=buf: e.matmul(banks.f32(bk)[:, 0:16], lhsT=buf[:, kc, m * 128:(m + 1) * 128], rhs=tail[:, kc, :],
                                                                                  start=(kc == 0), stop=(kc == 15)), [fr_] + tail_ops if kc == 0 else (), signal=(kc == 15))
                o = k.op("dve", lambda e, k8=k8, bk=bk: e.tensor_scalar(out=upT[:, k8, :, 0:2], in0=banks.f32(bk)[:, 0:16].rearrange("p (j c) -> p j c", j=8),
                                                                     scalar1=flag[:, 0:1], scalar2=None, op0=ALU.mult), [last, d_fl])
                banks.rel(bk, o)
                ops.append(o)
                up_ops[k8] = ops
            ws.release(bi, last)

        rc_free = [None]
        pl_ops = []
        tmp_free = {"A": o_ms[0:1], "B": o_ms[1:2]}
        for gi in range(4):
            d_rc = k.dma("sp", rct, rc_d[gi:gi + 1, :].rearrange("o (j c) -> o j c", j=8).broadcast_to([128, 8, 128]), deps=rc_free)
            gl = []
            for a in range(2):
                k8 = 2 * gi + a
                eng = "dve"
                u_ = upT[:, k8]
                cur = u_
                c_ = {"o": list(up_ops[k8]) + tmp_free["A"] + tmp_free["B"]}

                def pc(fn, extra=()):
                    o = k.op(eng, fn, c_["o"] + list(extra))
                    c_["o"] = [o]
                    return o

                names = ["A", "B"]
                tb = {"A": pA, "B": pB}
                lvl = 0
                for s_ in (1, 2, 4, 8)[:gi + 1]:
                    dst = tb[names[lvl % 2]]
                    if s_ < 8:
                        pc(lambda e, dst=dst, cur=cur, s_=s_: e.tensor_tensor(out=dst[:, s_:8, :], in0=cur[:, s_:8, :], in1=cur[:, 0:8 - s_, :], op=ALU.add))
                        pc(lambda e, dst=dst, cur=cur, s_=s_: e.tensor_tensor(out=dst[:, 0:s_, 1:130], in0=cur[:, 0:s_, 1:130], in1=cur[:, 8 - s_:8, 0:129], op=ALU.add))
                    else:
                        pc(lambda e, dst=dst, cur=cur: e.tensor_tensor(out=dst[:, :, 1:130], in0=cur[:, :, 1:130], in1=cur[:, :, 0:129], op=ALU.add))
                    cur = dst
                    lvl += 1
                oth = tb[names[lvl % 2]]
                pc(lambda e, oth=oth, cur=cur: e.tensor_tensor(out=oth[:, :, 2:130], in0=cur[:, :, 2:130], in1=rct, op=ALU.mult), [d_rc])
                o = pc(lambda e, oth=oth, u_=u_, k8=k8: e.tensor_tensor(out=pooledT[:, k8, :].rearrange("p (j c) -> p j c", j=8), in0=oth[:, :, 2:130], in1=u_[:, :, 2:130], op=ALU.subtract))
                tmp_free["A"] = [o]
                tmp_free["B"] = [o]
                gl.append(o)
                pl_ops.append(o)
            rc_free = gl
        buf, dops, bi = ws.load([(lambda b_: b_[:, 0:8, 0:256], w_pool.rearrange("(r p) n -> p r n", p=128))])
        last = None
        qs_ops = []
        for gi in range(4):
            for a in range(2):
                k8o = 2 * gi + a
                for half in range(2):
                    bk, fr_ = banks.get()
                    for bs in range(2):
                        last = k.op("pe", lambda e, gi=gi, a=a, bs=bs, half=half, bk=bk, buf=buf: e.matmul(banks.f32(bk), lhsT=buf[:, 2 * gi + bs, a * 128:(a + 1) * 128],
                                                                                                        rhs=pooledT[:, 2 * gi + bs, half * 512:(half + 1) * 512], start=(bs == 0), stop=(bs == 1)),
                                    [fr_] + dops + pl_ops[2 * gi:2 * gi + 2] if bs == 0 else (), signal=(bs == 1))
                    o = k.op("act", lambda e, k8o=k8o, half=half, bk=bk: e.activation(out=qsT[:, k8o, half * 512:(half + 1) * 512], in_=banks.f32(bk), func=AF.Identity,
                                                                                   scale=pscale[:, k8o:k8o + 1]), [last, d_ps])
                    banks.rel(bk, o)
                    qs_ops.append(o)
        ws.release(bi, last)
        k.barrier()

        merged = carve(RB, 0, 32 * KB, BF16).rearrange("p (m n) -> p m n", m=16)
        tmpA = carve(BIG, 32 * KB, 16 * KB, F32).rearrange("p (m n) -> p m n", m=4)
        tmpB = carve(BIG, 48 * KB, 16 * KB, F32).rearrange("p (m n) -> p m n", m=4)
        s1 = [carve(BIG, 72 * KB, 2 * KB, F32), carve(BIG, 74 * KB, 2 * KB, F32)]
        slot = [carve(BIG, (64 + 8 * i) * KB, 8 * KB, F32) for i in range(3)]
        g1_ops = []
        o_sh2 = []
        o_sc2 = []
        s1_free = [None, None]
        sidx = [0]
        AB_free = [None]

        def mm_group(bk, fr_, n_k, lhs_fn, rhs_fn, deps):
            last = None
            for kc in range(n_k):
                last = k.op("pe", lambda e, kc=kc: e.matmul(banks.f32(bk), lhsT=lhs_fn(kc), rhs=rhs_fn(kc), start=(kc == 0), stop=(kc == n_k - 1)),
                            [fr_] + list(deps) if kc == 0 else (), signal=(kc == n_k - 1))
            return last

        p5_specs = []
        for mb_ in range(4):
            c0_ = mb_ * 512
            p5_specs.append([(lambda b_: b_[:, 0:8, :], wsrc(w_val, 0, 1024, c0_, 512)), (lambda b_: b_[:, 8:16, :], wsrc(w_gate, 0, 1024, c0_, 512))])
            p5_specs.append([(lambda b_: b_[:], wsrc(w_in, 0, D, 2048 + c0_, 512))])
            p5_specs.append([(lambda b_: b_[:, 0:8, :], wsrc(w_pool_out, 0, 1024, c0_, 512))])
            p5_specs.append([(lambda b_: b_[:], wsrc(w_in, 0, D, 4096 + c0_, 512))])
        p5_loaded = {}

        def p5_load(idx):
            for i_ in (idx, idx + 1):
                if i_ < 16 and i_ not in p5_loaded:
                    p5_loaded[i_] = ws.load(p5_specs[i_])
            return p5_loaded[idx]

        adaW = carve(BIG, 76 * KB, 8 * KB, BF16).rearrange("p (a b) -> p a b", a=16)
        adaW_free = [None]

        def ada_half(n0, dst):
            d_w = k.dma("pool", adaW, wsrc(w_ada, 0, D, n0, 256), deps=[adaW_free[0]])
            d_b = k.dma("sp", bada[:, 0:256], b_ada[:, n0:n0 + 256].broadcast_to([128, 256]), deps=[state["bada_free"]])
            bk, fr = banks.get()
            last = None
            for kc in range(16):
                last = k.op("pe", lambda e, kc=kc, bk=bk: e.matmul(banks.f32(bk)[:, 0:256], lhsT=crep[:, kc, :], rhs=adaW[:, kc, :], start=(kc == 0), stop=(kc == 15)),
                            [fr, o_crep, d_w] if kc == 0 else (), signal=(kc == 15))
            adaW_free[0] = last
            o = k.op("dve", lambda e, bk=bk, dst=dst: e.tensor_tensor(out=dst, in0=banks.f32(bk)[:, 0:256], in1=bada[:, 0:256], op=ALU.add), [last, d_b, state["modbc_free"]])
            banks.rel(bk, o)
            state["bada_free"] = o
            return o

        def ada_half_a(vi, mrow, hb):
            o = ada_half(vi * D + hb * 256, modbc[:, 0:256])
            o2 = k.op("dve", lambda e: e.tensor_tensor(out=dtmp[:, 0:256].rearrange("p (a b) -> p a b", a=2), in0=modbc[:, 0:256].rearrange("p (a b) -> p a b", a=2),
                                                      in1=identf[:].unsqueeze(1).broadcast_to([128, 2, 128]), op=ALU.mult), [o])
            o3 = k.op("dve", lambda e: e.tensor_reduce(out=modA[:, mrow, 2 * hb:2 * hb + 2], in_=dtmp[:, 0:256].rearrange("p (a b) -> p a b", a=2), axis=AX.X, op=ALU.add), [o2])
            state["modbc_free"] = o3
            return o3

        def ada_p5(mb, step):
            if step == 0:
                for hb in (2 * mb, 2 * mb + 1):
                    g1_ops.append(ada_half(2 * D + hb * 256, slot[0][:, hb * 256:(hb + 1) * 256]))
            elif step == 1:
                for hb in (2 * mb, 2 * mb + 1):
                    o_sh2.append(ada_half_a(3, 2, hb))
            elif step == 2:
                o_sc2.append(ada_half_a(4, 3, 2 * mb))
            else:
                o_sc2.append(ada_half_a(4, 3, 2 * mb + 1))

        for mb in range(4):
            c0 = mb * 512
            buf, dops, bi = p5_load(4 * mb + 0)
            last = None
            stepA = {}
            for m in range(4):
                for half in range(2):
                    hs = slice(half * 512, (half + 1) * 512)
                    bv, fv = banks.get()
                    lv = mm_group(bv, fv, 8, lambda kc, m=m, buf=buf: buf[:, kc, m * 128:(m + 1) * 128], lambda kc, hs=hs: zT[:, kc, hs], dops + zt_ops)
                    bg, fg = banks.get()
                    last = mm_group(bg, fg, 8, lambda kc, m=m, buf=buf: buf[:, 8 + kc, m * 128:(m + 1) * 128], lambda kc, hs=hs: zT[:, kc, hs], ())
                    si = sidx[0] % 2; sidx[0] += 1
                    o1 = k.op("act", lambda e, bg=bg, si=si: e.activation(out=s1[si], in_=banks.f32(bg), func=AF.Sigmoid), [last, s1_free[si]])
                    banks.rel(bg, o1)
                    o2 = k.op("dve", lambda e, bv=bv, si=si, m=m, hs=hs: e.tensor_tensor(out=tmpA[:, m, hs], in0=banks.f32(bv), in1=s1[si], op=ALU.mult), [lv, o1, AB_free[0]])
                    banks.rel(bv, o2)
                    s1_free[si] = o2
                    stepA[(m, half)] = o2
            ws.release(bi, last)
            ada_p5(mb, 0)
            buf, dops, bi = p5_load(4 * mb + 1)
            for m in range(4):
                for half in range(2):
                    hs = slice(half * 512, (half + 1) * 512)
                    bg, fg = banks.get()
                    last = mm_group(bg, fg, 16, lambda kc, m=m, buf=buf: buf[:, kc, m * 128:(m + 1) * 128], lambda kc, hs=hs: HT[:, kc, hs], dops)
                    si = sidx[0] % 2; sidx[0] += 1
                    o1 = k.op("act", lambda e, bg=bg, si=si: e.activation(out=s1[si], in_=banks.f32(bg), func=AF.Sigmoid), [last, s1_free[si]])
                    banks.rel(bg, o1)
                    o2 = k.op("dve", lambda e, si=si, m=m, hs=hs: e.tensor_tensor(out=tmpA[:, m, hs], in0=tmpA[:, m, hs], in1=s1[si], op=ALU.mult), [o1, stepA[(m, half)]])
                    s1_free[si] = o2
                    stepA[(m, half)] = o2
            ws.release(bi, last)
            ada_p5(mb, 1)
            buf, dops, bi = p5_load(4 * mb + 2)
            stepB = {}
            for m in range(4):
                for half in range(2):
                    hs = slice(half * 512, (half + 1) * 512)
                    bg, fg = banks.get()
                    last = mm_group(bg, fg, 8, lambda kc, m=m, buf=buf: buf[:, kc, m * 128:(m + 1) * 128], lambda kc, hs=hs: qsT[:, kc, hs], dops + qs_ops)
                    o1 = k.op("act", lambda e, bg=bg, m=m, hs=hs: e.activation(out=tmpB[:, m, hs], in_=banks.f32(bg), func=AF.Copy), [last, AB_free[0]])
                    banks.rel(bg, o1)
                    stepB[(m, half)] = o1
            ws.release(bi, last)
            ada_p5(mb, 2)
            buf, dops, bi = p5_load(4 * mb + 3)
            fin = []
            for m in range(4):
                for half in range(2):
                    hs = slice(half * 512, (half + 1) * 512)
                    bg, fg = banks.get()
                    last = mm_group(bg, fg, 16, lambda kc, m=m, buf=buf: buf[:, kc, m * 128:(m + 1) * 128], lambda kc, hs=hs: HT[:, kc, hs], dops)
                    si = sidx[0] % 2; sidx[0] += 1
                    o1 = k.op("act", lambda e, bg=bg, si=si: e.activation(out=s1[si], in_=banks.f32(bg), func=AF.Sigmoid), [last, s1_free[si]])
                    banks.rel(bg, o1)
                    o2 = k.op("dve", lambda e, si=si, m=m, hs=hs: e.tensor_tensor(out=s1[si], in0=s1[si], in1=tmpB[:, m, hs], op=ALU.mult), [o1, stepB[(m, half)]])
                    o3 = k.op("dve", lambda e, si=si, m=m, hs=hs, mb=mb: e.tensor_tensor(out=merged[:, 4 * mb + m, hs], in0=s1[si], in1=tmpA[:, m, hs], op=ALU.add), [o2, stepA[(m, half)]])
                    s1_free[si] = o3
                    fin.append(o3)
            ws.release(bi, last)
            AB_free[0] = fin
            ada_p5(mb, 3)
        k.barrier()

        if debug == "merged":
            dd = carve(BIG, 0, 64 * KB, F32)
            o1 = k.op("dve", lambda e: e.tensor_copy(out=dd, in_=merged.rearrange("p m n -> p (m n)")))
            dump(dd, 0, [o1])
            finish()
            return nc

        xres = carve(BIG, 0, 64 * KB, F32).rearrange("p (j d) -> p j d", j=8)
        d_xr = [k.dma("sp", xres[:, j, :], xov[:, j, :]) for j in range(8)]
        HTf = HT[:].rearrange("p a b -> p (a b)")
        tq = [HTf[:, i * 1024:(i + 1) * 1024].bitcast(F32) for i in range(4)]
        tq_free = [None] * 4
        tix = [0]
        for nb in range(4):
            ns = slice(nb * 512, (nb + 1) * 512)
            buf, dops, bi = ws.load([(lambda b_: b_[:], wsrc(w_out, 0, D, nb * 512, 512))])
            last = None
            for j in range(8):
                js = slice(j * 128, (j + 1) * 128)
                bk, fr_ = banks.get()
                last = mm_group(bk, fr_, 16, lambda m, js=js: merged[:, m, js], lambda m, buf=buf: buf[:, m, :], dops)
                ti = tix[0] % 4; tix[0] += 1
                o1 = k.op("dve", lambda e, bk=bk, ti=ti, ns=ns: e.tensor_tensor(out=tq[ti], in0=banks.f32(bk), in1=slot[0][:, ns], op=ALU.mult), [last, tq_free[ti]] + g1_ops)
                banks.rel(bk, o1)
                o2 = k.op("dve", lambda e, ti=ti, j=j, ns=ns: e.scalar_tensor_tensor(out=xres[:, j, ns], in0=xres[:, j, ns], scalar=ALPHA, in1=tq[ti], op0=ALU.mult, op1=ALU.add), [o1, d_xr[j]])
                tq_free[ti] = o2
            ws.release(bi, last)
        k.barrier()

        oa = k.op("dve", lambda e: e.tensor_scalar(out=G2[:], in0=modA[:, 3, :], scalar1=1.0, scalar2=None, op0=ALU.add), o_sc2)
        ob = k.op("dve", lambda e: e.tensor_tensor(out=B2[:], in0=lnA[:, 1, :], in1=G2[:], op=ALU.mult), [oa, d_lnA1])
        ob = k.op("dve", lambda e: e.tensor_tensor(out=B2[:], in0=B2[:], in1=modA[:, 2, :], op=ALU.add), [ob] + o_sh2)
        oa = k.op("dve", lambda e: e.tensor_tensor(out=G2[:], in0=G2[:], in1=lnA[:, 0, :], op=ALU.mult), [ob, d_lnA0])
        d_l1 = k.dma("sp", slot[1], ln1_g.broadcast_to([128, D]))
        d_l2 = k.dma("sp", slot[2], ln1_b.broadcast_to([128, D]))
        o_l1 = k.op("dve", lambda e: e.tensor_scalar(out=slot[1], in0=slot[1], scalar1=ALPHA, scalar2=None, op0=ALU.mult), [d_l1])
        o_l2 = k.op("dve", lambda e: e.tensor_scalar(out=slot[2], in0=slot[2], scalar1=ALPHA, scalar2=None, op0=ALU.mult), [d_l2])
        nb2 = [carve(RB, 0, 4 * KB, BF16), carve(RB, 4 * KB, 4 * KB, BF16)]
        nb2_free = [None, None]
        g2_ops = []
        def ln7_part(j):
            b2 = j % 2
            o_st = ln_stats(j, lambda q, j=j: xres[:, j, q * 512:(q + 1) * 512], [])
            o_n = k.op("dve", lambda e, j=j: e.tensor_scalar(out=xres[:, j, :], in0=xres[:, j, :], scalar1=rstd[:, j:j + 1], scalar2=nmr[:, j:j + 1],
                                                          op0=ALU.mult, op1=ALU.add), [o_st])
            return k.op("act", lambda e, j=j, b2=b2: e.activation(out=nb2[b2], in_=xres[:, j, :], func=AF.Copy), [o_n, nb2_free[b2]])

        o_cs = {0: ln7_part(0)}
        for j in range(8):
            b2 = j % 2
            if j + 1 < 8:
                o_cs[j + 1] = ln7_part(j + 1)
            o_c = o_cs[j]
            evs = transposes_to_hT(nb2[b2], lambda kc, j=j: HT[:, kc, j * 128:(j + 1) * 128], G2, B2, [o_c, oa, ob])
            nb2_free[b2] = k.last("pe")
            if j % 2 == 1:
                blk = j // 2
                g2_ops.append(ada_block(5 * D + blk * 512, dst=slot[0][:, blk * 512:(blk + 1) * 512]))
        k.barrier()

        aT = carve(RB, 0, 32 * KB, BF16).rearrange("p (f n) -> p f n", f=16)
        tf = [slot[1][:, i * 512:(i + 1) * 512] for i in range(4)]
        tf_free = [None] * 4
        rl = [carve(RB, 32 * KB, 1024, BF16), carve(RB, 33 * KB, 1024, BF16)]
        rl_free = [None] * 2
        r2_ops = {}
        rix = [0]
        a_free = [None]
        for qd in range(4):
            a_ops = []
            for fb in range(4):
                buf, dops, bi = ws.load([(lambda b_: b_[:], wsrc(w_ff1, 0, D, qd * 2048 + fb * 512, 512))])
                last = None
                for f4 in range(4):
                    f = fb * 4 + f4
                    for half in range(2):
                        hs = slice(half * 512, (half + 1) * 512)
                        bk, fr_ = banks.get()
                        last = mm_group(bk, fr_, 16, lambda kc, f4=f4, buf=buf: buf[:, kc, f4 * 128:(f4 + 1) * 128], lambda kc, hs=hs: HT[:, kc, hs], dops)
                        ri = rix[0] % 2; rix[0] += 1
                        o1 = k.op("act", lambda e, bk=bk, ri=ri: e.activation(out=rl[ri], in_=banks.f32(bk), func=AF.Relu), [last, rl_free[ri]])
                        banks.rel(bk, o1)
                        o2 = k.op("dve", lambda e, ri=ri, f=f, hs=hs: e.tensor_tensor(out=aT[:, f, hs], in0=rl[ri], in1=rl[ri], op=ALU.mult), [o1, a_free[0]])
                        rl_free[ri] = o2
                        a_ops.append(o2)
                ws.release(bi, last)
                if qd == 0:
                    for j in (2 * fb, 2 * fb + 1):
                        o_r = k.op("dve", lambda e, j=j: e.tensor_tensor(out=xres[:, j, :], in0=xres[:, j, :], in1=slot[1], op=ALU.mult), [o_l1])
                        r2_ops[j] = k.op("dve", lambda e, j=j: e.tensor_tensor(out=xres[:, j, :], in0=xres[:, j, :], in1=slot[2], op=ALU.add), [o_r, o_l2])
                    if fb == 3:
                        tf_free = [list(r2_ops.values())] * 4
            lastq = []
            for nb in range(4):
                ns = slice(nb * 512, (nb + 1) * 512)
                buf, dops, bi = ws.load([(lambda b_: b_[:], wsrc(w_ff2, qd * 2048, 2048, nb * 512, 512))])
                last = None
                for j in range(8):
                    js = slice(j * 128, (j + 1) * 128)
                    bk, fr_ = banks.get()
                    last = mm_group(bk, fr_, 16, lambda f, js=js: aT[:, f, js], lambda f, buf=buf: buf[:, f, :], dops + a_ops)
                    ti = tix[0] % 4; tix[0] += 1
                    o1 = k.op("dve", lambda e, bk=bk, ti=ti, ns=ns: e.tensor_tensor(out=tf[ti], in0=banks.f32(bk), in1=slot[0][:, ns], op=ALU.mult), [last, tf_free[ti]] + g2_ops)
                    banks.rel(bk, o1)
                    o2 = k.op("dve", lambda e, ti=ti, j=j, ns=ns: e.tensor_tensor(out=xres[:, j, ns], in0=xres[:, j, ns], in1=tf[ti], op=ALU.add), [o1, r2_ops[j]])
                    tf_free[ti] = o2
                lastq.append(last)
                ws.release(bi, last)
            a_free[0] = lastq
        k.barrier()

        d_l1 = k.dma("sp", slot[1], ln2_g.broadcast_to([128, D]))
        d_l2 = k.dma("sp", slot[2], ln2_b.broadcast_to([128, D]))
        outv = out_d.rearrange("(c j) d -> c j d", j=8)
        outs = []
        for j in range(8):
            o_st = ln_stats(j, lambda q, j=j: xres[:, j, q * 512:(q + 1) * 512], [])
            o_n = k.op("act", lambda e, j=j: e.activation(out=xres[:, j, :], in_=xres[:, j, :], func=AF.Identity, scale=rstd[:, j:j + 1], bias=nmr[:, j:j + 1]), [o_st])
            o_g = k.op("pool", lambda e, j=j: e.tensor_tensor(out=xres[:, j, :], in0=xres[:, j, :], in1=slot[1], op=ALU.mult), [o_n, d_l1])
            o_b = k.op("dve", lambda e, j=j: e.tensor_tensor(out=xres[:, j, :], in0=xres[:, j, :], in1=slot[2], op=ALU.add), [o_g, d_l2])
            outs.append(k.dma("sp", outv[:, j, :], xres[:, j, :], deps=[o_b]))
        k.wait_all("sp", outs)
        dbg_ops.clear()
        finish()
        return nc
    return nc


def _consts(half):
    identf = np.eye(128, dtype=np.float32)
    ih = np.arange(128) // 16
    maskT = (ih[None, :] >= ih[:, None]).astype(np.float32)
    kq = np.arange(1, 9, dtype=np.float32)
    kr = np.arange(7, -1, -1).astype(np.float32)
    kp = -np.arange(1, 9, dtype=np.float32)
    kvec = np.tile(np.concatenate([kq, kr, kp])[None, :], (128, 1)).astype(np.float32)
    iota = np.tile(np.arange(129, dtype=np.float32)[None, :], (128, 1))
    jj, cc = np.meshgrid(np.arange(8), np.arange(128), indexing="ij")
    pos = (half * T + 8 * cc + jj).reshape(-1).astype(np.float32)
    rc = np.stack([1.0 / np.minimum(pos + 1.0, float(w)) for w in (2, 4, 8, 16)]).astype(np.float32)
    return dict(identf=identf, maskT=maskT, kvec=kvec, iota=iota, rc=rc)


def make_in_maps(inputs, cores=range(8)):
    f = lambda a: np.ascontiguousarray(np.asarray(a, dtype=np.float32))
    x = f(inputs["x"])
    shared = dict(
        w_ada=f(inputs["w_ada"][0]), b_ada=f(inputs["b_ada"][0]).reshape(1, -1), w_in=f(inputs["w_in"][0]),
        lam_re=f(inputs["lam_re"][0]), lam_im=f(inputs["lam_im"][0]), log_dt=f(inputs["log_dt"][0]),
        ssm_b_re=f(inputs["ssm_b_re"][0]), ssm_b_im=f(inputs["ssm_b_im"][0]),
        ssm_c_re=f(inputs["ssm_c_re"][0]), ssm_c_im=f(inputs["ssm_c_im"][0]), ssm_d=f(inputs["ssm_d"][0]),
        w_glu_val=f(inputs["w_glu_val"][0]), w_glu_gate=f(inputs["w_glu_gate"][0]),
        w_pool=f(inputs["w_pool"][0]).reshape(1024, 256), pool_scale=f(inputs["pool_scale"][0]),
        w_pool_out=f(inputs["w_pool_out"][0]), w_out=f(inputs["w_out"][0]),
        ln1_g=f(inputs["ln1_g"][0]).reshape(1, -1), ln1_b=f(inputs["ln1_b"][0]).reshape(1, -1),
        w_ff1=f(inputs["w_ff1"][0]), w_ff2=f(inputs["w_ff2"][0]),
        ln2_g=f(inputs["ln2_g"][0]).reshape(1, -1), ln2_b=f(inputs["ln2_b"][0]).reshape(1, -1),
    )
    consts = [_consts(0), _consts(1)]
    zeros = np.zeros((T, D), np.float32)
    maps = []
    for core in cores:
        b, half = core // 2, core % 2
        m = dict(shared)
        m.update(consts[half])
        m["x_own"] = np.ascontiguousarray(x[b, half * T:(half + 1) * T])
        m["x_prev"] = np.ascontiguousarray(x[b, 0:T]) if half == 1 else zeros
        m["c_row"] = f(inputs["c"][b])
        m["flag"] = np.full((128, 1), float(half), np.float32)
        maps.append(m)
    return maps


def kernel(**inputs):
    nc = build()
    maps = make_in_maps(inputs)
    res = run_bass_kernel_spmd(nc, maps, core_ids=list(range(8)))
    out = np.empty((4, 2 * T, D), np.float32)
    for core in range(8):
        b, half = core // 2, core % 2
        out[b, half * T:(half + 1) * T] = np.asarray(res.results[core]["out"], dtype=np.float32)
    return out
```

```python
import numpy as np
from contextlib import ExitStack
import concourse.bass as bass
import concourse.mybir as mybir
from concourse.bass_utils import run_bass_kernel_spmd

F32 = mybir.dt.float32
BF16 = mybir.dt.bfloat16
I32 = mybir.dt.int32
ALU = mybir.AluOpType
AF = mybir.ActivationFunctionType
AX = mybir.AxisListType

D = 2048
T = 1024
NCH = 128
L = 8
DFF = 8192
ALPHA = float(2.0 ** 0.25)
EPS = 1e-5
TWO_PI = float(2.0 * np.pi)
SIN_SCALE = TWO_PI * 0.9999995
GELU_C = 1.5957691216057308

ENGS = ("pe", "act", "dve", "pool", "sp")
CENGS = ("pe", "act", "dve", "pool")


class Op:
    __slots__ = ("kind", "eng", "idx", "sem", "val")

    def __init__(self, kind, eng=None, idx=None, sem=None, val=None):
        self.kind = kind
        self.eng = eng
        self.idx = idx
        self.sem = sem
        self.val = val


class K:
    def __init__(self, nc, es, n_dma_sems=28):
        self.nc = nc
        self.es = es
        self.prog = {e: [] for e in ENGS}
        self.csem = {e: es.enter_context(nc.semaphore("cs_" + e)) for e in CENGS}
        self.cnt = {e: 0 for e in ENGS}
        self.waited = {e: {p: 0 for p in ENGS} for e in ENGS}
        self.dsems = [es.enter_context(nc.semaphore("ds%d" % i)) for i in range(n_dma_sems)]
        self.dval = [0] * n_dma_sems
        self.dlast = [None] * n_dma_sems
        self.dnext = 0
        self.dnext_sw = 0
        self.n_hw = n_dma_sems - 6
        self.dwaited = {e: [0] * n_dma_sems for e in ENGS}
        self.nsb = 0

    def sb(self, shape, dtype, name=None):
        self.nsb += 1
        return self.es.enter_context(self.nc.sbuf_tensor("s_" + (name or ("sb%d" % self.nsb)), list(shape), dtype))

    def ps(self, shape, dtype, name=None):
        self.nsb += 1
        return self.es.enter_context(self.nc.psum_tensor(name or ("ps%d" % self.nsb), list(shape), dtype))

    def _waits(self, e, deps):
        out = []
        for d in deps:
            if d is None:
                continue
            if isinstance(d, (list, tuple)):
                out += self._waits(e, d)
                continue
            if d.kind == "c":
                if self.waited[e][d.eng] < d.idx:
                    self.waited[e][d.eng] = d.idx
                    out.append((self.csem[d.eng], d.idx))
            else:
                if self.dwaited[e][d.sem] < d.val:
                    self.dwaited[e][d.sem] = d.val
                    out.append((self.dsems[d.sem], d.val))
        return out

    def op(self, e, fn, deps=(), signal=True):
        ws = self._waits(e, deps)
        o = None
        sem = None
        if signal:
            self.cnt[e] += 1
            o = Op("c", eng=e, idx=self.cnt[e])
            sem = self.csem[e]

        def run(engine, ws=ws, fn=fn, sem=sem):
            for (s, v) in ws:
                engine.wait_ge(s, v)
            ins = fn(engine)
            if sem is not None:
                ins.then_inc(sem, 1)

        self.prog[e].append(run)
        return o

    def dma(self, q, out, in_, deps=(), **kw):
        if q == "pool":
            slot = self.n_hw + self.dnext_sw
            self.dnext_sw = (self.dnext_sw + 1) % (len(self.dsems) - self.n_hw)
        else:
            slot = self.dnext
            self.dnext = (self.dnext + 1) % self.n_hw
        deps = list(deps)
        if self.dlast[slot] is not None:
            deps.append(self.dlast[slot])
        ws = self._waits(q, deps)
        self.dval[slot] += 16
        o = Op("d", sem=slot, val=self.dval[slot])
        self.dlast[slot] = o
        s = self.dsems[slot]

        def run(engine, ws=ws, s=s, out=out, in_=in_, kw=kw):
            for (ss, v) in ws:
                engine.wait_ge(ss, v)
            engine.dma_start(out=out, in_=in_, **kw).then_inc(s, 16)

        self.prog[q].append(run)
        return o

    def wait_all(self, e, deps):
        ws = self._waits(e, deps)

        def run(engine, ws=ws):
            for (s, v) in ws:
                engine.wait_ge(s, v)

        self.prog[e].append(run)

    def last(self, e):
        return Op("c", eng=e, idx=self.cnt[e]) if self.cnt[e] > 0 else None

    def barrier(self):
        deps = [self.last(e) for e in CENGS] + [o for o in self.dlast if o is not None]
        for e in ENGS:
            self.wait_all(e, deps)

    def build(self):
        nc = self.nc
        with nc.Block() as block:
            @block.tensor
            def _(eng):
                for f in self.prog["pe"]:
                    f(eng)

            @block.scalar
            def _(eng):
                for f in self.prog["act"]:
                    f(eng)

            @block.vector
            def _(eng):
                for f in self.prog["dve"]:
                    f(eng)

            @block.gpsimd
            def _(eng):
                for f in self.prog["pool"]:
                    f(eng)

            @block.sync
            def _(eng):
                for f in self.prog["sp"]:
                    f(eng)


class Banks:
    def __init__(self, k):
        self.k = k
        self.t = [k.ps([128, 512], F32, name="bank%d" % i) for i in range(8)]
        self.free_op = [None] * 8
        self.nxt = 0

    def get(self):
        i = self.nxt
        self.nxt = (self.nxt + 1) % 8
        return i, self.free_op[i]

    def rel(self, i, op):
        self.free_op[i] = op

    def f32(self, i):
        return self.t[i][:]

    def bf(self, i):
        return self.t[i].bitcast(BF16)[:]


class WStream:
    def __init__(self, k, nbuf):
        self.k = k
        self.bufs = [k.sb([128, 16, 512], BF16, name="wb%d" % i) for i in range(nbuf)]
        self.rel_op = [None] * nbuf
        self.nxt = 0

    def load(self, parts):
        i = self.nxt
        self.nxt = (self.nxt + 1) % len(self.bufs)
        buf = self.bufs[i]
        ops = []
        for (dst_fn, src) in parts:
            ops.append(self.k.dma("pool", dst_fn(buf), src, deps=[self.rel_op[i]]))
        return buf, ops, i

    def release(self, i, op):
        self.rel_op[i] = op


def wsrc(w, r0, nrows, c0, ncols):
    return w[r0:r0 + nrows, c0:c0 + ncols].rearrange("(kc p) n -> p kc n", p=128)


def build(debug=None):
    nc = bass.Bass("TRN2", target_bir_lowering=False)

    def din(name, shape, dt=F32):
        return nc.dram_tensor(name, list(shape), dt, kind="ExternalInput").ap()

    x_own = din("x_own", [T, D])
    x_prev = din("x_prev", [T, D])
    c_row = din("c_row", [D])
    flag_d = din("flag", [128, 1])
    w_ada = din("w_ada", [D, 6 * D])
    b_ada = din("b_ada", [1, 6 * D])
    w_in = din("w_in", [D, 6144])
    lam_re = din("lam_re", [64, 64])
    lam_im = din("lam_im", [64, 64])
    log_dt = din("log_dt", [64])
    sb_re = din("ssm_b_re", [64, 64, 16])
    sb_im = din("ssm_b_im", [64, 64, 16])
    sc_re = din("ssm_c_re", [64, 16, 64])
    sc_im = din("ssm_c_im", [64, 16, 64])
    ssm_d = din("ssm_d", [1024])
    w_val = din("w_glu_val", [1024, D])
    w_gate = din("w_glu_gate", [1024, D])
    w_pool = din("w_pool", [1024, 256])
    pool_scale = din("pool_scale", [1024])
    w_pool_out = din("w_pool_out", [1024, D])
    w_out = din("w_out", [D, D])
    ln1_g = din("ln1_g", [1, D])
    ln1_b = din("ln1_b", [1, D])
    w_ff1 = din("w_ff1", [D, DFF])
    w_ff2 = din("w_ff2", [DFF, D])
    ln2_g = din("ln2_g", [1, D])
    ln2_b = din("ln2_b", [1, D])
    identf_d = din("identf", [128, 128])
    maskT_d = din("maskT", [128, 128])
    kvec_d = din("kvec", [128, 24])
    iota_d = din("iota", [128, 129])
    rc_d = din("rc", [4, T])
    out_d = nc.dram_tensor("out", [T, D], F32, kind="ExternalOutput").ap()
    dbg_d = None
    if debug is not None:
        dbg_d = nc.dram_tensor("dbg", [128, 16384], F32, kind="ExternalOutput").ap()

    es = ExitStack()
    with es:
        k = K(nc, es)
        banks = Banks(k)
        ws = WStream(k, 2)

        HT = k.sb([128, 16, 1024], BF16, name="HT")
        BIG = k.sb([128, 45056], BF16, name="BIG")
        RB = k.sb([128, 17408], BF16, name="RB")
        identf = k.sb([128, 128], F32, name="identf")
        identb = k.sb([128, 128], BF16, name="identb")
        maskT = k.sb([128, 128], F32, name="maskT")
        kvec = k.sb([128, 24], F32, name="kvec")
        iota = k.sb([128, 129], F32, name="iota")
        flag = k.sb([128, 1], F32, name="flag")
        crep = k.sb([128, 16, 128], BF16, name="crep")
        modA = k.sb([128, 4, 16], F32, name="modA")
        sc1p = k.sb([128, 16], F32, name="sc1p")
        G2 = k.sb([128, 16], F32, name="G2")
        B2 = k.sb([128, 16], F32, name="B2")
        lnA = k.sb([128, 2, 16], F32, name="lnA")
        pscale = k.sb([128, 8], F32, name="pscale")
        th8n = k.sb([128, 32], F32, name="th8n")
        m8 = k.sb([128, 32], F32, name="m8")
        tail = k.sb([128, 16, 16], BF16, name="tail")
        xinit = k.sb([128, 2, 32], F32, name="xinit")
        stat = k.sb([128, 4, 6], F32, name="stat")
        mv = k.sb([128, 8, 2], F32, name="mv")
        rstd = k.sb([128, 8], F32, name="rstd")
        nmr = k.sb([128, 8], F32, name="nmr")
        sdv = k.sb([128, 8], F32, name="sdv")

        def carve(reg, off, n, dt):
            v = reg[:, off // 2:(off + n) // 2]
            return v.bitcast(dt) if dt != BF16 else v

        KB = 1024
        MinT = carve(BIG, 40 * KB, 16 * KB, BF16).rearrange("p (g r m) -> p g r m", g=32, r=2)
        Tsb = carve(BIG, 56 * KB, 16 * KB, BF16).rearrange("p (g n) -> p g n", g=64)
        Cre = carve(BIG, 72 * KB, 8 * KB, BF16).rearrange("p (g n) -> p g n", g=32)
        Cimn = carve(BIG, 80 * KB, 8 * KB, BF16).rearrange("p (g n) -> p g n", g=32)
        TC = carve(RB, 0, 16512, F32).rearrange("p (g c) -> p g c", g=32)
        TS = carve(RB, 16512, 16512, F32).rearrange("p (g c) -> p g c", g=32)

        dbg_ops = []

        def dump(ap2d, col0, deps):
            n = ap2d.shape[1]
            dbg_ops.append(k.dma("sp", dbg_d[:, col0:col0 + n], ap2d, deps=deps))

        def finish():
            k.barrier()
            k.wait_all("sp", dbg_ops)
            k.build()

        kd = lambda *a_, **kw_: k.dma(*a_, allow_slow_non_contiguous=True, **kw_)
        d_id = kd("sp", identf[:], identf_d)
        d_mk = kd("sp", maskT[:], maskT_d)
        d_kv = kd("sp", kvec[:], kvec_d)
        d_io = kd("sp", iota[:], iota_d)
        d_fl = kd("sp", flag[:], flag_d)
        cT = carve(BIG, 0, 64, F32)
        lr = carve(BIG, 128, 128, F32)
        li = carve(BIG, 256, 128, F32)
        ldt = carve(BIG, 384, 128, F32)
        lll = carve(BIG, 128, 384, F32)
        bre = carve(BIG, 1 * KB, 2 * KB, F32).rearrange("p (g h) -> p g h", g=32)
        bim = carve(BIG, 3 * KB, 2 * KB, F32).rearrange("p (g h) -> p g h", g=32)
        dvec = carve(BIG, 9 * KB, 256, F32)
        Cq_re = carve(BIG, 18 * KB, 2 * KB, F32).rearrange("p (g h) -> p g h", g=32)
        Cq_im = carve(BIG, 20 * KB, 2 * KB, F32).rearrange("p (g h) -> p g h", g=32)
        bA = [carve(RB, 0, 8 * KB, F32), carve(RB, 8 * KB, 8 * KB, F32)]
        cA = [carve(RB, 16 * KB, 8 * KB, F32), carve(RB, 24 * KB, 8 * KB, F32)]
        lamA = [carve(RB, 32 * KB + 512 * i, 512, F32) for i in range(3)]
        ld2 = carve(RB, 32 * KB + 1536, 8, F32)
        vst = [carve(BIG, 35 * KB + 512 * i, 512, F32) for i in range(4)]
        dsm = carve(BIG, 37 * KB, 64, F32)
        dst_ = carve(BIG, 37 * KB + 64, 512, F32)
        d_bA = [kd("sp", bA[0][0:32, :], sb_re.rearrange("(gq g2) p h -> gq (g2 p h)", g2=2)),
                kd("sp", bA[1][0:32, :], sb_im.rearrange("(gq g2) p h -> gq (g2 p h)", g2=2))]
        d_cA = [kd("sp", cA[0][0:32, :], sc_re.rearrange("(gq g2) o p -> gq (g2 o p)", g2=2)),
                kd("sp", cA[1][0:32, :], sc_im.rearrange("(gq g2) o p -> gq (g2 o p)", g2=2))]
        d_lam = [kd("sp", lamA[0][0:32, :], lam_re.rearrange("(gq g2) p -> gq (g2 p)", g2=2)),
                 kd("sp", lamA[1][0:32, :], lam_im.rearrange("(gq g2) p -> gq (g2 p)", g2=2)),
                 kd("sp", ld2[0:32, :], log_dt.rearrange("(gq g2) -> gq g2", g2=2))]
        d_v = [kd("sp", vst[0][0:16, :], c_row.rearrange("(kc p) -> kc p", p=128)),
               kd("sp", vst[1][0:16, :], ln1_g[0].rearrange("(kc p) -> kc p", p=128)),
               kd("sp", vst[2][0:16, :], ln1_b[0].rearrange("(kc p) -> kc p", p=128)),
               kd("sp", vst[3][0:8, :], pool_scale.rearrange("(kc p) -> kc p", p=128))]
        d_ds = kd("sp", dsm[0:64, :], ssm_d.rearrange("(g h) -> g h", h=16))
        o_ldA = k.op("dve", lambda e: e.tensor_copy(out=lamA[2][0:32, :].rearrange("q (a p) -> q a p", a=2), in_=ld2[0:32, :].unsqueeze(2).broadcast_to([32, 2, 64])), [d_lam[2]])
        o_dst = k.op("dve", lambda e: e.tensor_copy(out=dst_[0:64, :].rearrange("q (i h) -> q i h", i=8), in_=dsm[0:64, :].unsqueeze(1).broadcast_to([64, 8, 16])), [d_ds])

        def tr32(bk, col0, ncol, in_ap, npart, deps, sig):
            return k.op("pe", lambda e: e.transpose(out=banks.f32(bk)[:, col0:col0 + ncol], in_=in_ap, identity=identf[0:npart, 0:npart]), deps, signal=sig)

        bk, fr_ = banks.get()
        tr32(bk, 0, 32, lamA[0][0:32, :], 32, [fr_, d_id, d_lam[0]], False)
        tr32(bk, 32, 32, lamA[1][0:32, :], 32, [d_lam[1]], False)
        last = tr32(bk, 64, 32, lamA[2][0:32, :], 32, [o_ldA], True)
        d_lr = k.op("dve", lambda e, bk=bk: e.tensor_copy(out=lll, in_=banks.f32(bk)[:, 0:96]), [last])
        banks.rel(bk, d_lr)
        d_li = d_ld0 = d_ld1 = d_lr
        bk, fr_ = banks.get()
        tr32(bk, 0, 16, vst[0][0:16, :], 16, [fr_, d_v[0]], False)
        tr32(bk, 16, 16, vst[1][0:16, :], 16, [d_v[1]], False)
        tr32(bk, 32, 16, vst[2][0:16, :], 16, [d_v[2]], False)
        tr32(bk, 48, 8, vst[3][0:8, :], 8, [d_v[3]], False)
        last = tr32(bk, 64, 64, dst_[0:64, :], 64, [o_dst], True)
        d_c = k.op("dve", lambda e, bk=bk: e.tensor_copy(out=cT, in_=banks.f32(bk)[:, 0:16]), [last])
        d_lnA0 = k.op("dve", lambda e, bk=bk: e.tensor_copy(out=lnA[:].rearrange("p a b -> p (a b)"), in_=banks.f32(bk)[:, 16:48]), [last])
        d_lnA1 = d_lnA0
        d_ps = k.op("dve", lambda e, bk=bk: e.tensor_copy(out=pscale[:], in_=banks.f32(bk)[:, 48:56]), [last])
        o_dv = k.op("dve", lambda e, bk=bk: e.tensor_copy(out=dvec, in_=banks.f32(bk)[:, 64:128]), [last])
        banks.rel(bk, o_dv)
        d_dv = [o_dv]
        d_bc = []
        HT32 = HT[:].rearrange("p a b -> p (a b)").bitcast(F32)
        for si_, (src, dstv, dd_, is_c) in enumerate( ((bA[0], bre, d_bA[0], False), (bA[1], bim, d_bA[1], False), (cA[0], Cq_re, d_cA[0], True), (cA[1], Cq_im, d_cA[1], True))):
            st2 = HT32[:, si_ * 2048:(si_ + 1) * 2048]
            if is_c:
                pin = src[0:32, :].rearrange("q (g o p) -> q o g p", g=2, o=16)
            else:
                pin = src[0:32, :].rearrange("q (g p h) -> q h g p", g=2, p=64)
            o_pm = k.op("dve", lambda e, st2=st2, pin=pin: e.tensor_copy(out=st2[0:32, :].rearrange("q (h g p) -> q h g p", h=16, g=2), in_=pin), [dd_])
            bk, fr_ = banks.get()
            last = None
            for h in range(16):
                inp = st2[0:32, h * 128:(h + 1) * 128]
                last = tr32(bk, h * 32, 32, inp, 32, [fr_, o_pm] if h == 0 else [], h == 15)
            o = k.op("dve", lambda e, bk=bk, dstv=dstv: e.tensor_copy(out=dstv, in_=banks.f32(bk).rearrange("p (h g) -> p g h", h=16)), [last])
            banks.rel(bk, o)
            d_bc.append(o)
        d_br, d_bi = d_bc[0], d_bc[1]
        d_cst = [d_bc[2], d_bc[3]]

        o_idb = k.op("dve", lambda e: e.tensor_copy(out=identb[:], in_=identf[:]), [d_id])

        cact = carve(BIG, 9 * KB + 256, 64, F32)
        o_silu = k.op("act", lambda e: e.activation(out=cact, in_=cT, func=AF.Silu), [d_c])
        o_crep = k.op("dve", lambda e: e.tensor_copy(out=crep[:], in_=cact.unsqueeze(2).broadcast_to([128, 16, 128])), [o_silu])

        ADA = k.sb([128, 3, 512], F32, name="ADA")
        bada = ADA[:, 0, :]
        modbc = ADA[:, 1, :]
        dtmp = ADA[:, 2, :]
        state = {"bada_free": None, "modbc_free": None}

        def ada_block(n0, dst=None):
            dst = modbc if dst is None else dst
            buf, dops, bi = ws.load([(lambda b: b[:], wsrc(w_ada, 0, D, n0, 512))])
            d_b = k.dma("sp", bada, b_ada[:, n0:n0 + 512].broadcast_to([128, 512]), deps=[state["bada_free"]])
            bk, fr = banks.get()
            last = None
            for kc in range(16):
                last = k.op("pe", lambda e, kc=kc, bk=bk, buf=buf: e.matmul(banks.f32(bk), lhsT=crep[:, kc, :], rhs=buf[:, kc, :],
                                                                            start=(kc == 0), stop=(kc == 15)),
                            [fr, o_crep] + dops if kc == 0 else (), signal=(kc == 15))
            ws.release(bi, last)
            o = k.op("dve", lambda e, bk=bk, dst=dst: e.tensor_tensor(out=dst, in0=banks.f32(bk), in1=bada, op=ALU.add),
                     [last, d_b, state["modbc_free"]])
            banks.rel(bk, o)
            state["bada_free"] = o
            return o

        def ada_vec_a(vi, dst_fn, blks=range(4)):
            outs = []
            for blk in blks:
                o = ada_block(vi * D + blk * 512)
                o2 = k.op("dve", lambda e: e.tensor_tensor(out=dtmp.rearrange("p (a b) -> p a b", a=4),
                                                          in0=modbc.rearrange("p (a b) -> p a b", a=4),
                                                          in1=identf[:].unsqueeze(1).broadcast_to([128, 4, 128]), op=ALU.mult), [o, d_id])
                o3 = k.op("dve", lambda e, blk=blk: e.tensor_reduce(out=dst_fn(blk), in_=dtmp.rearrange("p (a b) -> p a b", a=4),
                                                                    axis=AX.X, op=ALU.add), [o2])
                state["modbc_free"] = o3
                outs.append(o3)
            return outs

        o_sh1 = []
        o_sc1 = []
        pending = []
        for blk_ in range(4):
            pending.append(lambda blk_=blk_: o_sh1.extend(ada_vec_a(0, lambda blk: modA[:, 0, 4 * blk:4 * blk + 4], blks=(blk_,))))
        for blk_ in range(4):
            pending.append(lambda blk_=blk_: o_sc1.extend(ada_vec_a(1, lambda blk: modA[:, 1, 4 * blk:4 * blk + 4], blks=(blk_,))))
        tickc = [0]

        def tick():
            tickc[0] += 1
            if tickc[0] % 10 == 0 and pending:
                pending.pop(0)()

        pending.pop(0)()
        pending.pop(0)()


        SM = lambda i: carve(BIG, 16 * KB + 128 * i, 128, F32)
        dt_t, a_t, th_t, mag1, cos1, sin1, abm1, abim, den, rden, fre, fim, tA, tB, tC, thn = [SM(i) for i in range(16)]
        Cq_re = carve(BIG, 18 * KB, 2 * KB, F32).rearrange("p (g h) -> p g h", g=32)
        Cq_im = carve(BIG, 20 * KB, 2 * KB, F32).rearrange("p (g h) -> p g h", g=32)
        bb_re = carve(BIG, 22 * KB, 2 * KB, F32).rearrange("p (g h) -> p g h", g=32)
        bb_im = carve(BIG, 24 * KB, 2 * KB, F32).rearrange("p (g h) -> p g h", g=32)
        LT = lambda i: carve(BIG, 26 * KB + 1024 * i, 1024, F32).rearrange("p (g k) -> p g k", g=32)
        LTi = carve(BIG, 26 * KB + 1024 * 7, 1024, I32).rearrange("p (g k) -> p g k", g=32)
        smi = carve(BIG, 34 * KB, 128, I32)
        smf = carve(BIG, 34 * KB + 128, 128, F32)
        smr = carve(BIG, 34 * KB + 256, 128, F32)
        HTb = HT[:].rearrange("p a b -> p (a b)")
        Rm_re = HTb[:, 0:4096].rearrange("p (g n) -> p g n", g=32)
        Rm_im = HTb[:, 4096:8192].rearrange("p (g n) -> p g n", g=32)
        Rp_re = HTb[:, 8192:12288].rearrange("p (g n) -> p g n", g=32)
        Rp_im = HTb[:, 12288:16384].rearrange("p (g n) -> p g n", g=32)
        P1 = carve(RB, 0, 16 * KB, F32).rearrange("p (g k h) -> p g k h", g=32, k=8)
        P2 = carve(RB, 16 * KB, 16 * KB, F32).rearrange("p (g k h) -> p g k h", g=32, k=8)

        ch = {"dve": None, "act": None}

        def dv(fn, deps=()):
            o = k.op("dve", fn, list(deps) + [ch["dve"], ch["act"]])
            ch["dve"] = o
            tick()
            return o

        def ac(fn, deps=()):
            o = k.op("act", fn, list(deps) + [ch["dve"], ch["act"]])
            ch["act"] = o
            return o

        def sincos(sin_out, cos_out, turns, ti, tf, fr, shape_fn=lambda v: v):
            for (dst, off) in ((sin_out, 0.0), (cos_out, 0.25)):
                if off != 0.0:
                    dv(lambda e: e.tensor_scalar(out=fr, in0=turns, scalar1=off, scalar2=None, op0=ALU.add))
                    src = fr
                else:
                    src = turns
                dv(lambda e, src=src: e.tensor_copy(out=ti, in_=src))
                dv(lambda e: e.tensor_copy(out=tf, in_=ti))
                dv(lambda e, src=src: e.tensor_tensor(out=fr, in0=src, in1=tf, op=ALU.subtract))
                ac(lambda e, dst=dst: e.activation(out=dst, in_=fr, func=AF.Sin, scale=SIN_SCALE))

        ac(lambda e: e.activation(out=dt_t, in_=ldt, func=AF.Exp), [d_ld0, d_ld1])
        dv(lambda e: e.tensor_tensor(out=a_t, in0=lr, in1=dt_t, op=ALU.mult), [d_lr])
        dv(lambda e: e.tensor_tensor(out=th_t, in0=li, in1=dt_t, op=ALU.mult), [d_li])
        dv(lambda e: e.tensor_scalar(out=thn, in0=th_t, scalar1=1.0 / TWO_PI, scalar2=None, op0=ALU.mult))
        ac(lambda e: e.activation(out=mag1, in_=a_t, func=AF.Exp))
        sincos(sin1, cos1, thn, smi, smf, smr)
        dv(lambda e: e.tensor_tensor(out=abim, in0=mag1, in1=sin1, op=ALU.mult))
        dv(lambda e: e.tensor_tensor(out=abm1, in0=mag1, in1=cos1, op=ALU.mult))
        dv(lambda e: e.tensor_scalar(out=abm1, in0=abm1, scalar1=-1.0, scalar2=None, op0=ALU.add))
        dv(lambda e: e.tensor_tensor(out=tA, in0=lr, in1=lr, op=ALU.mult))
        dv(lambda e: e.tensor_tensor(out=tB, in0=li, in1=li, op=ALU.mult))
        dv(lambda e: e.tensor_tensor(out=den, in0=tA, in1=tB, op=ALU.add))
        dv(lambda e: e.reciprocal(out=rden, in_=den))
        dv(lambda e: e.tensor_tensor(out=tA, in0=abm1, in1=lr, op=ALU.mult))
        dv(lambda e: e.tensor_tensor(out=tB, in0=abim, in1=li, op=ALU.mult))
        dv(lambda e: e.tensor_tensor(out=tA, in0=tA, in1=tB, op=ALU.add))
        dv(lambda e: e.tensor_tensor(out=fre, in0=tA, in1=rden, op=ALU.mult))
        dv(lambda e: e.tensor_tensor(out=tA, in0=abim, in1=lr, op=ALU.mult))
        dv(lambda e: e.tensor_tensor(out=tB, in0=abm1, in1=li, op=ALU.mult))
        dv(lambda e: e.tensor_tensor(out=tA, in0=tA, in1=tB, op=ALU.subtract))
        dv(lambda e: e.tensor_tensor(out=fim, in0=tA, in1=rden, op=ALU.mult))
        dv(lambda e: e.tensor_scalar(out=th8n[:], in0=thn, scalar1=8.0, scalar2=None, op0=ALU.mult))
        ac(lambda e: e.activation(out=m8[:], in_=a_t, func=AF.Exp, scale=8.0))
        bc16 = lambda v: v.unsqueeze(2).broadcast_to([128, 32, 16])
        T1 = carve(RB, 0, 2 * KB, F32).rearrange("p (g h) -> p g h", g=32)
        T2 = carve(RB, 2 * KB, 2 * KB, F32).rearrange("p (g h) -> p g h", g=32)
        dv(lambda e: e.tensor_tensor(out=T1, in0=bre, in1=bc16(fre), op=ALU.mult), [d_br])
        dv(lambda e: e.tensor_tensor(out=T2, in0=bim, in1=bc16(fim), op=ALU.mult), [d_bi])
        dv(lambda e: e.tensor_tensor(out=bb_re, in0=T1, in1=T2, op=ALU.subtract))
        dv(lambda e: e.tensor_tensor(out=T1, in0=bim, in1=bc16(fre), op=ALU.mult))
        dv(lambda e: e.tensor_tensor(out=T2, in0=bre, in1=bc16(fim), op=ALU.mult))
        dv(lambda e: e.tensor_tensor(out=bb_im, in0=T1, in1=T2, op=ALU.add))
        def lamtab(ks):
            kv = kvec[:, ks * 8:ks * 8 + 8].unsqueeze(1).broadcast_to([128, 32, 8])
            b8 = lambda v: v.unsqueeze(2).broadcast_to([128, 32, 8])
            dv(lambda e: e.tensor_tensor(out=LT(2), in0=b8(thn), in1=kv, op=ALU.mult), [d_kv])
            dv(lambda e: e.tensor_tensor(out=LT(3), in0=b8(a_t), in1=kv, op=ALU.mult))
            ac(lambda e: e.activation(out=LT(3), in_=LT(3), func=AF.Exp))
            sincos(LT(4), LT(5), LT(2), LTi, LT(6), LT(0))
            dv(lambda e: e.tensor_tensor(out=LT(0), in0=LT(3), in1=LT(5), op=ALU.mult))
            dv(lambda e: e.tensor_tensor(out=LT(1), in0=LT(3), in1=LT(4), op=ALU.mult))

        bk16 = lambda v: v.unsqueeze(3).broadcast_to([128, 32, 8, 16])
        bj8 = lambda v: v.unsqueeze(2).broadcast_to([128, 32, 8, 16])
        v4 = lambda v: v.rearrange("p g (k h) -> p g k h", k=8)

        def cprod(out_re, out_im, Xre, Xim, neg_im=False):
            dv(lambda e: e.tensor_tensor(out=P1, in0=bj8(Xre), in1=bk16(LT(0)), op=ALU.mult))
            dv(lambda e: e.tensor_tensor(out=P2, in0=bj8(Xim), in1=bk16(LT(1)), op=ALU.mult))
            dv(lambda e: e.tensor_tensor(out=v4(out_re), in0=P1, in1=P2, op=ALU.subtract))
            dv(lambda e: e.tensor_tensor(out=P1, in0=bj8(Xre), in1=bk16(LT(1)), op=ALU.mult))
            dv(lambda e: e.tensor_tensor(out=P2, in0=bj8(Xim), in1=bk16(LT(0)), op=ALU.mult))
            if neg_im:
                dv(lambda e: e.scalar_tensor_tensor(out=out_im, in0=P1.rearrange("p g k h -> p g (k h)"), scalar=-1.0,
                                                    in1=P2.rearrange("p g k h -> p g (k h)"), op0=ALU.mult, op1=ALU.subtract))
            else:
                dv(lambda e: e.tensor_tensor(out=v4(out_im), in0=P1, in1=P2, op=ALU.add))

        lamtab(0)
        ch["dve"] = [ch["dve"]] + d_cst
        cprod(Cre, Cimn, Cq_re, Cq_im, neg_im=True)
        lamtab(1)
        cprod(Rm_re, Rm_im, bb_re, bb_im)
        lamtab(2)
        o_rp = cprod(Rp_re, Rp_im, bb_re, bb_im)
        o_params = ch["dve"]
        while pending:
            pending.pop(0)()
        o_sc1p = k.op("dve", lambda e: e.tensor_scalar(out=sc1p[:], in0=modA[:, 1, :], scalar1=1.0, scalar2=None, op0=ALU.add), o_sc1)
        if debug == "b1":
            finish()
            return nc

        for r, Rm in enumerate((Rm_re, Rm_im)):
            for g8 in range(4):
                bk, fr_ = banks.get()
                last = None
                for gi in range(8):
                    gq = g8 * 8 + gi
                    last = k.op("pe", lambda e, gq=gq, gi=gi, bk=bk, Rm=Rm: e.transpose(out=banks.bf(bk)[:, gi * 128:(gi + 1) * 128],
                                                                                      in_=Rm[:, gq, :], identity=identb[:]),
                                [fr_, o_params, o_idb] if gi == 0 else (), signal=(gi == 7))
                o = k.op("act", lambda e, g8=g8, r=r, bk=bk: e.activation(out=MinT[:, g8 * 8:(g8 + 1) * 8, r, :],
                                                                      in_=banks.bf(bk).rearrange("p (g m) -> p g m", g=8), func=AF.Copy), [last])
                banks.rel(bk, o)
        o_minT = k.last("act")
        if debug == "b2":
            finish()
            return nc

        Ttmp = carve(BIG, 10 * KB, 2 * KB, F32).rearrange("p (a b) -> p a b", a=4)
        o_T = []
        tfree = None
        for g4 in range(16):
            gqb, g2 = g4 // 2, g4 % 2
            pr = slice(64 * g2, 64 * g2 + 64)
            bk, fr_ = banks.get()
            last = None
            for gi in range(4):
                gq = gqb * 4 + gi
                k.op("pe", lambda e, gq=gq, pr=pr, gi=gi, bk=bk: e.matmul(banks.f32(bk)[:, gi * 128:(gi + 1) * 128], lhsT=Rp_re[pr, gq, :], rhs=Cre[pr, gq, :],
                                                                            start=True, stop=False),
                     [fr_, o_params] if gi == 0 else (), signal=False)
                last = k.op("pe", lambda e, gq=gq, pr=pr, gi=gi, bk=bk: e.matmul(banks.f32(bk)[:, gi * 128:(gi + 1) * 128], lhsT=Rp_im[pr, gq, :], rhs=Cimn[pr, gq, :],
                                                                                   start=False, stop=True), signal=(gi == 3))
            o = k.op("dve", lambda e, bk=bk: e.tensor_tensor(out=Ttmp, in0=banks.f32(bk).rearrange("p (a b) -> p a b", a=4),
                                                            in1=maskT[:].unsqueeze(1).broadcast_to([128, 4, 128]), op=ALU.mult), [last, d_mk, tfree])
            banks.rel(bk, o)
            for gi in range(4):
                g = 2 * (gqb * 4 + gi) + g2
                tfree = k.op("dve", lambda e, g=g, gi=gi: e.scalar_tensor_tensor(out=Tsb[:, g, :], in0=identf[:], scalar=dvec[:, g:g + 1], in1=Ttmp[:, gi, :],
                                                                                op0=ALU.mult, op1=ALU.add), [o] + d_dv)
            o_T.append(tfree)
        if debug == "b3":
            finish()
            return nc
        k.barrier()
        TT = lambda i: carve(BIG, i * 16512, 16512, F32).rearrange("p (g c) -> p g c", g=32)
        TT2 = HT[:].rearrange("p a b -> p (a b)")[:, 0:8256].bitcast(F32).rearrange("p (g c) -> p g c", g=32)
        TTi = carve(BIG, 0, 16512, I32).rearrange("p (g c) -> p g c", g=32)
        ch["dve"] = k.last("dve"); ch["act"] = k.last("act")
        dv(lambda e: e.tensor_tensor(out=TT(1), in0=th8n[:].unsqueeze(2).broadcast_to([128, 32, 129]),
                                     in1=iota[:].unsqueeze(1).broadcast_to([128, 32, 129]), op=ALU.mult), [d_io])
        for (dst, off) in ((TS, 0.0), (TC, 0.25)):
            if off != 0.0:
                dv(lambda e: e.tensor_scalar(out=TT(1), in0=TT(1), scalar1=off, scalar2=None, op0=ALU.add))
            dv(lambda e: e.tensor_copy(out=TTi, in_=TT(1)))
            dv(lambda e: e.tensor_copy(out=TT2, in_=TTi))
            dv(lambda e: e.tensor_tensor(out=TT2, in0=TT(1),
                                         in1=TT2, op=ALU.subtract))
            ac(lambda e, dst=dst: e.activation(out=dst, in_=TT2, func=AF.Sin, scale=SIN_SCALE))
        k.barrier()

        if debug == "s5mats":
            dd = carve(BIG, 0, 32 * KB, F32)
            o1 = k.op("dve", lambda e: e.tensor_copy(out=dd[:, 0:4096], in_=MinT.rearrange("p g r m -> p (g r m)")[:, 0:4096]))
            dump(dd[:, 0:4096], 0, [o1])
            o2 = k.op("dve", lambda e: e.tensor_copy(out=dd[:, 4096:8192], in_=Tsb.rearrange("p g n -> p (g n)")[:, 0:4096]))
            dump(dd[:, 4096:8192], 4096, [o2])
            k.barrier()
            o3 = k.op("dve", lambda e: e.tensor_copy(out=dd[:, 0:4096], in_=Cre.rearrange("p g n -> p (g n)")))
            dump(dd[:, 0:4096], 8192, [o3])
            o4 = k.op("dve", lambda e: e.tensor_copy(out=dd[:, 4096:8192], in_=Cimn.rearrange("p g n -> p (g n)")))
            dump(dd[:, 4096:8192], 12288, [o4])
            finish()
            return nc
        if debug == "tables":
            dump(TC.rearrange("p g c -> p (g c)"), 0, [])
            dump(TS.rearrange("p g c -> p (g c)"), 4128, [])
            dump(m8[:], 8256, [])
            dump(th8n[:], 8288, [])
            finish()
            return nc


        def ln_stats(j, src_fn, deps):
            o = None
            for q in range(4):
                o = k.op("dve", lambda e, q=q: e.bn_stats(out=stat[:, q, :], in_=src_fn(q)), deps if q == 0 else ())
            o = k.op("dve", lambda e: e.bn_aggr(out=mv[:, j, :], in_=stat[:].rearrange("p a b -> p (a b)")), [o])
            o = k.op("act", lambda e: e.activation(out=sdv[:, j:j + 1], in_=mv[:, j, 1:2], func=AF.Sqrt, bias=EPS), [o])
            o = k.op("dve", lambda e: e.reciprocal(out=rstd[:, j:j + 1], in_=sdv[:, j:j + 1]), [o])
            o = k.op("dve", lambda e: e.tensor_scalar(out=nmr[:, j:j + 1], in0=mv[:, j, 0:1], scalar1=rstd[:, j:j + 1], scalar2=-1.0,
                                                     op0=ALU.mult, op1=ALU.mult), [o])
            return o

        def transposes_to_hT(nbf, dst_fn, scaleA, biasA, deps, all_dve=False):
            evs = []
            for q in range(4):
                bk, fr_ = banks.get()
                last = None
                for kk in range(4):
                    kc = 4 * q + kk
                    last = k.op("pe", lambda e, kc=kc, kk=kk, bk=bk: e.transpose(out=banks.bf(bk)[:, kk * 128:(kk + 1) * 128],
                                                                              in_=nbf[:, kc * 128:(kc + 1) * 128], identity=identb[:]),
                                [fr_] + list(deps) if kk == 0 else (), signal=(kk == 3))
                o = None
                for kk in range(4):
                    kc = 4 * q + kk
                    if q % 2 == 0 and not all_dve:
                        o = k.op("act", lambda e, kc=kc, kk=kk, bk=bk: e.activation(out=dst_fn(kc), in_=banks.bf(bk)[:, kk * 128:(kk + 1) * 128], func=AF.Identity,
                                                                                 scale=scaleA[:, kc:kc + 1], bias=biasA[:, kc:kc + 1]), [last])
                    else:
                        o = k.op("dve", lambda e, kc=kc, kk=kk, bk=bk: e.tensor_scalar(out=dst_fn(kc), in0=banks.bf(bk)[:, kk * 128:(kk + 1) * 128],
                                                                                    scalar1=scaleA[:, kc:kc + 1], scalar2=biasA[:, kc:kc + 1],
                                                                                    op0=ALU.mult, op1=ALU.add), [last])
                    evs.append(o)
                banks.rel(bk, evs[-1])
            return evs

        SCR = lambda off, n, dt: carve(BIG, off, n, dt)
        xt = [SCR(16 * KB, 8 * KB, F32), SCR(24 * KB, 8 * KB, F32)]
        nbf = [SCR(32 * KB, 4 * KB, BF16), SCR(36 * KB, 4 * KB, BF16)]
        xt_free = [None, None]
        nbf_free = [None, None]

        wu = []
        wu_d = []
        wu_i = []
        for n2 in range(2):
            buf, dops, bi = ws.load([(lambda b: b[:], wsrc(w_in, 0, D, n2 * 512, 512))])
            wu.append(buf); wu_d.append(dops); wu_i.append(bi)

        def u_mm(hT_fn, j, deps):
            res = []
            last = None
            for n2 in range(2):
                bk, fr_ = banks.get()
                for kc in range(16):
                    last = k.op("pe", lambda e, kc=kc, bk=bk, n2=n2: e.matmul(banks.f32(bk), lhsT=hT_fn(kc), rhs=wu[n2][:, kc, :], start=(kc == 0), stop=(kc == 15)),
                                [fr_] + list(deps) + wu_d[n2] if kc == 0 else (), signal=(kc == 15))
                res.append((bk, last))
            return res, last

        def u_ev(Udst, j, res):
            evs = []
            for n2, (bk, last) in enumerate(res):
                o = k.op("dve", lambda e, bk=bk, n2=n2: e.tensor_copy(out=Udst[:, 32 * n2:32 * n2 + 32, j, :], in_=banks.f32(bk).rearrange("p (g h) -> p g h", g=32)), [last])
                banks.rel(bk, o)
                evs.append(o)
            return evs

        def ln_a(xv, j, extra):
            b2 = j % 2
            d_x = k.dma("sp", xt[b2], xv[:, j, :], deps=[xt_free[b2]] + list(extra))
            o = None
            for q in range(4):
                o = k.op("dve", lambda e, q=q, b2=b2: e.bn_stats(out=stat[:, q, :], in_=xt[b2][:, q * 512:(q + 1) * 512]), [d_x] if q == 0 else ())
            o = k.op("dve", lambda e: e.bn_aggr(out=mv[:, j, :], in_=stat[:].rearrange("p a b -> p (a b)")), [o])
            return k.op("act", lambda e: e.activation(out=sdv[:, j:j + 1], in_=mv[:, j, 1:2], func=AF.Sqrt, bias=EPS), [o])

        def ln_b(j, o_sq):
            b2 = j % 2
            o = k.op("dve", lambda e: e.reciprocal(out=rstd[:, j:j + 1], in_=sdv[:, j:j + 1]), [o_sq])
            o = k.op("dve", lambda e: e.tensor_scalar(out=nmr[:, j:j + 1], in0=mv[:, j, 0:1], scalar1=rstd[:, j:j + 1], scalar2=-1.0,
                                                     op0=ALU.mult, op1=ALU.mult), [o])
            o_n = k.op("dve", lambda e, b2=b2: e.tensor_scalar(out=nbf[b2], in0=xt[b2], scalar1=rstd[:, j:j + 1], scalar2=nmr[:, j:j + 1],
                                                            op0=ALU.mult, op1=ALU.add), [o, nbf_free[b2]])
            xt_free[b2] = o_n
            return o_n

        def u_matmuls(hT_fn, Udst, j, deps):
            evs = []
            last = None
            for n2 in range(2):
                bk, fr_ = banks.get()
                for kc in range(16):
                    last = k.op("pe", lambda e, kc=kc, bk=bk, n2=n2: e.matmul(banks.f32(bk), lhsT=hT_fn(kc), rhs=wu[n2][:, kc, :], start=(kc == 0), stop=(kc == 15)),
                                [fr_] + list(deps) + wu_d[n2] if kc == 0 else (), signal=(kc == 15))
                eng = "act" if n2 == 0 else "dve"
                if eng == "act":
                    o = k.op("act", lambda e, bk=bk, n2=n2: e.activation(out=Udst[:, 32 * n2:32 * n2 + 32, j, :], in_=banks.f32(bk).rearrange("p (g h) -> p g h", g=32), func=AF.Copy), [last])
                else:
                    o = k.op("dve", lambda e, bk=bk, n2=n2: e.tensor_copy(out=Udst[:, 32 * n2:32 * n2 + 32, j, :], in_=banks.f32(bk).rearrange("p (g h) -> p g h", g=32)), [last])
                banks.rel(bk, o)
                evs.append(o)
            return evs, last

        HTb2 = HT[:].rearrange("p a b -> p (a b)")
        Uprev = HTb2[:, 0:8192].rearrange("p (g i h) -> p g i h", g=64, i=8)
        hpT = [HTb2[:, 8192:10240].rearrange("p (a b) -> p a b", a=16), HTb2[:, 10240:12288].rearrange("p (a b) -> p a b", a=16)]
        hpT_free = [None, None]
        xpv = x_prev.rearrange("(c j) d -> c j d", j=8)
        xov = x_own.rearrange("(c j) d -> c j d", j=8)
        u_evs = []
        tail_ops = []
        sq_ = {0: ln_a(xpv, 0, [o_T[-1]])}
        o_ns = {0: ln_b(0, sq_[0])}
        sq_[1] = ln_a(xpv, 1, [o_T[-1]])
        for j in range(8):
            b2 = j % 2
            evs = transposes_to_hT(nbf[b2], lambda kc, b2=b2: hpT[b2][:, kc, :], sc1p, modA[:, 0, :], [o_ns[j], hpT_free[b2], o_sc1p] + o_sh1, all_dve=True)
            nbf_free[b2] = k.last("pe")
            o_tl = k.op("act", lambda e, b2=b2, j=j: e.activation(out=tail[:, :, 2 * j:2 * j + 2], in_=hpT[b2][:, :, 126:128], func=AF.Copy), evs)
            tail_ops.append(o_tl)
            if j + 1 < 8:
                o_ns[j + 1] = ln_b(j + 1, sq_[j + 1])
            if j + 2 < 8:
                sq_[j + 2] = ln_a(xpv, j + 2, [o_T[-1]])
            um, lastpe = u_mm(lambda kc, b2=b2: hpT[b2][:, kc, :], j, evs)
            u_evs += u_ev(Uprev, j, um)
            hpT_free[b2] = [lastpe, o_tl]

        sofs = {"o": 0}

        def salloc(n, base):
            o = sofs["o"]
            sofs["o"] += (n + 63) // 64 * 64
            return base + o

        def s5_temps(base):
            sofs["o"] = 0
            t = {}
            t["UgT"] = carve(BIG, salloc(2048, base), 2048, BF16).rearrange("p (g c) -> p g c", g=8)
            for nm in ("Vre", "Vim", "t1", "t2"):
                t[nm] = carve(BIG, salloc(2048, base), 2048, F32).rearrange("p (g c) -> p g c", g=4)
            for nm in ("Wre", "Wim"):
                t[nm] = carve(BIG, salloc(2064, base), 2064, F32).rearrange("p (g c) -> p g c", g=4)
            for nm in ("Xre", "Xim"):
                t[nm] = carve(BIG, salloc(1024, base), 1024, BF16).rearrange("p (g c) -> p g c", g=4)
            for nm in ("ysb", "sg", "ztm"):
                t[nm] = salloc(2048, base)
            t["end"] = sofs["o"]
            return t

        def s5_states(U, tm, b, start_deps, own):
            gqs = slice(4 * b, 4 * b + 4)
            bk, fr_ = banks.get()
            last = None
            for gi in range(8):
                g = 8 * b + gi
                last = k.op("pe", lambda e, g=g, gi=gi, bk=bk: e.transpose(out=banks.bf(bk)[:, gi * 128:(gi + 1) * 128],
                                                                        in_=U[:, g].rearrange("p i h -> p (i h)"), identity=identb[:]),
                            [fr_] + list(start_deps) if gi == 0 else (), signal=(gi == 7))
            o_ug = k.op("act", lambda e, bk=bk: e.activation(out=tm["UgT"].rearrange("p g c -> p (g c)"), in_=banks.bf(bk), func=AF.Copy), [last] + list(start_deps))
            banks.rel(bk, o_ug)
            bre_, fr1 = banks.get()
            bim_, fr2 = banks.get()
            lastS = None
            for ri, bkk, frr in ((0, bre_, fr1), (1, bim_, fr2)):
                first = True
                for gi in range(4):
                    for g2 in range(2):
                        lastS = k.op("pe", lambda e, gi=gi, g2=g2, ri=ri, bkk=bkk: e.matmul(banks.f32(bkk)[64 * g2:64 * g2 + 64, gi * 128:(gi + 1) * 128],
                                                                                            lhsT=MinT[:, 4 * b + gi, ri, 64 * g2:64 * g2 + 64], rhs=tm["UgT"][:, 2 * gi + g2, :],
                                                                                            start=True, stop=True),
                                     [frr, o_ug, o_minT] if first else (), signal=(gi == 3 and g2 == 1))
                        first = False
            Sre = banks.f32(bre_).rearrange("p (g c) -> p g c", g=4)
            Sim = banks.f32(bim_).rearrange("p (g c) -> p g c", g=4)
            tcd = TC[:, gqs, 1:129]
            tsd = TS[:, gqs, 1:129]
            c_ = {"o": [lastS] + list(start_deps)}

            def dvc(fn, extra=()):
                o = k.op("dve", fn, c_["o"] + list(extra))
                c_["o"] = [o]
                return o

            dvc(lambda e: e.tensor_tensor(out=tm["t1"], in0=Sre, in1=tcd, op=ALU.mult))
            dvc(lambda e: e.tensor_tensor(out=tm["t2"], in0=Sim, in1=tsd, op=ALU.mult))
            dvc(lambda e: e.tensor_tensor(out=tm["Vre"], in0=tm["t1"], in1=tm["t2"], op=ALU.add))
            dvc(lambda e: e.tensor_tensor(out=tm["t1"], in0=Sim, in1=tcd, op=ALU.mult))
            o_s = dvc(lambda e: e.tensor_tensor(out=tm["t2"], in0=Sre, in1=tsd, op=ALU.mult))
            banks.rel(bre_, o_s)
            banks.rel(bim_, o_s)
            dvc(lambda e: e.tensor_tensor(out=tm["Vim"], in0=tm["t1"], in1=tm["t2"], op=ALU.subtract))
            if own:
                dvc(lambda e: e.tensor_scalar(out=tm["Wre"][:, :, 0], in0=xinit[:, 0, gqs], scalar1=flag[:, 0:1], scalar2=None, op0=ALU.mult), [d_fl])
                dvc(lambda e: e.tensor_scalar(out=tm["Wim"][:, :, 0], in0=xinit[:, 1, gqs], scalar1=flag[:, 0:1], scalar2=None, op0=ALU.mult))
            else:
                dvc(lambda e: e.memset(tm["Wre"][:, :, 0], 0.0))
                dvc(lambda e: e.memset(tm["Wim"][:, :, 0], 0.0))
            for gi in range(4):
                gq = 4 * b + gi
                for nm_w, nm_v in (("Wre", "Vre"), ("Wim", "Vim")):
                    dvc(lambda e, gi=gi, gq=gq, nm_w=nm_w, nm_v=nm_v: e.tensor_tensor_scan(out=tm[nm_w][:, gi, 1:129], data0=m8[:, gq:gq + 1].broadcast_to([128, 128]),
                                                                                        data1=tm[nm_v][:, gi, :], initial=tm[nm_w][:, gi, 0:1], op0=ALU.mult, op1=ALU.add))
            if not own:
                tce = TC[:, gqs, 128]
                tse = TS[:, gqs, 128]
                s1 = tm["t1"][:, :, 0]
                s2 = tm["t2"][:, :, 0]
                dvc(lambda e: e.tensor_tensor(out=s1, in0=tce, in1=tm["Wre"][:, :, 128], op=ALU.mult))
                dvc(lambda e: e.tensor_tensor(out=s2, in0=tse, in1=tm["Wim"][:, :, 128], op=ALU.mult))
                dvc(lambda e: e.tensor_tensor(out=xinit[:, 0, gqs], in0=s1, in1=s2, op=ALU.subtract))
                dvc(lambda e: e.tensor_tensor(out=s1, in0=tce, in1=tm["Wim"][:, :, 128], op=ALU.mult))
                dvc(lambda e: e.tensor_tensor(out=s2, in0=tse, in1=tm["Wre"][:, :, 128], op=ALU.mult))
                return dvc(lambda e: e.tensor_tensor(out=xinit[:, 1, gqs], in0=s1, in1=s2, op=ALU.add))
            tcr = TC[:, gqs, 0:128]
            tsr = TS[:, gqs, 0:128]
            Wr = tm["Wre"][:, :, 0:128]
            Wi = tm["Wim"][:, :, 0:128]
            dvc(lambda e: e.tensor_tensor(out=tm["t1"], in0=tcr, in1=Wr, op=ALU.mult))
            dvc(lambda e: e.tensor_tensor(out=tm["t2"], in0=tsr, in1=Wi, op=ALU.mult))
            dvc(lambda e: e.tensor_tensor(out=tm["Xre"], in0=tm["t1"], in1=tm["t2"], op=ALU.subtract))
            dvc(lambda e: e.tensor_tensor(out=tm["t1"], in0=tcr, in1=Wi, op=ALU.mult))
            dvc(lambda e: e.tensor_tensor(out=tm["t2"], in0=tsr, in1=Wr, op=ALU.mult))
            return dvc(lambda e: e.tensor_tensor(out=tm["Xim"], in0=tm["t1"], in1=tm["t2"], op=ALU.add))

        tmP = s5_temps(0)
        prev = list(u_evs)
        for b in range(8):
            prev = [s5_states(Uprev, tmP, b, prev, own=False)]
        o_xinit = prev[0]

        if debug == "xinit":
            dump(xinit[:].rearrange("p a b -> p (a b)"), 0, [o_xinit])
            dd = carve(BIG, 16 * KB, 2 * KB, F32)
            o1 = k.op("dve", lambda e: e.tensor_copy(out=dd.rearrange("p (a b) -> p a b", a=16)[:, :, 0:16], in_=tail[:]), tail_ops + [o_xinit])
            dump(dd[:, 0:256], 64, [o1])
            finish()
            return nc
        k.barrier()

        U = carve(BIG, 0, 16 * KB, BF16).rearrange("p (g i h) -> p g i h", g=64, i=8)
        u_evs = []
        lastpe = None
        sq_ = {0: ln_a(xov, 0, [])}
        o_ns = {0: ln_b(0, sq_[0])}
        sq_[1] = ln_a(xov, 1, [])
        for j in range(8):
            b2 = j % 2
            evs = transposes_to_hT(nbf[b2], lambda kc, j=j: HT[:, kc, j * 128:(j + 1) * 128], sc1p, modA[:, 0, :], [o_ns[j]], all_dve=True)
            nbf_free[b2] = k.last("pe")
            if j + 1 < 8:
                o_ns[j + 1] = ln_b(j + 1, sq_[j + 1])
            if j + 2 < 8:
                sq_[j + 2] = ln_a(xov, j + 2, [])
            um, lastpe = u_mm(lambda kc, j=j: HT[:, kc, j * 128:(j + 1) * 128], j, evs)
            u_evs += u_ev(U, j, um)
        for n2 in range(2):
            ws.release(wu_i[n2], lastpe)
        k.barrier()

        tm = s5_temps(16 * KB)
        assert 16 * KB + tm["end"] <= 40 * KB, tm["end"]
        zT = carve(BIG, 0, 16 * KB, BF16).rearrange("p (b n) -> p b n", b=8)
        ysb = carve(BIG, tm["ysb"], 2048, F32)
        sgb = carve(BIG, tm["sg"], 2048, F32)
        ztm = carve(BIG, tm["ztm"], 2048, BF16).rearrange("p (j q g2 h) -> p j q g2 h", j=8, q=4, g2=2)
        tmpC = tm["t1"].rearrange("p g c -> p (g c)")
        sqb = tm["t2"].rearrange("p g c -> p (g c)")
        prev = []
        zt_ops = []
        for b in range(8):
            o_x = s5_states(U, tm, b, prev, own=True)
            gl = []
            for g2 in range(2):
                pr = slice(64 * g2, 64 * g2 + 64)
                bA, frA = banks.get()
                lastA = None
                for gi in range(4):
                    k.op("pe", lambda e, gi=gi, pr=pr, bA=bA, b=b: e.matmul(banks.f32(bA)[:, gi * 128:(gi + 1) * 128], lhsT=tm["Xre"][pr, gi, :], rhs=Cre[pr, 4 * b + gi, :],
                                                                       start=True, stop=False), [frA, o_x] if gi == 0 else (), signal=False)
                    lastA = k.op("pe", lambda e, gi=gi, pr=pr, bA=bA, b=b: e.matmul(banks.f32(bA)[:, gi * 128:(gi + 1) * 128], lhsT=tm["Xim"][pr, gi, :], rhs=Cimn[pr, 4 * b + gi, :],
                                                                               start=False, stop=True), signal=(gi == 3))
                bC, frC = banks.get()
                lastC = None
                for gi in range(4):
                    lastC = k.op("pe", lambda e, gi=gi, g2=g2, bC=bC, b=b: e.matmul(banks.f32(bC)[:, gi * 128:(gi + 1) * 128], lhsT=tm["UgT"][:, 2 * gi + g2, :], rhs=Tsb[:, 8 * b + 2 * gi + g2, :],
                                                                               start=True, stop=True), [frC] + o_T if gi == 0 else (), signal=(gi == 3))
                o1 = k.op("act", lambda e, bC=bC: e.activation(out=tmpC, in_=banks.f32(bC), func=AF.Copy), [lastC, o_x] + gl)
                banks.rel(bC, o1)
                o2 = k.op("dve", lambda e, bA=bA: e.tensor_tensor(out=ysb, in0=banks.f32(bA), in1=tmpC, op=ALU.add), [lastA, o1] + gl)
                banks.rel(bA, o2)
                o3 = k.op("act", lambda e: e.activation(out=sqb, in_=ysb, func=AF.Square), [o2])
                o4 = k.op("dve", lambda e: e.tensor_scalar(out=sqb, in0=sqb, scalar1=0.044715, scalar2=1.0, op0=ALU.mult, op1=ALU.add), [o3])
                o5 = k.op("dve", lambda e: e.tensor_tensor(out=sqb, in0=sqb, in1=ysb, op=ALU.mult), [o4])
                o6 = k.op("act", lambda e: e.activation(out=sgb, in_=sqb, func=AF.Sigmoid, scale=GELU_C), [o5])
                o7 = k.op("dve", lambda e, g2=g2: e.tensor_tensor(out=ztm[:, :, :, g2, :].rearrange("p j q h -> p q j h"),
                                                                in0=sgb.rearrange("p (q j h) -> p q j h", q=4, j=8), in1=ysb.rearrange("p (q j h) -> p q j h", q=4, j=8),
                                                                op=ALU.mult), [o6] + zt_ops[-1:])
                gl = [o7]
            bk, fr_ = banks.get()
            last = None
            for j in range(8):
                last = k.op("pe", lambda e, j=j, bk=bk: e.transpose(out=banks.bf(bk)[:, j * 128:(j + 1) * 128], in_=ztm[:, j].rearrange("p q g h -> p (q g h)"), identity=identb[:]),
                            [fr_, o7] if j == 0 else (), signal=(j == 7))
            oz = k.op("act", lambda e, b=b, bk=bk: e.activation(out=zT[:, b, :], in_=banks.bf(bk), func=AF.Copy), [last])
            banks.rel(bk, oz)
            zt_ops.append(oz)
            prev = [o7, oz]

        if debug == "z":
            k.barrier()
            dd = carve(BIG, 40 * KB, 32 * KB, F32)
            o1 = k.op("dve", lambda e: e.tensor_copy(out=dd, in_=zT.rearrange("p b n -> p (b n)")))
            dump(dd, 0, [o1])
            finish()
            return nc
        k.barrier()


        upT = carve(BIG, 40 * KB, 33280, F32).rearrange("p (k j c) -> p k j c", k=8, j=8)
        pA = carve(BIG, 73 * KB, 4160, F32).rearrange("p (j c) -> p j c", j=8)
        pB = carve(BIG, 78 * KB, 4160, F32).rearrange("p (j c) -> p j c", j=8)
        rct = carve(BIG, 83 * KB, 4096, F32).rearrange("p (j c) -> p j c", j=8)
        pooledT = carve(RB, 0, 16 * KB, BF16).rearrange("p (k n) -> p k n", k=8)
        qsT = carve(BIG, 16 * KB, 16 * KB, BF16).rearrange("p (k n) -> p k n", k=8)
        o_ms = [k.op("dve", lambda e: e.memset(pA, 0.0)), k.op("dve", lambda e: e.memset(pB, 0.0))]
        up_ops = {}
        for mb in range(2):
            buf, dops, bi = ws.load([(lambda b_: b_[:], wsrc(w_in, 0, D, 1024 + mb * 512, 512))])
            last = None
            for m in range(4):
                k8 = mb * 4 + m
                ops = []
                for half in range(2):
                    bk, fr_ = banks.get()
                    for kc in range(16):
                        last = k.op("pe", lambda e, kc=kc, m=m, half=half, bk=bk, buf=buf: e.matmul(banks.f32(bk), lhsT=buf[:, kc, m * 128:(m + 1) * 128],
                                                                                                 rhs=HT[:, kc, half * 512:(half + 1) * 512], start=(kc == 0), stop=(kc == 15)),
                                    [fr_] + dops if kc == 0 else (), signal=(kc == 15))
                    o = k.op("act", lambda e, k8=k8, half=half, bk=bk: e.activation(out=upT[:, k8, 4 * half:4 * half + 4, 2:130], in_=banks.f32(bk).rearrange("p (j c) -> p j c", j=4),
                                                                                 func=AF.Copy), [last])
                    banks.rel(bk, o)
                    ops.append(o)
                bk, fr_ = banks.get()
                for kc in range(16):
                    last = k.op("pe", lambda e, kc=kc, m=m, bk=bk, buf=buf: e.matmul(banks.f32(bk)[:, 0:16], lhsT=buf[:, kc, m * 128:(m + 1) * 128], rhs=tail[:, kc, :],
                                                                                  start=(kc == 0), stop=(kc == 15)), [fr_] + tail_ops if kc == 0 else (), signal=(kc == 15))
                o = k.op("dve", lambda e, k8=k8, bk=bk: e.tensor_scalar(out=upT[:, k8, :, 0:2], in0=banks.f32(bk)[:, 0:16].rearrange("p (j c) -> p j c", j=8),
                                                                     scalar1=flag[:, 0:1], scalar2=None, op0=ALU.mult), [last, d_fl])
                banks.rel(bk, o)
                ops.append(o)
                up_ops[k8] = ops
            ws.release(bi, last)

        rc_free = [None]
        pl_ops = []
        tmp_free = {"A": o_ms[0:1], "B": o_ms[1:2]}
        for gi in range(4):
            d_rc = k.dma("sp", rct, rc_d[gi:gi + 1, :].rearrange("o (j c) -> o j c", j=8).broadcast_to([128, 8, 128]), deps=rc_free)
            gl = []
            for a in range(2):
                k8 = 2 * gi + a
                eng = "dve"
                u_ = upT[:, k8]
                cur = u_
                c_ = {"o": list(up_ops[k8]) + tmp_free["A"] + tmp_free["B"]}

                def pc(fn, extra=()):
                    o = k.op(eng, fn, c_["o"] + list(extra))
                    c_["o"] = [o]
                    return o

                names = ["A", "B"]
                tb = {"A": pA, "B": pB}
                lvl = 0
                for s_ in (1, 2, 4, 8)[:gi + 1]:
                    dst = tb[names[lvl % 2]]
                    if s_ < 8:
                        pc(lambda e, dst=dst, cur=cur, s_=s_: e.tensor_tensor(out=dst[:, s_:8, :], in0=cur[:, s_:8, :], in1=cur[:, 0:8 - s_, :], op=ALU.add))
                        pc(lambda e, dst=dst, cur=cur, s_=s_: e.tensor_tensor(out=dst[:, 0:s_, 1:130], in0=cur[:, 0:s_, 1:130], in1=cur[:, 8 - s_:8, 0:129], op=ALU.add))
                    else:
                        pc(lambda e, dst=dst, cur=cur: e.tensor_tensor(out=dst[:, :, 1:130], in0=cur[:, :, 1:130], in1=cur[:, :, 0:129], op=ALU.add))
                    cur = dst
                    lvl += 1
                oth = tb[names[lvl % 2]]
                pc(lambda e, oth=oth, cur=cur: e.tensor_tensor(out=oth[:, :, 2:130], in0=cur[:, :, 2:130], in1=rct, op=ALU.mult), [d_rc])
                o = pc(lambda e, oth=oth, u_=u_, k8=k8: e.tensor_tensor(out=pooledT[:, k8, :].rearrange("p (j c) -> p j c", j=8), in0=oth[:, :, 2:130], in1=u_[:, :, 2:130], op=ALU.subtract))
                tmp_free["A"] = [o]
                tmp_free["B"] = [o]
                gl.append(o)
                pl_ops.append(o)
            rc_free = gl
        buf, dops, bi = ws.load([(lambda b_: b_[:, 0:8, 0:256], w_pool.rearrange("(r p) n -> p r n", p=128))])
        last = None
        qs_ops = []
        for gi in range(4):
            for a in range(2):
                k8o = 2 * gi + a
                for half in range(2):
                    bk, fr_ = banks.get()
                    for bs in range(2):
                        last = k.op("pe", lambda e, gi=gi, a=a, bs=bs, half=half, bk=bk, buf=buf: e.matmul(banks.f32(bk), lhsT=buf[:, 2 * gi + bs, a * 128:(a + 1) * 128],
                                                                                                        rhs=pooledT[:, 2 * gi + bs, half * 512:(half + 1) * 512], start=(bs == 0), stop=(bs == 1)),
                                    [fr_] + dops + pl_ops[2 * gi:2 * gi + 2] if bs == 0 else (), signal=(bs == 1))
                    o = k.op("act", lambda e, k8o=k8o, half=half, bk=bk: e.activation(out=qsT[:, k8o, half * 512:(half + 1) * 512], in_=banks.f32(bk), func=AF.Identity,
                                                                                   scale=pscale[:, k8o:k8o + 1]), [last, d_ps])
                    banks.rel(bk, o)
                    qs_ops.append(o)
        ws.release(bi, last)
        k.barrier()

        merged = carve(RB, 0, 32 * KB, BF16).rearrange("p (m n) -> p m n", m=16)
        tmpA = carve(BIG, 32 * KB, 16 * KB, F32).rearrange("p (m n) -> p m n", m=4)
        tmpB = carve(BIG, 48 * KB, 16 * KB, F32).rearrange("p (m n) -> p m n", m=4)
        s1 = [carve(BIG, 72 * KB, 2 * KB, F32), carve(BIG, 74 * KB, 2 * KB, F32)]
        slot = [carve(BIG, (64 + 8 * i) * KB, 8 * KB, F32) for i in range(3)]
        g1_ops = []
        o_sh2 = []
        o_sc2 = []
        s1_free = [None, None]
        sidx = [0]
        AB_free = [None]

        def mm_group(bk, fr_, n_k, lhs_fn, rhs_fn, deps):
            last = None
            for kc in range(n_k):
                last = k.op("pe", lambda e, kc=kc: e.matmul(banks.f32(bk), lhsT=lhs_fn(kc), rhs=rhs_fn(kc), start=(kc == 0), stop=(kc == n_k - 1)),
                            [fr_] + list(deps) if kc == 0 else (), signal=(kc == n_k - 1))
            return last

        p5_specs = []
        for mb_ in range(4):
            c0_ = mb_ * 512
            p5_specs.append([(lambda b_: b_[:, 0:8, :], wsrc(w_val, 0, 1024, c0_, 512)), (lambda b_: b_[:, 8:16, :], wsrc(w_gate, 0, 1024, c0_, 512))])
            p5_specs.append([(lambda b_: b_[:], wsrc(w_in, 0, D, 2048 + c0_, 512))])
            p5_specs.append([(lambda b_: b_[:, 0:8, :], wsrc(w_pool_out, 0, 1024, c0_, 512))])
            p5_specs.append([(lambda b_: b_[:], wsrc(w_in, 0, D, 4096 + c0_, 512))])
        p5_loaded = {}

        def p5_load(idx):
            for i_ in (idx, idx + 1):
                if i_ < 16 and i_ not in p5_loaded:
                    p5_loaded[i_] = ws.load(p5_specs[i_])
            return p5_loaded[idx]

        adaW = carve(BIG, 76 * KB, 8 * KB, BF16).rearrange("p (a b) -> p a b", a=16)
        adaW_free = [None]

        def ada_half(n0, dst):
            d_w = k.dma("pool", adaW, wsrc(w_ada, 0, D, n0, 256), deps=[adaW_free[0]])
            d_b = k.dma("sp", bada[:, 0:256], b_ada[:, n0:n0 + 256].broadcast_to([128, 256]), deps=[state["bada_free"]])
            bk, fr = banks.get()
            last = None
            for kc in range(16):
                last = k.op("pe", lambda e, kc=kc, bk=bk: e.matmul(banks.f32(bk)[:, 0:256], lhsT=crep[:, kc, :], rhs=adaW[:, kc, :], start=(kc == 0), stop=(kc == 15)),
                            [fr, o_crep, d_w] if kc == 0 else (), signal=(kc == 15))
            adaW_free[0] = last
            o = k.op("dve", lambda e, bk=bk, dst=dst: e.tensor_tensor(out=dst, in0=banks.f32(bk)[:, 0:256], in1=bada[:, 0:256], op=ALU.add), [last, d_b, state["modbc_free"]])
            banks.rel(bk, o)
            state["bada_free"] = o
            return o

        def ada_half_a(vi, mrow, hb):
            o = ada_half(vi * D + hb * 256, modbc[:, 0:256])
            o2 = k.op("dve", lambda e: e.tensor_tensor(out=dtmp[:, 0:256].rearrange("p (a b) -> p a b", a=2), in0=modbc[:, 0:256].rearrange("p (a b) -> p a b", a=2),
                                                      in1=identf[:].unsqueeze(1).broadcast_to([128, 2, 128]), op=ALU.mult), [o])
            o3 = k.op("dve", lambda e: e.tensor_reduce(out=modA[:, mrow, 2 * hb:2 * hb + 2], in_=dtmp[:, 0:256].rearrange("p (a b) -> p a b", a=2), axis=AX.X, op=ALU.add), [o2])
            state["modbc_free"] = o3
            return o3

        def ada_p5(mb, step):
            if step == 0:
                for hb in (2 * mb, 2 * mb + 1):
                    g1_ops.append(ada_half(2 * D + hb * 256, slot[0][:, hb * 256:(hb + 1) * 256]))
            elif step == 1:
                for hb in (2 * mb, 2 * mb + 1):
                    o_sh2.append(ada_half_a(3, 2, hb))
            elif step == 2:
                o_sc2.append(ada_half_a(4, 3, 2 * mb))
            else:
                o_sc2.append(ada_half_a(4, 3, 2 * mb + 1))

        for mb in range(4):
            c0 = mb * 512
            buf, dops, bi = p5_load(4 * mb + 0)
            last = None
            stepA = {}
            for m in range(4):
                for half in range(2):
                    hs = slice(half * 512, (half + 1) * 512)
                    bv, fv = banks.get()
                    lv = mm_group(bv, fv, 8, lambda kc, m=m, buf=buf: buf[:, kc, m * 128:(m + 1) * 128], lambda kc, hs=hs: zT[:, kc, hs], dops + zt_ops)
                    bg, fg = banks.get()
                    last = mm_group(bg, fg, 8, lambda kc, m=m, buf=buf: buf[:, 8 + kc, m * 128:(m + 1) * 128], lambda kc, hs=hs: zT[:, kc, hs], ())
                    si = sidx[0] % 2; sidx[0] += 1
                    o1 = k.op("act", lambda e, bg=bg, si=si: e.activation(out=s1[si], in_=banks.f32(bg), func=AF.Sigmoid), [last, s1_free[si]])
                    banks.rel(bg, o1)
                    o2 = k.op("dve", lambda e, bv=bv, si=si, m=m, hs=hs: e.tensor_tensor(out=tmpA[:, m, hs], in0=banks.f32(bv), in1=s1[si], op=ALU.mult), [lv, o1, AB_free[0]])
                    banks.rel(bv, o2)
                    s1_free[si] = o2
                    stepA[(m, half)] = o2
            ws.release(bi, last)
            ada_p5(mb, 0)
            buf, dops, bi = p5_load(4 * mb + 1)
            for m in range(4):
                for half in range(2):
                    hs = slice(half * 512, (half + 1) * 512)
                    bg, fg = banks.get()
                    last = mm_group(bg, fg, 16, lambda kc, m=m, buf=buf: buf[:, kc, m * 128:(m + 1) * 128], lambda kc, hs=hs: HT[:, kc, hs], dops)
                    si = sidx[0] % 2; sidx[0] += 1
                    o1 = k.op("act", lambda e, bg=bg, si=si: e.activation(out=s1[si], in_=banks.f32(bg), func=AF.Sigmoid), [last, s1_free[si]])
                    banks.rel(bg, o1)
                    o2 = k.op("dve", lambda e, si=si, m=m, hs=hs: e.tensor_tensor(out=tmpA[:, m, hs], in0=tmpA[:, m, hs], in1=s1[si], op=ALU.mult), [o1, stepA[(m, half)]])
                    s1_free[si] = o2
                    stepA[(m, half)] = o2
            ws.release(bi, last)
            ada_p5(mb, 1)
            buf, dops, bi = p5_load(4 * mb + 2)
            stepB = {}
            for m in range(4):
                for half in range(2):
                    hs = slice(half * 512, (half + 1) * 512)
                    bg, fg = banks.get()
                    last = mm_group(bg, fg, 8, lambda kc, m=m, buf=buf: buf[:, kc, m * 128:(m + 1) * 128], lambda kc, hs=hs: qsT[:, kc, hs], dops + qs_ops)
                    o1 = k.op("act", lambda e, bg=bg, m=m, hs=hs: e.activation(out=tmpB[:, m, hs], in_=banks.f32(bg), func=AF.Copy), [last, AB_free[0]])
                    banks.rel(bg, o1)
                    stepB[(m, half)] = o1
            ws.release(bi, last)
            ada_p5(mb, 2)
            buf, dops, bi = p5_load(4 * mb + 3)
            fin = []
            for m in range(4):
                for half in range(2):
                    hs = slice(half * 512, (half + 1) * 512)
                    bg, fg = banks.get()
                    last = mm_group(bg, fg, 16, lambda kc, m=m, buf=buf: buf[:, kc, m * 128:(m + 1) * 128], lambda kc, hs=hs: HT[:, kc, hs], dops)
                    si = sidx[0] % 2; sidx[0] += 1
                    o1 = k.op("act", lambda e, bg=bg, si=si: e.activation(out=s1[si], in_=banks.f32(bg), func=AF.Sigmoid), [last, s1_free[si]])
                    banks.rel(bg, o1)
                    o2 = k.op("dve", lambda e, si=si, m=m, hs=hs: e.tensor_tensor(out=s1[si], in0=s1[si], in1=tmpB[:, m, hs], op=ALU.mult), [o1, stepB[(m, half)]])
                    o3 = k.op("dve", lambda e, si=si, m=m, hs=hs, mb=mb: e.tensor_tensor(out=merged[:, 4 * mb + m, hs], in0=s1[si], in1=tmpA[:, m, hs], op=ALU.add), [o2, stepA[(m, half)]])
                    s1_free[si] = o3
                    fin.append(o3)
            ws.release(bi, last)
            AB_free[0] = fin
            ada_p5(mb, 3)
        k.barrier()

        if debug == "merged":
            dd = carve(BIG, 0, 64 * KB, F32)
            o1 = k.op("dve", lambda e: e.tensor_copy(out=dd, in_=merged.rearrange("p m n -> p (m n)")))
            dump(dd, 0, [o1])
            finish()
            return nc

        xres = carve(BIG, 0, 64 * KB, F32).rearrange("p (j d) -> p j d", j=8)
        d_xr = [k.dma("sp", xres[:, j, :], xov[:, j, :]) for j in range(8)]
        HTf = HT[:].rearrange("p a b -> p (a b)")
        tq = [HTf[:, i * 1024:(i + 1) * 1024].bitcast(F32) for i in range(4)]
        tq_free = [None] * 4
        tix = [0]
        for nb in range(4):
            ns = slice(nb * 512, (nb + 1) * 512)
            buf, dops, bi = ws.load([(lambda b_: b_[:], wsrc(w_out, 0, D, nb * 512, 512))])
            last = None
            for j in range(8):
                js = slice(j * 128, (j + 1) * 128)
                bk, fr_ = banks.get()
                last = mm_group(bk, fr_, 16, lambda m, js=js: merged[:, m, js], lambda m, buf=buf: buf[:, m, :], dops)
                ti = tix[0] % 4; tix[0] += 1
                o1 = k.op("dve", lambda e, bk=bk, ti=ti, ns=ns: e.tensor_tensor(out=tq[ti], in0=banks.f32(bk), in1=slot[0][:, ns], op=ALU.mult), [last, tq_free[ti]] + g1_ops)
                banks.rel(bk, o1)
                o2 = k.op("dve", lambda e, ti=ti, j=j, ns=ns: e.scalar_tensor_tensor(out=xres[:, j, ns], in0=xres[:, j, ns], scalar=ALPHA, in1=tq[ti], op0=ALU.mult, op1=ALU.add), [o1, d_xr[j]])
                tq_free[ti] = o2
            ws.release(bi, last)
        k.barrier()

        oa = k.op("dve", lambda e: e.tensor_scalar(out=G2[:], in0=modA[:, 3, :], scalar1=1.0, scalar2=None, op0=ALU.add), o_sc2)
        ob = k.op("dve", lambda e: e.tensor_tensor(out=B2[:], in0=lnA[:, 1, :], in1=G2[:], op=ALU.mult), [oa, d_lnA1])
        ob = k.op("dve", lambda e: e.tensor_tensor(out=B2[:], in0=B2[:], in1=modA[:, 2, :], op=ALU.add), [ob] + o_sh2)
        oa = k.op("dve", lambda e: e.tensor_tensor(out=G2[:], in0=G2[:], in1=lnA[:, 0, :], op=ALU.mult), [ob, d_lnA0])
        d_l1 = k.dma("sp", slot[1], ln1_g.broadcast_to([128, D]))
        d_l2 = k.dma("sp", slot[2], ln1_b.broadcast_to([128, D]))
        o_l1 = k.op("dve", lambda e: e.tensor_scalar(out=slot[1], in0=slot[1], scalar1=ALPHA, scalar2=None, op0=ALU.mult), [d_l1])
        o_l2 = k.op("dve", lambda e: e.tensor_scalar(out=slot[2], in0=slot[2], scalar1=ALPHA, scalar2=None, op0=ALU.mult), [d_l2])
        nb2 = [carve(RB, 0, 4 * KB, BF16), carve(RB, 4 * KB, 4 * KB, BF16)]
        nb2_free = [None, None]
        g2_ops = []
        def ln7_part(j):
            b2 = j % 2
            o_st = ln_stats(j, lambda q, j=j: xres[:, j, q * 512:(q + 1) * 512], [])
            o_n = k.op("dve", lambda e, j=j: e.tensor_scalar(out=xres[:, j, :], in0=xres[:, j, :], scalar1=rstd[:, j:j + 1], scalar2=nmr[:, j:j + 1],
                                                          op0=ALU.mult, op1=ALU.add), [o_st])
            return k.op("act", lambda e, j=j, b2=b2: e.activation(out=nb2[b2], in_=xres[:, j, :], func=AF.Copy), [o_n, nb2_free[b2]])

        o_cs = {0: ln7_part(0)}
        for j in range(8):
            b2 = j % 2
            if j + 1 < 8:
                o_cs[j + 1] = ln7_part(j + 1)
            o_c = o_cs[j]
            evs = transposes_to_hT(nb2[b2], lambda kc, j=j: HT[:, kc, j * 128:(j + 1) * 128], G2, B2, [o_c, oa, ob])
            nb2_free[b2] = k.last("pe")
            if j % 2 == 1:
                blk = j // 2
                g2_ops.append(ada_block(5 * D + blk * 512, dst=slot[0][:, blk * 512:(blk + 1) * 512]))
        k.barrier()

        aT = carve(RB, 0, 32 * KB, BF16).rearrange("p (f n) -> p f n", f=16)
        tf = [slot[1][:, i * 512:(i + 1) * 512] for i in range(4)]
        tf_free = [None] * 4
        rl = [carve(RB, 32 * KB, 1024, BF16), carve(RB, 33 * KB, 1024, BF16)]
        rl_free = [None] * 2
        r2_ops = {}
        rix = [0]
        a_free = [None]
        for qd in range(4):
            a_ops = []
            for fb in range(4):
                buf, dops, bi = ws.load([(lambda b_: b_[:], wsrc(w_ff1, 0, D, qd * 2048 + fb * 512, 512))])
                last = None
                for f4 in range(4):
                    f = fb * 4 + f4
                    for half in range(2):
                        hs = slice(half * 512, (half + 1) * 512)
                        bk, fr_ = banks.get()
                        last = mm_group(bk, fr_, 16, lambda kc, f4=f4, buf=buf: buf[:, kc, f4 * 128:(f4 + 1) * 128], lambda kc, hs=hs: HT[:, kc, hs], dops)
                        ri = rix[0] % 2; rix[0] += 1
                        o1 = k.op("act", lambda e, bk=bk, ri=ri: e.activation(out=rl[ri], in_=banks.f32(bk), func=AF.Relu), [last, rl_free[ri]])
                        banks.rel(bk, o1)
                        o2 = k.op("dve", lambda e, ri=ri, f=f, hs=hs: e.tensor_tensor(out=aT[:, f, hs], in0=rl[ri], in1=rl[ri], op=ALU.mult), [o1, a_free[0]])
                        rl_free[ri] = o2
                        a_ops.append(o2)
                ws.release(bi, last)
                if qd == 0:
                    for j in (2 * fb, 2 * fb + 1):
                        o_r = k.op("dve", lambda e, j=j: e.tensor_tensor(out=xres[:, j, :], in0=xres[:, j, :], in1=slot[1], op=ALU.mult), [o_l1])
                        r2_ops[j] = k.op("dve", lambda e, j=j: e.tensor_tensor(out=xres[:, j, :], in0=xres[:, j, :], in1=slot[2], op=ALU.add), [o_r, o_l2])
                    if fb == 3:
                        tf_free = [list(r2_ops.values())] * 4
            lastq = []
            for nb in range(4):
                ns = slice(nb * 512, (nb + 1) * 512)
                buf, dops, bi = ws.load([(lambda b_: b_[:], wsrc(w_ff2, qd * 2048, 2048, nb * 512, 512))])
                last = None
                for j in range(8):
                    js = slice(j * 128, (j + 1) * 128)
                    bk, fr_ = banks.get()
                    last = mm_group(bk, fr_, 16, lambda f, js=js: aT[:, f, js], lambda f, buf=buf: buf[:, f, :], dops + a_ops)
                    ti = tix[0] % 4; tix[0] += 1
                    o1 = k.op("dve", lambda e, bk=bk, ti=ti, ns=ns: e.tensor_tensor(out=tf[ti], in0=banks.f32(bk), in1=slot[0][:, ns], op=ALU.mult), [last, tf_free[ti]] + g2_ops)
                    banks.rel(bk, o1)
                    o2 = k.op("dve", lambda e, ti=ti, j=j, ns=ns: e.tensor_tensor(out=xres[:, j, ns], in0=xres[:, j, ns], in1=tf[ti], op=ALU.add), [o1, r2_ops[j]])
                    tf_free[ti] = o2
                lastq.append(last)
                ws.release(bi, last)
            a_free[0] = lastq
        k.barrier()

        d_l1 = k.dma("sp", slot[1], ln2_g.broadcast_to([128, D]))
        d_l2 = k.dma("sp", slot[2], ln2_b.broadcast_to([128, D]))
        outv = out_d.rearrange("(c j) d -> c j d", j=8)
        outs = []
        for j in range(8):
            o_st = ln_stats(j, lambda q, j=j: xres[:, j, q * 512:(q + 1) * 512], [])
            o_n = k.op("act", lambda e, j=j: e.activation(out=xres[:, j, :], in_=xres[:, j, :], func=AF.Identity, scale=rstd[:, j:j + 1], bias=nmr[:, j:j + 1]), [o_st])
            o_g = k.op("pool", lambda e, j=j: e.tensor_tensor(out=xres[:, j, :], in0=xres[:, j, :], in1=slot[1], op=ALU.mult), [o_n, d_l1])
            o_b = k.op("dve", lambda e, j=j: e.tensor_tensor(out=xres[:, j, :], in0=xres[:, j, :], in1=slot[2], op=ALU.add), [o_g, d_l2])
            outs.append(k.dma("sp", outv[:, j, :], xres[:, j, :], deps=[o_b]))
        k.wait_all("sp", outs)
        dbg_ops.clear()
        finish()
        return nc
    return nc


def _consts(half):
    identf = np.eye(128, dtype=np.float32)
    ih = np.arange(128) // 16
    maskT = (ih[None, :] >= ih[:, None]).astype(np.float32)
    kq = np.arange(1, 9, dtype=np.float32)
    kr = np.arange(7, -1, -1).astype(np.float32)
    kp = -np.arange(1, 9, dtype=np.float32)
    kvec = np.tile(np.concatenate([kq, kr, kp])[None, :], (128, 1)).astype(np.float32)
    iota = np.tile(np.arange(129, dtype=np.float32)[None, :], (128, 1))
    jj, cc = np.meshgrid(np.arange(8), np.arange(128), indexing="ij")
    pos = (half * T + 8 * cc + jj).reshape(-1).astype(np.float32)
    rc = np.stack([1.0 / np.minimum(pos + 1.0, float(w)) for w in (2, 4, 8, 16)]).astype(np.float32)
    return dict(identf=identf, maskT=maskT, kvec=kvec, iota=iota, rc=rc)


def make_in_maps(inputs, cores=range(8)):
    f = lambda a: np.ascontiguousarray(np.asarray(a, dtype=np.float32))
    x = f(inputs["x"])
    shared = dict(
        w_ada=f(inputs["w_ada"][0]), b_ada=f(inputs["b_ada"][0]).reshape(1, -1), w_in=f(inputs["w_in"][0]),
        lam_re=f(inputs["lam_re"][0]), lam_im=f(inputs["lam_im"][0]), log_dt=f(inputs["log_dt"][0]),
        ssm_b_re=f(inputs["ssm_b_re"][0]), ssm_b_im=f(inputs["ssm_b_im"][0]),
        ssm_c_re=f(inputs["ssm_c_re"][0]), ssm_c_im=f(inputs["ssm_c_im"][0]), ssm_d=f(inputs["ssm_d"][0]),
        w_glu_val=f(inputs["w_glu_val"][0]), w_glu_gate=f(inputs["w_glu_gate"][0]),
        w_pool=f(inputs["w_pool"][0]).reshape(1024, 256), pool_scale=f(inputs["pool_scale"][0]),
        w_pool_out=f(inputs["w_pool_out"][0]), w_out=f(inputs["w_out"][0]),
        ln1_g=f(inputs["ln1_g"][0]).reshape(1, -1), ln1_b=f(inputs["ln1_b"][0]).reshape(1, -1),
        w_ff1=f(inputs["w_ff1"][0]), w_ff2=f(inputs["w_ff2"][0]),
        ln2_g=f(inputs["ln2_g"][0]).reshape(1, -1), ln2_b=f(inputs["ln2_b"][0]).reshape(1, -1),
    )
    consts = [_consts(0), _consts(1)]
    zeros = np.zeros((T, D), np.float32)
    maps = []
    for core in cores:
        b, half = core // 2, core % 2
        m = dict(shared)
        m.update(consts[half])
        m["x_own"] = np.ascontiguousarray(x[b, half * T:(half + 1) * T])
        m["x_prev"] = np.ascontiguousarray(x[b, 0:T]) if half == 1 else zeros
        m["c_row"] = f(inputs["c"][b])
        m["flag"] = np.full((128, 1), float(half), np.float32)
        maps.append(m)
    return maps


def kernel(**inputs):
    nc = build()
    maps = make_in_maps(inputs)
    res = run_bass_kernel_spmd(nc, maps, core_ids=list(range(8)))
    out = np.empty((4, 2 * T, D), np.float32)
    for core in range(8):
        b, half = core // 2, core % 2
        out[b, half * T:(half + 1) * T] = np.asarray(res.results[core]["out"], dtype=np.float32)
    return out
```

```python
import numpy as np
from contextlib import ExitStack
import concourse.bass as bass
import concourse.mybir as mybir
from concourse.bass_utils import run_bass_kernel_spmd

F32 = mybir.dt.float32
BF16 = mybir.dt.bfloat16
I32 = mybir.dt.int32
ALU = mybir.AluOpType
AF = mybir.ActivationFunctionType
AX = mybir.AxisListType

D = 2048
T = 1024
NCH = 128
L = 8
DFF = 8192
ALPHA = float(2.0 ** 0.25)
EPS = 1e-5
TWO_PI = float(2.0 * np.pi)
SIN_SCALE = TWO_PI * 0.9999995
GELU_C = 1.5957691216057308

ENGS = ("pe", "act", "dve", "pool", "sp")
CENGS = ("pe", "act", "dve", "pool")


class Op:
    __slots__ = ("kind", "eng", "idx", "sem", "val")

    def __init__(self, kind, eng=None, idx=None, sem=None, val=None):
        self.kind = kind
        self.eng = eng
        self.idx = idx
        self.sem = sem
        self.val = val


class K:
    def __init__(self, nc, es, n_dma_sems=28):
        self.nc = nc
        self.es = es
        self.prog = {e: [] for e in ENGS}
        self.csem = {e: es.enter_context(nc.semaphore("cs_" + e)) for e in CENGS}
        self.cnt = {e: 0 for e in ENGS}
        self.waited = {e: {p: 0 for p in ENGS} for e in ENGS}
        self.dsems = [es.enter_context(nc.semaphore("ds%d" % i)) for i in range(n_dma_sems)]
        self.dval = [0] * n_dma_sems
        self.dlast = [None] * n_dma_sems
        self.dnext = 0
        self.dnext_sw = 0
        self.n_hw = n_dma_sems - 6
        self.dwaited = {e: [0] * n_dma_sems for e in ENGS}
        self.nsb = 0

    def sb(self, shape, dtype, name=None):
        self.nsb += 1
        return self.es.enter_context(self.nc.sbuf_tensor("s_" + (name or ("sb%d" % self.nsb)), list(shape), dtype))

    def ps(self, shape, dtype, name=None):
        self.nsb += 1
        return self.es.enter_context(self.nc.psum_tensor(name or ("ps%d" % self.nsb), list(shape), dtype))

    def _waits(self, e, deps):
        out = []
        for d in deps:
            if d is None:
                continue
            if isinstance(d, (list, tuple)):
                out += self._waits(e, d)
                continue
            if d.kind == "c":
                if self.waited[e][d.eng] < d.idx:
                    self.waited[e][d.eng] = d.idx
                    out.append((self.csem[d.eng], d.idx))
            else:
                if self.dwaited[e][d.sem] < d.val:
                    self.dwaited[e][d.sem] = d.val
                    out.append((self.dsems[d.sem], d.val))
        return out

    def op(self, e, fn, deps=(), signal=True):
        ws = self._waits(e, deps)
        o = None
        sem = None
        if signal:
            self.cnt[e] += 1
            o = Op("c", eng=e, idx=self.cnt[e])
            sem = self.csem[e]

        def run(engine, ws=ws, fn=fn, sem=sem):
            for (s, v) in ws:
                engine.wait_ge(s, v)
            ins = fn(engine)
            if sem is not None:
                ins.then_inc(sem, 1)

        self.prog[e].append(run)
        return o

    def dma(self, q, out, in_, deps=(), **kw):
        if q == "pool":
            slot = self.n_hw + self.dnext_sw
            self.dnext_sw = (self.dnext_sw + 1) % (len(self.dsems) - self.n_hw)
        else:
            slot = self.dnext
            self.dnext = (self.dnext + 1) % self.n_hw
        deps = list(deps)
        if self.dlast[slot] is not None:
            deps.append(self.dlast[slot])
        ws = self._waits(q, deps)
        self.dval[slot] += 16
        o = Op("d", sem=slot, val=self.dval[slot])
        self.dlast[slot] = o
        s = self.dsems[slot]

        def run(engine, ws=ws, s=s, out=out, in_=in_, kw=kw):
            for (ss, v) in ws:
                engine.wait_ge(ss, v)
            engine.dma_start(out=out, in_=in_, **kw).then_inc(s, 16)

        self.prog[q].append(run)
        return o

    def wait_all(self, e, deps):
        ws = self._waits(e, deps)

        def run(engine, ws=ws):
            for (s, v) in ws:
                engine.wait_ge(s, v)

        self.prog[e].append(run)

    def last(self, e):
        return Op("c", eng=e, idx=self.cnt[e]) if self.cnt[e] > 0 else None

    def barrier(self):
        deps = [self.last(e) for e in CENGS] + [o for o in self.dlast if o is not None]
        for e in ENGS:
            self.wait_all(e, deps)

    def build(self):
        nc = self.nc
        with nc.Block() as block:
            @block.tensor
            def _(eng):
                for f in self.prog["pe"]:
                    f(eng)

            @block.scalar
            def _(eng):
                for f in self.prog["act"]:
                    f(eng)

            @block.vector
            def _(eng):
                for f in self.prog["dve"]:
                    f(eng)

            @block.gpsimd
            def _(eng):
                for f in self.prog["pool"]:
                    f(eng)

            @block.sync
            def _(eng):
                for f in self.prog["sp"]:
                    f(eng)


class Banks:
    def __init__(self, k):
        self.k = k
        self.t = [k.ps([128, 512], F32, name="bank%d" % i) for i in range(8)]
        self.free_op = [None] * 8
        self.nxt = 0

    def get(self):
        i = self.nxt
        self.nxt = (self.nxt + 1) % 8
        return i, self.free_op[i]

    def rel(self, i, op):
        self.free_op[i] = op

    def f32(self, i):
        return self.t[i][:]

    def bf(self, i):
        return self.t[i].bitcast(BF16)[:]


class WStream:
    def __init__(self, k, nbuf):
        self.k = k
        self.bufs = [k.sb([128, 16, 512], BF16, name="wb%d" % i) for i in range(nbuf)]
        self.rel_op = [None] * nbuf
        self.nxt = 0

    def load(self, parts):
        i = self.nxt
        self.nxt = (self.nxt + 1) % len(self.bufs)
        buf = self.bufs[i]
        ops = []
        for (dst_fn, src) in parts:
            ops.append(self.k.dma("pool", dst_fn(buf), src, deps=[self.rel_op[i]]))
        return buf, ops, i

    def release(self, i, op):
        self.rel_op[i] = op


def wsrc(w, r0, nrows, c0, ncols):
    return w[r0:r0 + nrows, c0:c0 + ncols].rearrange("(kc p) n -> p kc n", p=128)


def build(debug=None):
    nc = bass.Bass("TRN2", target_bir_lowering=False)

    def din(name, shape, dt=F32):
        return nc.dram_tensor(name, list(shape), dt, kind="ExternalInput").ap()

    x_own = din("x_own", [T, D])
    x_prev = din("x_prev", [T, D])
    c_row = din("c_row", [D])
    flag_d = din("flag", [128, 1])
    w_ada = din("w_ada", [D, 6 * D])
    b_ada = din("b_ada", [1, 6 * D])
    w_in = din("w_in", [D, 6144])
    lam_re = din("lam_re", [64, 64])
    lam_im = din("lam_im", [64, 64])
    log_dt = din("log_dt", [64])
    sb_re = din("ssm_b_re", [64, 64, 16])
    sb_im = din("ssm_b_im", [64, 64, 16])
    sc_re = din("ssm_c_re", [64, 16, 64])
    sc_im = din("ssm_c_im", [64, 16, 64])
    ssm_d = din("ssm_d", [1024])
    w_val = din("w_glu_val", [1024, D])
    w_gate = din("w_glu_gate", [1024, D])
    w_pool = din("w_pool", [1024, 256])
    pool_scale = din("pool_scale", [1024])
    w_pool_out = din("w_pool_out", [1024, D])
    w_out = din("w_out", [D, D])
    ln1_g = din("ln1_g", [1, D])
    ln1_b = din("ln1_b", [1, D])
    w_ff1 = din("w_ff1", [D, DFF])
    w_ff2 = din("w_ff2", [DFF, D])
    ln2_g = din("ln2_g", [1, D])
    ln2_b = din("ln2_b", [1, D])
    identf_d = din("identf", [128, 128])
    maskT_d = din("maskT", [128, 128])
    kvec_d = din("kvec", [128, 24])
    iota_d = din("iota", [128, 129])
    rc_d = din("rc", [4, T])
    out_d = nc.dram_tensor("out", [T, D], F32, kind="ExternalOutput").ap()
    dbg_d = None
    if debug is not None:
        dbg_d = nc.dram_tensor("dbg", [128, 16384], F32, kind="ExternalOutput").ap()

    es = ExitStack()
    with es:
        k = K(nc, es)
        banks = Banks(k)
        ws = WStream(k, 2)

        HT = k.sb([128, 16, 1024], BF16, name="HT")
        BIG = k.sb([128, 45056], BF16, name="BIG")
        RB = k.sb([128, 17408], BF16, name="RB")
        identf = k.sb([128, 128], F32, name="identf")
        identb = k.sb([128, 128], BF16, name="identb")
        maskT = k.sb([128, 128], F32, name="maskT")
        kvec = k.sb([128, 24], F32, name="kvec")
        iota = k.sb([128, 129], F32, name="iota")
        flag = k.sb([128, 1], F32, name="flag")
        crep = k.sb([128, 16, 128], BF16, name="crep")
        modA = k.sb([128, 4, 16], F32, name="modA")
        sc1p = k.sb([128, 16], F32, name="sc1p")
        G2 = k.sb([128, 16], F32, name="G2")
        B2 = k.sb([128, 16], F32, name="B2")
        lnA = k.sb([128, 2, 16], F32, name="lnA")
        pscale = k.sb([128, 8], F32, name="pscale")
        th8n = k.sb([128, 32], F32, name="th8n")
        m8 = k.sb([128, 32], F32, name="m8")
        tail = k.sb([128, 16, 16], BF16, name="tail")
        xinit = k.sb([128, 2, 32], F32, name="xinit")
        stat = k.sb([128, 4, 6], F32, name="stat")
        mv = k.sb([128, 8, 2], F32, name="mv")
        rstd = k.sb([128, 8], F32, name="rstd")
        nmr = k.sb([128, 8], F32, name="nmr")
        sdv = k.sb([128, 8], F32, name="sdv")

        def carve(reg, off, n, dt):
            v = reg[:, off // 2:(off + n) // 2]
            return v.bitcast(dt) if dt != BF16 else v

        KB = 1024
        MinT = carve(BIG, 40 * KB, 16 * KB, BF16).rearrange("p (g r m) -> p g r m", g=32, r=2)
        Tsb = carve(BIG, 56 * KB, 16 * KB, BF16).rearrange("p (g n) -> p g n", g=64)
        Cre = carve(BIG, 72 * KB, 8 * KB, BF16).rearrange("p (g n) -> p g n", g=32)
        Cimn = carve(BIG, 80 * KB, 8 * KB, BF16).rearrange("p (g n) -> p g n", g=32)
        TC = carve(RB, 0, 16512, F32).rearrange("p (g c) -> p g c", g=32)
        TS = carve(RB, 16512, 16512, F32).rearrange("p (g c) -> p g c", g=32)

        dbg_ops = []

        def dump(ap2d, col0, deps):
            n = ap2d.shape[1]
            dbg_ops.append(k.dma("sp", dbg_d[:, col0:col0 + n], ap2d, deps=deps))

        def finish():
            k.barrier()
            k.wait_all("sp", dbg_ops)
            k.build()

        kd = lambda *a_, **kw_: k.dma(*a_, allow_slow_non_contiguous=True, **kw_)
        d_id = kd("sp", identf[:], identf_d)
        d_mk = kd("sp", maskT[:], maskT_d)
        d_kv = kd("sp", kvec[:], kvec_d)
        d_io = kd("sp", iota[:], iota_d)
        d_fl = kd("sp", flag[:], flag_d)
        cT = carve(BIG, 0, 64, F32)
        lr = carve(BIG, 128, 128, F32)
        li = carve(BIG, 256, 128, F32)
        ldt = carve(BIG, 384, 128, F32)
        lll = carve(BIG, 128, 384, F32)
        bre = carve(BIG, 1 * KB, 2 * KB, F32).rearrange("p (g h) -> p g h", g=32)
        bim = carve(BIG, 3 * KB, 2 * KB, F32).rearrange("p (g h) -> p g h", g=32)
        dvec = carve(BIG, 9 * KB, 256, F32)
        Cq_re = carve(BIG, 18 * KB, 2 * KB, F32).rearrange("p (g h) -> p g h", g=32)
        Cq_im = carve(BIG, 20 * KB, 2 * KB, F32).rearrange("p (g h) -> p g h", g=32)
        bA = [carve(RB, 0, 8 * KB, F32), carve(RB, 8 * KB, 8 * KB, F32)]
        cA = [carve(RB, 16 * KB, 8 * KB, F32), carve(RB, 24 * KB, 8 * KB, F32)]
        lamA = [carve(RB, 32 * KB + 512 * i, 512, F32) for i in range(3)]
        ld2 = carve(RB, 32 * KB + 1536, 8, F32)
        vst = [carve(BIG, 35 * KB + 512 * i, 512, F32) for i in range(4)]
        dsm = carve(BIG, 37 * KB, 64, F32)
        dst_ = carve(BIG, 37 * KB + 64, 512, F32)
        d_bA = [kd("sp", bA[0][0:32, :], sb_re.rearrange("(gq g2) p h -> gq (g2 p h)", g2=2)),
                kd("sp", bA[1][0:32, :], sb_im.rearrange("(gq g2) p h -> gq (g2 p h)", g2=2))]
        d_cA = [kd("sp", cA[0][0:32, :], sc_re.rearrange("(gq g2) o p -> gq (g2 o p)", g2=2)),
                kd("sp", cA[1][0:32, :], sc_im.rearrange("(gq g2) o p -> gq (g2 o p)", g2=2))]
        d_lam = [kd("sp", lamA[0][0:32, :], lam_re.rearrange("(gq g2) p -> gq (g2 p)", g2=2)),
                 kd("sp", lamA[1][0:32, :], lam_im.rearrange("(gq g2) p -> gq (g2 p)", g2=2)),
                 kd("sp", ld2[0:32, :], log_dt.rearrange("(gq g2) -> gq g2", g2=2))]
        d_v = [kd("sp", vst[0][0:16, :], c_row.rearrange("(kc p) -> kc p", p=128)),
               kd("sp", vst[1][0:16, :], ln1_g[0].rearrange("(kc p) -> kc p", p=128)),
               kd("sp", vst[2][0:16, :], ln1_b[0].rearrange("(kc p) -> kc p", p=128)),
               kd("sp", vst[3][0:8, :], pool_scale.rearrange("(kc p) -> kc p", p=128))]
        d_ds = kd("sp", dsm[0:64, :], ssm_d.rearrange("(g h) -> g h", h=16))
        o_ldA = k.op("dve", lambda e: e.tensor_copy(out=lamA[2][0:32, :].rearrange("q (a p) -> q a p", a=2), in_=ld2[0:32, :].unsqueeze(2).broadcast_to([32, 2, 64])), [d_lam[2]])
        o_dst = k.op("dve", lambda e: e.tensor_copy(out=dst_[0:64, :].rearrange("q (i h) -> q i h", i=8), in_=dsm[0:64, :].unsqueeze(1).broadcast_to([64, 8, 16])), [d_ds])

        def tr32(bk, col0, ncol, in_ap, npart, deps, sig):
            return k.op("pe", lambda e: e.transpose(out=banks.f32(bk)[:, col0:col0 + ncol], in_=in_ap, identity=identf[0:npart, 0:npart]), deps, signal=sig)

        bk, fr_ = banks.get()
        tr32(bk, 0, 32, lamA[0][0:32, :], 32, [fr_, d_id, d_lam[0]], False)
        tr32(bk, 32, 32, lamA[1][0:32, :], 32, [d_lam[1]], False)
        last = tr32(bk, 64, 32, lamA[2][0:32, :], 32, [o_ldA], True)
        d_lr = k.op("dve", lambda e, bk=bk: e.tensor_copy(out=lll, in_=banks.f32(bk)[:, 0:96]), [last])
        banks.rel(bk, d_lr)
        d_li = d_ld0 = d_ld1 = d_lr
        bk, fr_ = banks.get()
        tr32(bk, 0, 16, vst[0][0:16, :], 16, [fr_, d_v[0]], False)
        tr32(bk, 16, 16, vst[1][0:16, :], 16, [d_v[1]], False)
        tr32(bk, 32, 16, vst[2][0:16, :], 16, [d_v[2]], False)
        tr32(bk, 48, 8, vst[3][0:8, :], 8, [d_v[3]], False)
        last = tr32(bk, 64, 64, dst_[0:64, :], 64, [o_dst], True)
        d_c = k.op("dve", lambda e, bk=bk: e.tensor_copy(out=cT, in_=banks.f32(bk)[:, 0:16]), [last])
        d_lnA0 = k.op("dve", lambda e, bk=bk: e.tensor_copy(out=lnA[:].rearrange("p a b -> p (a b)"), in_=banks.f32(bk)[:, 16:48]), [last])
        d_lnA1 = d_lnA0
        d_ps = k.op("dve", lambda e, bk=bk: e.tensor_copy(out=pscale[:], in_=banks.f32(bk)[:, 48:56]), [last])
        o_dv = k.op("dve", lambda e, bk=bk: e.tensor_copy(out=dvec, in_=banks.f32(bk)[:, 64:128]), [last])
        banks.rel(bk, o_dv)
        d_dv = [o_dv]
        d_bc = []
        HT32 = HT[:].rearrange("p a b -> p (a b)").bitcast(F32)
        for si_, (src, dstv, dd_, is_c) in enumerate( ((bA[0], bre, d_bA[0], False), (bA[1], bim, d_bA[1], False), (cA[0], Cq_re, d_cA[0], True), (cA[1], Cq_im, d_cA[1], True))):
            st2 = HT32[:, si_ * 2048:(si_ + 1) * 2048]
            if is_c:
                pin = src[0:32, :].rearrange("q (g o p) -> q o g p", g=2, o=16)
            else:
                pin = src[0:32, :].rearrange("q (g p h) -> q h g p", g=2, p=64)
            o_pm = k.op("dve", lambda e, st2=st2, pin=pin: e.tensor_copy(out=st2[0:32, :].rearrange("q (h g p) -> q h g p", h=16, g=2), in_=pin), [dd_])
            bk, fr_ = banks.get()
            last = None
            for h in range(16):
                inp = st2[0:32, h * 128:(h + 1) * 128]
                last = tr32(bk, h * 32, 32, inp, 32, [fr_, o_pm] if h == 0 else [], h == 15)
            o = k.op("dve", lambda e, bk=bk, dstv=dstv: e.tensor_copy(out=dstv, in_=banks.f32(bk).rearrange("p (h g) -> p g h", h=16)), [last])
            banks.rel(bk, o)
            d_bc.append(o)
        d_br, d_bi = d_bc[0], d_bc[1]
        d_cst = [d_bc[2], d_bc[3]]

        o_idb = k.op("dve", lambda e: e.tensor_copy(out=identb[:], in_=identf[:]), [d_id])

        cact = carve(BIG, 9 * KB + 256, 64, F32)
        o_silu = k.op("act", lambda e: e.activation(out=cact, in_=cT, func=AF.Silu), [d_c])
        o_crep = k.op("dve", lambda e: e.tensor_copy(out=crep[:], in_=cact.unsqueeze(2).broadcast_to([128, 16, 128])), [o_silu])

        ADA = k.sb([128, 3, 512], F32, name="ADA")
        bada = ADA[:, 0, :]
        modbc = ADA[:, 1, :]
        dtmp = ADA[:, 2, :]
        state = {"bada_free": None, "modbc_free": None}

        def ada_block(n0, dst=None):
            dst = modbc if dst is None else dst
            buf, dops, bi = ws.load([(lambda b: b[:], wsrc(w_ada, 0, D, n0, 512))])
            d_b = k.dma("sp", bada, b_ada[:, n0:n0 + 512].broadcast_to([128, 512]), deps=[state["bada_free"]])
            bk, fr = banks.get()
            last = None
            for kc in range(16):
                last = k.op("pe", lambda e, kc=kc, bk=bk, buf=buf: e.matmul(banks.f32(bk), lhsT=crep[:, kc, :], rhs=buf[:, kc, :],
                                                                            start=(kc == 0), stop=(kc == 15)),
                            [fr, o_crep] + dops if kc == 0 else (), signal=(kc == 15))
            ws.release(bi, last)
            o = k.op("dve", lambda e, bk=bk, dst=dst: e.tensor_tensor(out=dst, in0=banks.f32(bk), in1=bada, op=ALU.add),
                     [last, d_b, state["modbc_free"]])
            banks.rel(bk, o)
            state["bada_free"] = o
            return o

        def ada_vec_a(vi, dst_fn, blks=range(4)):
            outs = []
            for blk in blks:
                o = ada_block(vi * D + blk * 512)
                o2 = k.op("dve", lambda e: e.tensor_tensor(out=dtmp.rearrange("p (a b) -> p a b", a=4),
                                                          in0=modbc.rearrange("p (a b) -> p a b", a=4),
                                                          in1=identf[:].unsqueeze(1).broadcast_to([128, 4, 128]), op=ALU.mult), [o, d_id])
                o3 = k.op("dve", lambda e, blk=blk: e.tensor_reduce(out=dst_fn(blk), in_=dtmp.rearrange("p (a b) -> p a b", a=4),
                                                                    axis=AX.X, op=ALU.add), [o2])
                state["modbc_free"] = o3
                outs.append(o3)
            return outs

        o_sh1 = []
        o_sc1 = []
        pending = []
        for blk_ in range(4):
            pending.append(lambda blk_=blk_: o_sh1.extend(ada_vec_a(0, lambda blk: modA[:, 0, 4 * blk:4 * blk + 4], blks=(blk_,))))
        for blk_ in range(4):
            pending.append(lambda blk_=blk_: o_sc1.extend(ada_vec_a(1, lambda blk: modA[:, 1, 4 * blk:4 * blk + 4], blks=(blk_,))))
        tickc = [0]

        def tick():
            tickc[0] += 1
            if tickc[0] % 10 == 0 and pending:
                pending.pop(0)()

        pending.pop(0)()
        pending.pop(0)()


        SM = lambda i: carve(BIG, 16 * KB + 128 * i, 128, F32)
        dt_t, a_t, th_t, mag1, cos1, sin1, abm1, abim, den, rden, fre, fim, tA, tB, tC, thn = [SM(i) for i in range(16)]
        Cq_re = carve(BIG, 18 * KB, 2 * KB, F32).rearrange("p (g h) -> p g h", g=32)
        Cq_im = carve(BIG, 20 * KB, 2 * KB, F32).rearrange("p (g h) -> p g h", g=32)
        bb_re = carve(BIG, 22 * KB, 2 * KB, F32).rearrange("p (g h) -> p g h", g=32)
        bb_im = carve(BIG, 24 * KB, 2 * KB, F32).rearrange("p (g h) -> p g h", g=32)
        LT = lambda i: carve(BIG, 26 * KB + 1024 * i, 1024, F32).rearrange("p (g k) -> p g k", g=32)
        LTi = carve(BIG, 26 * KB + 1024 * 7, 1024, I32).rearrange("p (g k) -> p g k", g=32)
        smi = carve(BIG, 34 * KB, 128, I32)
        smf = carve(BIG, 34 * KB + 128, 128, F32)
        smr = carve(BIG, 34 * KB + 256, 128, F32)
        HTb = HT[:].rearrange("p a b -> p (a b)")
        Rm_re = HTb[:, 0:4096].rearrange("p (g n) -> p g n", g=32)
        Rm_im = HTb[:, 4096:8192].rearrange("p (g n) -> p g n", g=32)
        Rp_re = HTb[:, 8192:12288].rearrange("p (g n) -> p g n", g=32)
        Rp_im = HTb[:, 12288:16384].rearrange("p (g n) -> p g n", g=32)
        P1 = carve(RB, 0, 16 * KB, F32).rearrange("p (g k h) -> p g k h", g=32, k=8)
        P2 = carve(RB, 16 * KB, 16 * KB, F32).rearrange("p (g k h) -> p g k h", g=32, k=8)

        ch = {"dve": None, "act": None}

        def dv(fn, deps=()):
            o = k.op("dve", fn, list(deps) + [ch["dve"], ch["act"]])
            ch["dve"] = o
            tick()
            return o

        def ac(fn, deps=()):
            o = k.op("act", fn, list(deps) + [ch["dve"], ch["act"]])
            ch["act"] = o
            return o

        def sincos(sin_out, cos_out, turns, ti, tf, fr, shape_fn=lambda v: v):
            for (dst, off) in ((sin_out, 0.0), (cos_out, 0.25)):
                if off != 0.0:
                    dv(lambda e: e.tensor_scalar(out=fr, in0=turns, scalar1=off, scalar2=None, op0=ALU.add))
                    src = fr
                else:
                    src = turns
                dv(lambda e, src=src: e.tensor_copy(out=ti, in_=src))
                dv(lambda e: e.tensor_copy(out=tf, in_=ti))
                dv(lambda e, src=src: e.tensor_tensor(out=fr, in0=src, in1=tf, op=ALU.subtract))
                ac(lambda e, dst=dst: e.activation(out=dst, in_=fr, func=AF.Sin, scale=SIN_SCALE))

        ac(lambda e: e.activation(out=dt_t, in_=ldt, func=AF.Exp), [d_ld0, d_ld1])
        dv(lambda e: e.tensor_tensor(out=a_t, in0=lr, in1=dt_t, op=ALU.mult), [d_lr])
        dv(lambda e: e.tensor_tensor(out=th_t, in0=li, in1=dt_t, op=ALU.mult), [d_li])
        dv(lambda e: e.tensor_scalar(out=thn, in0=th_t, scalar1=1.0 / TWO_PI, scalar2=None, op0=ALU.mult))
        ac(lambda e: e.activation(out=mag1, in_=a_t, func=AF.Exp))
        sincos(sin1, cos1, thn, smi, smf, smr)
        dv(lambda e: e.tensor_tensor(out=abim, in0=mag1, in1=sin1, op=ALU.mult))
        dv(lambda e: e.tensor_tensor(out=abm1, in0=mag1, in1=cos1, op=ALU.mult))
        dv(lambda e: e.tensor_scalar(out=abm1, in0=abm1, scalar1=-1.0, scalar2=None, op0=ALU.add))
        dv(lambda e: e.tensor_tensor(out=tA, in0=lr, in1=lr, op=ALU.mult))
        dv(lambda e: e.tensor_tensor(out=tB, in0=li, in1=li, op=ALU.mult))
        dv(lambda e: e.tensor_tensor(out=den, in0=tA, in1=tB, op=ALU.add))
        dv(lambda e: e.reciprocal(out=rden, in_=den))
        dv(lambda e: e.tensor_tensor(out=tA, in0=abm1, in1=lr, op=ALU.mult))
        dv(lambda e: e.tensor_tensor(out=tB, in0=abim, in1=li, op=ALU.mult))
        dv(lambda e: e.tensor_tensor(out=tA, in0=tA, in1=tB, op=ALU.add))
        dv(lambda e: e.tensor_tensor(out=fre, in0=tA, in1=rden, op=ALU.mult))
        dv(lambda e: e.tensor_tensor(out=tA, in0=abim, in1=lr, op=ALU.mult))
        dv(lambda e: e.tensor_tensor(out=tB, in0=abm1, in1=li, op=ALU.mult))
        dv(lambda e: e.tensor_tensor(out=tA, in0=tA, in1=tB, op=ALU.subtract))
        dv(lambda e: e.tensor_tensor(out=fim, in0=tA, in1=rden, op=ALU.mult))
        dv(lambda e: e.tensor_scalar(out=th8n[:], in0=thn, scalar1=8.0, scalar2=None, op0=ALU.mult))
        ac(lambda e: e.activation(out=m8[:], in_=a_t, func=AF.Exp, scale=8.0))
        bc16 = lambda v: v.unsqueeze(2).broadcast_to([128, 32, 16])
        T1 = carve(RB, 0, 2 * KB, F32).rearrange("p (g h) -> p g h", g=32)
        T2 = carve(RB, 2 * KB, 2 * KB, F32).rearrange("p (g h) -> p g h", g=32)
        dv(lambda e: e.tensor_tensor(out=T1, in0=bre, in1=bc16(fre), op=ALU.mult), [d_br])
        dv(lambda e: e.tensor_tensor(out=T2, in0=bim, in1=bc16(fim), op=ALU.mult), [d_bi])
        dv(lambda e: e.tensor_tensor(out=bb_re, in0=T1, in1=T2, op=ALU.subtract))
        dv(lambda e: e.tensor_tensor(out=T1, in0=bim, in1=bc16(fre), op=ALU.mult))
        dv(lambda e: e.tensor_tensor(out=T2, in0=bre, in1=bc16(fim), op=ALU.mult))
        dv(lambda e: e.tensor_tensor(out=bb_im, in0=T1, in1=T2, op=ALU.add))
        def lamtab(ks):
            kv = kvec[:, ks * 8:ks * 8 + 8].unsqueeze(1).broadcast_to([128, 32, 8])
            b8 = lambda v: v.unsqueeze(2).broadcast_to([128, 32, 8])
            dv(lambda e: e.tensor_tensor(out=LT(2), in0=b8(thn), in1=kv, op=ALU.mult), [d_kv])
            dv(lambda e: e.tensor_tensor(out=LT(3), in0=b8(a_t), in1=kv, op=ALU.mult))
            ac(lambda e: e.activation(out=LT(3), in_=LT(3), func=AF.Exp))
            sincos(LT(4), LT(5), LT(2), LTi, LT(6), LT(0))
            dv(lambda e: e.tensor_tensor(out=LT(0), in0=LT(3), in1=LT(5), op=ALU.mult))
            dv(lambda e: e.tensor_tensor(out=LT(1), in0=LT(3), in1=LT(4), op=ALU.mult))

        bk16 = lambda v: v.unsqueeze(3).broadcast_to([128, 32, 8, 16])
        bj8 = lambda v: v.unsqueeze(2).broadcast_to([128, 32, 8, 16])
        v4 = lambda v: v.rearrange("p g (k h) -> p g k h", k=8)

        def cprod(out_re, out_im, Xre, Xim, neg_im=False):
            dv(lambda e: e.tensor_tensor(out=P1, in0=bj8(Xre), in1=bk16(LT(0)), op=ALU.mult))
            dv(lambda e: e.tensor_tensor(out=P2, in0=bj8(Xim), in1=bk16(LT(1)), op=ALU.mult))
            dv(lambda e: e.tensor_tensor(out=v4(out_re), in0=P1, in1=P2, op=ALU.subtract))
            dv(lambda e: e.tensor_tensor(out=P1, in0=bj8(Xre), in1=bk16(LT(1)), op=ALU.mult))
            dv(lambda e: e.tensor_tensor(out=P2, in0=bj8(Xim), in1=bk16(LT(0)), op=ALU.mult))
            if neg_im:
                dv(lambda e: e.scalar_tensor_tensor(out=out_im, in0=P1.rearrange("p g k h -> p g (k h)"), scalar=-1.0,
                                                    in1=P2.rearrange("p g k h -> p g (k h)"), op0=ALU.mult, op1=ALU.subtract))
            else:
                dv(lambda e: e.tensor_tensor(out=v4(out_im), in0=P1, in1=P2, op=ALU.add))

        lamtab(0)
        ch["dve"] = [ch["dve"]] + d_cst
        cprod(Cre, Cimn, Cq_re, Cq_im, neg_im=True)
        lamtab(1)
        cprod(Rm_re, Rm_im, bb_re, bb_im)
        lamtab(2)
        o_rp = cprod(Rp_re, Rp_im, bb_re, bb_im)
        o_params = ch["dve"]
        while pending:
            pending.pop(0)()
        o_sc1p = k.op("dve", lambda e: e.tensor_scalar(out=sc1p[:], in0=modA[:, 1, :], scalar1=1.0, scalar2=None, op0=ALU.add), o_sc1)
        if debug == "b1":
            finish()
            return nc

        for r, Rm in enumerate((Rm_re, Rm_im)):
            for g8 in range(4):
                bk, fr_ = banks.get()
                last = None
                for gi in range(8):
                    gq = g8 * 8 + gi
                    last = k.op("pe", lambda e, gq=gq, gi=gi, bk=bk, Rm=Rm: e.transpose(out=banks.bf(bk)[:, gi * 128:(gi + 1) * 128],
                                                                                      in_=Rm[:, gq, :], identity=identb[:]),
                                [fr_, o_params, o_idb] if gi == 0 else (), signal=(gi == 7))
                o = k.op("act", lambda e, g8=g8, r=r, bk=bk: e.activation(out=MinT[:, g8 * 8:(g8 + 1) * 8, r, :],
                                                                      in_=banks.bf(bk).rearrange("p (g m) -> p g m", g=8), func=AF.Copy), [last])
                banks.rel(bk, o)
        o_minT = k.last("act")
        if debug == "b2":
            finish()
            return nc

        Ttmp = carve(BIG, 10 * KB, 2 * KB, F32).rearrange("p (a b) -> p a b", a=4)
        o_T = []
        tfree = None
        for g4 in range(16):
            gqb, g2 = g4 // 2, g4 % 2
            pr = slice(64 * g2, 64 * g2 + 64)
            bk, fr_ = banks.get()
            last = None
            for gi in range(4):
                gq = gqb * 4 + gi
                k.op("pe", lambda e, gq=gq, pr=pr, gi=gi, bk=bk: e.matmul(banks.f32(bk)[:, gi * 128:(gi + 1) * 128], lhsT=Rp_re[pr, gq, :], rhs=Cre[pr, gq, :],
                                                                            start=True, stop=False),
                     [fr_, o_params] if gi == 0 else (), signal=False)
                last = k.op("pe", lambda e, gq=gq, pr=pr, gi=gi, bk=bk: e.matmul(banks.f32(bk)[:, gi * 128:(gi + 1) * 128], lhsT=Rp_im[pr, gq, :], rhs=Cimn[pr, gq, :],
                                                                                   start=False, stop=True), signal=(gi == 3))
            o = k.op("dve", lambda e, bk=bk: e.tensor_tensor(out=Ttmp, in0=banks.f32(bk).rearrange("p (a b) -> p a b", a=4),
                                                            in1=maskT[:].unsqueeze(1).broadcast_to([128, 4, 128]), op=ALU.mult), [last, d_mk, tfree])
            banks.rel(bk, o)
            for gi in range(4):
                g = 2 * (gqb * 4 + gi) + g2
                tfree = k.op("dve", lambda e, g=g, gi=gi: e.scalar_tensor_tensor(out=Tsb[:, g, :], in0=identf[:], scalar=dvec[:, g:g + 1], in1=Ttmp[:, gi, :],
                                                                                op0=ALU.mult, op1=ALU.add), [o] + d_dv)
            o_T.append(tfree)
        if debug == "b3":
            finish()
            return nc
        k.barrier()
        TT = lambda i: carve(BIG, i * 16512, 16512, F32).rearrange("p (g c) -> p g c", g=32)
        TT2 = HT[:].rearrange("p a b -> p (a b)")[:, 0:8256].bitcast(F32).rearrange("p (g c) -> p g c", g=32)
        TTi = carve(BIG, 0, 16512, I32).rearrange("p (g c) -> p g c", g=32)
        ch["dve"] = k.last("dve"); ch["act"] = k.last("act")
        dv(lambda e: e.tensor_tensor(out=TT(1), in0=th8n[:].unsqueeze(2).broadcast_to([128, 32, 129]),
                                     in1=iota[:].unsqueeze(1).broadcast_to([128, 32, 129]), op=ALU.mult), [d_io])
        for (dst, off) in ((TS, 0.0), (TC, 0.25)):
            if off != 0.0:
                dv(lambda e: e.tensor_scalar(out=TT(1), in0=TT(1), scalar1=off, scalar2=None, op0=ALU.add))
            dv(lambda e: e.tensor_copy(out=TTi, in_=TT(1)))
            dv(lambda e: e.tensor_copy(out=TT2, in_=TTi))
            dv(lambda e: e.tensor_tensor(out=TT2, in0=TT(1),
                                         in1=TT2, op=ALU.subtract))
            ac(lambda e, dst=dst: e.activation(out=dst, in_=TT2, func=AF.Sin, scale=SIN_SCALE))
        k.barrier()

        if debug == "s5mats":
            dd = carve(BIG, 0, 32 * KB, F32)
            o1 = k.op("dve", lambda e: e.tensor_copy(out=dd[:, 0:4096], in_=MinT.rearrange("p g r m -> p (g r m)")[:, 0:4096]))
            dump(dd[:, 0:4096], 0, [o1])
            o2 = k.op("dve", lambda e: e.tensor_copy(out=dd[:, 4096:8192], in_=Tsb.rearrange("p g n -> p (g n)")[:, 0:4096]))
            dump(dd[:, 4096:8192], 4096, [o2])
            k.barrier()
            o3 = k.op("dve", lambda e: e.tensor_copy(out=dd[:, 0:4096], in_=Cre.rearrange("p g n -> p (g n)")))
            dump(dd[:, 0:4096], 8192, [o3])
            o4 = k.op("dve", lambda e: e.tensor_copy(out=dd[:, 4096:8192], in_=Cimn.rearrange("p g n -> p (g n)")))
            dump(dd[:, 4096:8192], 12288, [o4])
            finish()
            return nc
        if debug == "tables":
            dump(TC.rearrange("p g c -> p (g c)"), 0, [])
            dump(TS.rearrange("p g c -> p (g c)"), 4128, [])
            dump(m8[:], 8256, [])
            dump(th8n[:], 8288, [])
            finish()
            return nc


        def ln_stats(j, src_fn, deps):
            o = None
            for q in range(4):
                o = k.op("dve", lambda e, q=q: e.bn_stats(out=stat[:, q, :], in_=src_fn(q)), deps if q == 0 else ())
            o = k.op("dve", lambda e: e.bn_aggr(out=mv[:, j, :], in_=stat[:].rearrange("p a b -> p (a b)")), [o])
            o = k.op("act", lambda e: e.activation(out=sdv[:, j:j + 1], in_=mv[:, j, 1:2], func=AF.Sqrt, bias=EPS), [o])
            o = k.op("dve", lambda e: e.reciprocal(out=rstd[:, j:j + 1], in_=sdv[:, j:j + 1]), [o])
            o = k.op("dve", lambda e: e.tensor_scalar(out=nmr[:, j:j + 1], in0=mv[:, j, 0:1], scalar1=rstd[:, j:j + 1], scalar2=-1.0,
                                                     op0=ALU.mult, op1=ALU.mult), [o])
            return o

        def transposes_to_hT(nbf, dst_fn, scaleA, biasA, deps, all_dve=False):
            evs = []
            for q in range(4):
                bk, fr_ = banks.get()
                last = None
                for kk in range(4):
                    kc = 4 * q + kk
                    last = k.op("pe", lambda e, kc=kc, kk=kk, bk=bk: e.transpose(out=banks.bf(bk)[:, kk * 128:(kk + 1) * 128],
                                                                              in_=nbf[:, kc * 128:(kc + 1) * 128], identity=identb[:]),
                                [fr_] + list(deps) if kk == 0 else (), signal=(kk == 3))
                o = None
                for kk in range(4):
                    kc = 4 * q + kk
                    if q % 2 == 0 and not all_dve:
                        o = k.op("act", lambda e, kc=kc, kk=kk, bk=bk: e.activation(out=dst_fn(kc), in_=banks.bf(bk)[:, kk * 128:(kk + 1) * 128], func=AF.Identity,
                                                                                 scale=scaleA[:, kc:kc + 1], bias=biasA[:, kc:kc + 1]), [last])
                    else:
                        o = k.op("dve", lambda e, kc=kc, kk=kk, bk=bk: e.tensor_scalar(out=dst_fn(kc), in0=banks.bf(bk)[:, kk * 128:(kk + 1) * 128],
                                                                                    scalar1=scaleA[:, kc:kc + 1], scalar2=biasA[:, kc:kc + 1],
                                                                                    op0=ALU.mult, op1=ALU.add), [last])
                    evs.append(o)
                banks.rel(bk, evs[-1])
            return evs

        SCR = lambda off, n, dt: carve(BIG, off, n, dt)
        xt = [SCR(16 * KB, 8 * KB, F32), SCR(24 * KB, 8 * KB, F32)]
        nbf = [SCR(32 * KB, 4 * KB, BF16), SCR(36 * KB, 4 * KB, BF16)]
        xt_free = [None, None]
        nbf_free = [None, None]

        wu = []
        wu_d = []
        wu_i = []
        for n2 in range(2):
            buf, dops, bi = ws.load([(lambda b: b[:], wsrc(w_in, 0, D, n2 * 512, 512))])
            wu.append(buf); wu_d.append(dops); wu_i.append(bi)

        def u_mm(hT_fn, j, deps):
            res = []
            last = None
            for n2 in range(2):
                bk, fr_ = banks.get()
                for kc in range(16):
                    last = k.op("pe", lambda e, kc=kc, bk=bk, n2=n2: e.matmul(banks.f32(bk), lhsT=hT_fn(kc), rhs=wu[n2][:, kc, :], start=(kc == 0), stop=(kc == 15)),
                                [fr_] + list(deps) + wu_d[n2] if kc == 0 else (), signal=(kc == 15))
                res.append((bk, last))
            return res, last

        def u_ev(Udst, j, res):
            evs = []
            for n2, (bk, last) in enumerate(res):
                o = k.op("dve", lambda e, bk=bk, n2=n2: e.tensor_copy(out=Udst[:, 32 * n2:32 * n2 + 32, j, :], in_=banks.f32(bk).rearrange("p (g h) -> p g h", g=32)), [last])
                banks.rel(bk, o)
                evs.append(o)
            return evs

        def ln_a(xv, j, extra):
            b2 = j % 2
            d_x = k.dma("sp", xt[b2], xv[:, j, :], deps=[xt_free[b2]] + list(extra))
            o = None
            for q in range(4):
                o = k.op("dve", lambda e, q=q, b2=b2: e.bn_stats(out=stat[:, q, :], in_=xt[b2][:, q * 512:(q + 1) * 512]), [d_x] if q == 0 else ())
            o = k.op("dve", lambda e: e.bn_aggr(out=mv[:, j, :], in_=stat[:].rearrange("p a b -> p (a b)")), [o])
            return k.op("act", lambda e: e.activation(out=sdv[:, j:j + 1], in_=mv[:, j, 1:2], func=AF.Sqrt, bias=EPS), [o])

        def ln_b(j, o_sq):
            b2 = j % 2
            o = k.op("dve", lambda e: e.reciprocal(out=rstd[:, j:j + 1], in_=sdv[:, j:j + 1]), [o_sq])
            o = k.op("dve", lambda e: e.tensor_scalar(out=nmr[:, j:j + 1], in0=mv[:, j, 0:1], scalar1=rstd[:, j:j + 1], scalar2=-1.0,
                                                     op0=ALU.mult, op1=ALU.mult), [o])
            o_n = k.op("dve", lambda e, b2=b2: e.tensor_scalar(out=nbf[b2], in0=xt[b2], scalar1=rstd[:, j:j + 1], scalar2=nmr[:, j:j + 1],
                                                            op0=ALU.mult, op1=ALU.add), [o, nbf_free[b2]])
            xt_free[b2] = o_n
            return o_n

        def u_matmuls(hT_fn, Udst, j, deps):
            evs = []
            last = None
            for n2 in range(2):
                bk, fr_ = banks.get()
                for kc in range(16):
                    last = k.op("pe", lambda e, kc=kc, bk=bk, n2=n2: e.matmul(banks.f32(bk), lhsT=hT_fn(kc), rhs=wu[n2][:, kc, :], start=(kc == 0), stop=(kc == 15)),
                                [fr_] + list(deps) + wu_d[n2] if kc == 0 else (), signal=(kc == 15))
                eng = "act" if n2 == 0 else "dve"
                if eng == "act":
                    o = k.op("act", lambda e, bk=bk, n2=n2: e.activation(out=Udst[:, 32 * n2:32 * n2 + 32, j, :], in_=banks.f32(bk).rearrange("p (g h) -> p g h", g=32), func=AF.Copy), [last])
                else:
                    o = k.op("dve", lambda e, bk=bk, n2=n2: e.tensor_copy(out=Udst[:, 32 * n2:32 * n2 + 32, j, :], in_=banks.f32(bk).rearrange("p (g h) -> p g h", g=32)), [last])
                banks.rel(bk, o)
                evs.append(o)
            return evs, last

        HTb2 = HT[:].rearrange("p a b -> p (a b)")
        Uprev = HTb2[:, 0:8192].rearrange("p (g i h) -> p g i h", g=64, i=8)
        hpT = [HTb2[:, 8192:10240].rearrange("p (a b) -> p a b", a=16), HTb2[:, 10240:12288].rearrange("p (a b) -> p a b", a=16)]
        hpT_free = [None, None]
        xpv = x_prev.rearrange("(c j) d -> c j d", j=8)
        xov = x_own.rearrange("(c j) d -> c j d", j=8)
        u_evs = []
        tail_ops = []
        sq_ = {0: ln_a(xpv, 0, [o_T[-1]])}
        o_ns = {0: ln_b(0, sq_[0])}
        sq_[1] = ln_a(xpv, 1, [o_T[-1]])
        for j in range(8):
            b2 = j % 2
            evs = transposes_to_hT(nbf[b2], lambda kc, b2=b2: hpT[b2][:, kc, :], sc1p, modA[:, 0, :], [o_ns[j], hpT_free[b2], o_sc1p] + o_sh1, all_dve=True)
            nbf_free[b2] = k.last("pe")
            o_tl = k.op("act", lambda e, b2=b2, j=j: e.activation(out=tail[:, :, 2 * j:2 * j + 2], in_=hpT[b2][:, :, 126:128], func=AF.Copy), evs)
            tail_ops.append(o_tl)
            if j + 1 < 8:
                o_ns[j + 1] = ln_b(j + 1, sq_[j + 1])
            if j + 2 < 8:
                sq_[j + 2] = ln_a(xpv, j + 2, [o_T[-1]])
            um, lastpe = u_mm(lambda kc, b2=b2: hpT[b2][:, kc, :], j, evs)
            u_evs += u_ev(Uprev, j, um)
            hpT_free[b2] = [lastpe, o_tl]

        sofs = {"o": 0}

        def salloc(n, base):
            o = sofs["o"]
            sofs["o"] += (n + 63) // 64 * 64
            return base + o

        def s5_temps(base):
            sofs["o"] = 0
            t = {}
            t["UgT"] = carve(BIG, salloc(2048, base), 2048, BF16).rearrange("p (g c) -> p g c", g=8)
            for nm in ("Vre", "Vim", "t1", "t2"):
                t[nm] = carve(BIG, salloc(2048, base), 2048, F32).rearrange("p (g c) -> p g c", g=4)
            for nm in ("Wre", "Wim"):
                t[nm] = carve(BIG, salloc(2064, base), 2064, F32).rearrange("p (g c) -> p g c", g=4)
            for nm in ("Xre", "Xim"):
                t[nm] = carve(BIG, salloc(1024, base), 1024, BF16).rearrange("p (g c) -> p g c", g=4)
            for nm in ("ysb", "sg", "ztm"):
                t[nm] = salloc(2048, base)
            t["end"] = sofs["o"]
            return t

        def s5_states(U, tm, b, start_deps, own):
            gqs = slice(4 * b, 4 * b + 4)
            bk, fr_ = banks.get()
            last = None
            for gi in range(8):
                g = 8 * b + gi
                last = k.op("pe", lambda e, g=g, gi=gi, bk=bk: e.transpose(out=banks.bf(bk)[:, gi * 128:(gi + 1) * 128],
                                                                        in_=U[:, g].rearrange("p i h -> p (i h)"), identity=identb[:]),
                            [fr_] + list(start_deps) if gi == 0 else (), signal=(gi == 7))
            o_ug = k.op("act", lambda e, bk=bk: e.activation(out=tm["UgT"].rearrange("p g c -> p (g c)"), in_=banks.bf(bk), func=AF.Copy), [last] + list(start_deps))
            banks.rel(bk, o_ug)
            bre_, fr1 = banks.get()
            bim_, fr2 = banks.get()
            lastS = None
            for ri, bkk, frr in ((0, bre_, fr1), (1, bim_, fr2)):
                first = True
                for gi in range(4):
                    for g2 in range(2):
                        lastS = k.op("pe", lambda e, gi=gi, g2=g2, ri=ri, bkk=bkk: e.matmul(banks.f32(bkk)[64 * g2:64 * g2 + 64, gi * 128:(gi + 1) * 128],
                                                                                            lhsT=MinT[:, 4 * b + gi, ri, 64 * g2:64 * g2 + 64], rhs=tm["UgT"][:, 2 * gi + g2, :],
                                                                                            start=True, stop=True),
                                     [frr, o_ug, o_minT] if first else (), signal=(gi == 3 and g2 == 1))
                        first = False
            Sre = banks.f32(bre_).rearrange("p (g c) -> p g c", g=4)
            Sim = banks.f32(bim_).rearrange("p (g c) -> p g c", g=4)
            tcd = TC[:, gqs, 1:129]
            tsd = TS[:, gqs, 1:129]
            c_ = {"o": [lastS] + list(start_deps)}

            def dvc(fn, extra=()):
                o = k.op("dve", fn, c_["o"] + list(extra))
                c_["o"] = [o]
                return o

            dvc(lambda e: e.tensor_tensor(out=tm["t1"], in0=Sre, in1=tcd, op=ALU.mult))
            dvc(lambda e: e.tensor_tensor(out=tm["t2"], in0=Sim, in1=tsd, op=ALU.mult))
            dvc(lambda e: e.tensor_tensor(out=tm["Vre"], in0=tm["t1"], in1=tm["t2"], op=ALU.add))
            dvc(lambda e: e.tensor_tensor(out=tm["t1"], in0=Sim, in1=tcd, op=ALU.mult))
            o_s = dvc(lambda e: e.tensor_tensor(out=tm["t2"], in0=Sre, in1=tsd, op=ALU.mult))
            banks.rel(bre_, o_s)
            banks.rel(bim_, o_s)
            dvc(lambda e: e.tensor_tensor(out=tm["Vim"], in0=tm["t1"], in1=tm["t2"], op=ALU.subtract))
            if own:
                dvc(lambda e: e.tensor_scalar(out=tm["Wre"][:, :, 0], in0=xinit[:, 0, gqs], scalar1=flag[:, 0:1], scalar2=None, op0=ALU.mult), [d_fl])
                dvc(lambda e: e.tensor_scalar(out=tm["Wim"][:, :, 0], in0=xinit[:, 1, gqs], scalar1=flag[:, 0:1], scalar2=None, op0=ALU.mult))
            else:
                dvc(lambda e: e.memset(tm["Wre"][:, :, 0], 0.0))
                dvc(lambda e: e.memset(tm["Wim"][:, :, 0], 0.0))
            for gi in range(4):
                gq = 4 * b + gi
                for nm_w, nm_v in (("Wre", "Vre"), ("Wim", "Vim")):
                    dvc(lambda e, gi=gi, gq=gq, nm_w=nm_w, nm_v=nm_v: e.tensor_tensor_scan(out=tm[nm_w][:, gi, 1:129], data0=m8[:, gq:gq + 1].broadcast_to([128, 128]),
                                                                                        data1=tm[nm_v][:, gi, :], initial=tm[nm_w][:, gi, 0:1], op0=ALU.mult, op1=ALU.add))
            if not own:
                tce = TC[:, gqs, 128]
                tse = TS[:, gqs, 128]
                s1 = tm["t1"][:, :, 0]
                s2 = tm["t2"][:, :, 0]
                dvc(lambda e: e.tensor_tensor(out=s1, in0=tce, in1=tm["Wre"][:, :, 128], op=ALU.mult))
                dvc(lambda e: e.tensor_tensor(out=s2, in0=tse, in1=tm["Wim"][:, :, 128], op=ALU.mult))
                dvc(lambda e: e.tensor_tensor(out=xinit[:, 0, gqs], in0=s1, in1=s2, op=ALU.subtract))
                dvc(lambda e: e.tensor_tensor(out=s1, in0=tce, in1=tm["Wim"][:, :, 128], op=ALU.mult))
                dvc(lambda e: e.tensor_tensor(out=s2, in0=tse, in1=tm["Wre"][:, :, 128], op=ALU.mult))
                return dvc(lambda e: e.tensor_tensor(out=xinit[:, 1, gqs], in0=s1, in1=s2, op=ALU.add))
            tcr = TC[:, gqs, 0:128]
            tsr = TS[:, gqs, 0:128]
            Wr = tm["Wre"][:, :, 0:128]
            Wi = tm["Wim"][:, :, 0:128]
            dvc(lambda e: e.tensor_tensor(out=tm["t1"], in0=tcr, in1=Wr, op=ALU.mult))
            dvc(lambda e: e.tensor_tensor(out=tm["t2"], in0=tsr, in1=Wi, op=ALU.mult))
            dvc(lambda e: e.tensor_tensor(out=tm["Xre"], in0=tm["t1"], in1=tm["t2"], op=ALU.subtract))
            dvc(lambda e: e.tensor_tensor(out=tm["t1"], in0=tcr, in1=Wi, op=ALU.mult))
            dvc(lambda e: e.tensor_tensor(out=tm["t2"], in0=tsr, in1=Wr, op=ALU.mult))
            return dvc(lambda e: e.tensor_tensor(out=tm["Xim"], in0=tm["t1"], in1=tm["t2"], op=ALU.add))

        tmP = s5_temps(0)
        prev = list(u_evs)
        for b in range(8):
            prev = [s5_states(Uprev, tmP, b, prev, own=False)]
        o_xinit = prev[0]

        if debug == "xinit":
            dump(xinit[:].rearrange("p a b -> p (a b)"), 0, [o_xinit])
            dd = carve(BIG, 16 * KB, 2 * KB, F32)
            o1 = k.op("dve", lambda e: e.tensor_copy(out=dd.rearrange("p (a b) -> p a b", a=16)[:, :, 0:16], in_=tail[:]), tail_ops + [o_xinit])
            dump(dd[:, 0:256], 64, [o1])
            finish()
            return nc
        k.barrier()

        U = carve(BIG, 0, 16 * KB, BF16).rearrange("p (g i h) -> p g i h", g=64, i=8)
        u_evs = []
        lastpe = None
        sq_ = {0: ln_a(xov, 0, [])}
        o_ns = {0: ln_b(0, sq_[0])}
        sq_[1] = ln_a(xov, 1, [])
        for j in range(8):
            b2 = j % 2
            evs = transposes_to_hT(nbf[b2], lambda kc, j=j: HT[:, kc, j * 128:(j + 1) * 128], sc1p, modA[:, 0, :], [o_ns[j]], all_dve=True)
            nbf_free[b2] = k.last("pe")
            if j + 1 < 8:
                o_ns[j + 1] = ln_b(j + 1, sq_[j + 1])
            if j + 2 < 8:
                sq_[j + 2] = ln_a(xov, j + 2, [])
            um, lastpe = u_mm(lambda kc, j=j: HT[:, kc, j * 128:(j + 1) * 128], j, evs)
            u_evs += u_ev(U, j, um)
        for n2 in range(2):
            ws.release(wu_i[n2], lastpe)
        k.barrier()

        tm = s5_temps(16 * KB)
        assert 16 * KB + tm["end"] <= 40 * KB, tm["end"]
        zT = carve(BIG, 0, 16 * KB, BF16).rearrange("p (b n) -> p b n", b=8)
        ysb = carve(BIG, tm["ysb"], 2048, F32)
        sgb = carve(BIG, tm["sg"], 2048, F32)
        ztm = carve(BIG, tm["ztm"], 2048, BF16).rearrange("p (j q g2 h) -> p j q g2 h", j=8, q=4, g2=2)
        tmpC = tm["t1"].rearrange("p g c -> p (g c)")
        sqb = tm["t2"].rearrange("p g c -> p (g c)")
        prev = []
        zt_ops = []
        for b in range(8):
            o_x = s5_states(U, tm, b, prev, own=True)
            gl = []
            for g2 in range(2):
                pr = slice(64 * g2, 64 * g2 + 64)
                bA, frA = banks.get()
                lastA = None
                for gi in range(4):
                    k.op("pe", lambda e, gi=gi, pr=pr, bA=bA, b=b: e.matmul(banks.f32(bA)[:, gi * 128:(gi + 1) * 128], lhsT=tm["Xre"][pr, gi, :], rhs=Cre[pr, 4 * b + gi, :],
                                                                       start=True, stop=False), [frA, o_x] if gi == 0 else (), signal=False)
                    lastA = k.op("pe", lambda e, gi=gi, pr=pr, bA=bA, b=b: e.matmul(banks.f32(bA)[:, gi * 128:(gi + 1) * 128], lhsT=tm["Xim"][pr, gi, :], rhs=Cimn[pr, 4 * b + gi, :],
                                                                               start=False, stop=True), signal=(gi == 3))
                bC, frC = banks.get()
                lastC = None
                for gi in range(4):
                    lastC = k.op("pe", lambda e, gi=gi, g2=g2, bC=bC, b=b: e.matmul(banks.f32(bC)[:, gi * 128:(gi + 1) * 128], lhsT=tm["UgT"][:, 2 * gi + g2, :], rhs=Tsb[:, 8 * b + 2 * gi + g2, :],
                                                                               start=True, stop=True), [frC] + o_T if gi == 0 else (), signal=(gi == 3))
                o1 = k.op("act", lambda e, bC=bC: e.activation(out=tmpC, in_=banks.f32(bC), func=AF.Copy), [lastC, o_x] + gl)
                banks.rel(bC, o1)
                o2 = k.op("dve", lambda e, bA=bA: e.tensor_tensor(out=ysb, in0=banks.f32(bA), in1=tmpC, op=ALU.add), [lastA, o1] + gl)
                banks.rel(bA, o2)
                o3 = k.op("act", lambda e: e.activation(out=sqb, in_=ysb, func=AF.Square), [o2])
                o4 = k.op("dve", lambda e: e.tensor_scalar(out=sqb, in0=sqb, scalar1=0.044715, scalar2=1.0, op0=ALU.mult, op1=ALU.add), [o3])
                o5 = k.op("dve", lambda e: e.tensor_tensor(out=sqb, in0=sqb, in1=ysb, op=ALU.mult), [o4])
                o6 = k.op("act", lambda e: e.activation(out=sgb, in_=sqb, func=AF.Sigmoid, scale=GELU_C), [o5])
                o7 = k.op("dve", lambda e, g2=g2: e.tensor_tensor(out=ztm[:, :, :, g2, :].rearrange("p j q h -> p q j h"),
                                                                in0=sgb.rearrange("p (q j h) -> p q j h", q=4, j=8), in1=ysb.rearrange("p (q j h) -> p q j h", q=4, j=8),
                                                                op=ALU.mult), [o6] + zt_ops[-1:])
                gl = [o7]
            bk, fr_ = banks.get()
            last = None
            for j in range(8):
                last = k.op("pe", lambda e, j=j, bk=bk: e.transpose(out=banks.bf(bk)[:, j * 128:(j + 1) * 128], in_=ztm[:, j].rearrange("p q g h -> p (q g h)"), identity=identb[:]),
                            [fr_, o7] if j == 0 else (), signal=(j == 7))
            oz = k.op("act", lambda e, b=b, bk=bk: e.activation(out=zT[:, b, :], in_=banks.bf(bk), func=AF.Copy), [last])
            banks.rel(bk, oz)
            zt_ops.append(oz)
            prev = [o7, oz]

        if debug == "z":
            k.barrier()
            dd = carve(BIG, 40 * KB, 32 * KB, F32)
            o1 = k.op("dve", lambda e: e.tensor_copy(out=dd, in_=zT.rearrange("p b n -> p (b n)")))
            dump(dd, 0, [o1])
            finish()
            return nc
        k.barrier()


        upT = carve(BIG, 40 * KB, 33280, F32).rearrange("p (k j c) -> p k j c", k=8, j=8)
        pA = carve(BIG, 73 * KB, 4160, F32).rearrange("p (j c) -> p j c", j=8)
        pB = carve(BIG, 78 * KB, 4160, F32).rearrange("p (j c) -> p j c", j=8)
        rct = carve(BIG, 83 * KB, 4096, F32).rearrange("p (j c) -> p j c", j=8)
        pooledT = carve(RB, 0, 16 * KB, BF16).rearrange("p (k n) -> p k n", k=8)
        qsT = carve(BIG, 16 * KB, 16 * KB, BF16).rearrange("p (k n) -> p k n", k=8)
        o_ms = [k.op("dve", lambda e: e.memset(pA, 0.0)), k.op("dve", lambda e: e.memset(pB, 0.0))]
        up_ops = {}
        for mb in range(2):
            buf, dops, bi = ws.load([(lambda b_: b_[:], wsrc(w_in, 0, D, 1024 + mb * 512, 512))])
            last = None
            for m in range(4):
                k8 = mb * 4 + m
                ops = []
                for half in range(2):
                    bk, fr_ = banks.get()
                    for kc in range(16):
                        last = k.op("pe", lambda e, kc=kc, m=m, half=half, bk=bk, buf=buf: e.matmul(banks.f32(bk), lhsT=buf[:, kc, m * 128:(m + 1) * 128],
                                                                                                 rhs=HT[:, kc, half * 512:(half + 1) * 512], start=(kc == 0), stop=(kc == 15)),
                                    [fr_] + dops if kc == 0 else (), signal=(kc == 15))
                    o = k.op("act", lambda e, k8=k8, half=half, bk=bk: e.activation(out=upT[:, k8, 4 * half:4 * half + 4, 2:130], in_=banks.f32(bk).rearrange("p (j c) -> p j c", j=4),
                                                                                 func=AF.Copy), [last])
                    banks.rel(bk, o)
                    ops.append(o)
                bk, fr_ = banks.get()
                for kc in range(16):
                    last = k.op("pe", lambda e, kc=kc, m=m, bk=bk, buf=buf: e.matmul(banks.f32(bk)[:, 0:16], lhsT=buf[:, kc, m * 128:(m + 1) * 128], rhs=tail[:, kc, :],
                                                                                  start=(kc == 0), stop=(kc == 15)), [fr_] + tail_ops if kc == 0 else (), signal=(kc == 15))
                o = k.op("dve", lambda e, k8=k8, bk=bk: e.tensor_scalar(out=upT[:, k8, :, 0:2], in0=banks.f32(bk)[:, 0:16].rearrange("p (j c) -> p j c", j=8),
                                                                     scalar1=flag[:, 0:1], scalar2=None, op0=ALU.mult), [last, d_fl])
                banks.rel(bk, o)
                ops.append(o)
                up_ops[k8] = ops
            ws.release(bi, last)

        rc_free = [None]
        pl_ops = []
        tmp_free = {"A": o_ms[0:1], "B": o_ms[1:2]}
        for gi in range(4):
            d_rc = k.dma("sp", rct, rc_d[gi:gi + 1, :].rearrange("o (j c) -> o j c", j=8).broadcast_to([128, 8, 128]), deps=rc_free)
            gl = []
            for a in range(2):
                k8 = 2 * gi + a
                eng = "dve"
                u_ = upT[:, k8]
                cur = u_
                c_ = {"o": list(up_ops[k8]) + tmp_free["A"] + tmp_free["B"]}

                def pc(fn, extra=()):
                    o = k.op(eng, fn, c_["o"] + list(extra))
                    c_["o"] = [o]
                    return o

                names = ["A", "B"]
                tb = {"A": pA, "B": pB}
                lvl = 0
                for s_ in (1, 2, 4, 8)[:gi + 1]:
                    dst = tb[names[lvl % 2]]
                    if s_ < 8:
                        pc(lambda e, dst=dst, cur=cur, s_=s_: e.tensor_tensor(out=dst[:, s_:8, :], in0=cur[:, s_:8, :], in1=cur[:, 0:8 - s_, :], op=ALU.add))
                        pc(lambda e, dst=dst, cur=cur, s_=s_: e.tensor_tensor(out=dst[:, 0:s_, 1:130], in0=cur[:, 0:s_, 1:130], in1=cur[:, 8 - s_:8, 0:129], op=ALU.add))
                    else:
                        pc(lambda e, dst=dst, cur=cur: e.tensor_tensor(out=dst[:, :, 1:130], in0=cur[:, :, 1:130], in1=cur[:, :, 0:129], op=ALU.add))
                    cur = dst
                    lvl += 1
                oth = tb[names[lvl % 2]]
                pc(lambda e, oth=oth, cur=cur: e.tensor_tensor(out=oth[:, :, 2:130], in0=cur[:, :, 2:130], in1=rct, op=ALU.mult), [d_rc])
                o = pc(lambda e, oth=oth, u_=u_, k8=k8: e.tensor_tensor(out=pooledT[:, k8, :].rearrange("p (j c) -> p j c", j=8), in0=oth[:, :, 2:130], in1=u_[:, :, 2:130], op=ALU.subtract))
                tmp_free["A"] = [o]
                tmp_free["B"] = [o]
                gl.append(o)
                pl_ops.append(o)
            rc_free = gl
        buf, dops, bi = ws.load([(lambda b_: b_[:, 0:8, 0:256], w_pool.rearrange("(r p) n -> p r n", p=128))])
        last = None
        qs_ops = []
        for gi in range(4):
            for a in range(2):
                k8o = 2 * gi + a
                for half in range(2):
                    bk, fr_ = banks.get()
                    for bs in range(2):
                        last = k.op("pe", lambda e, gi=gi, a=a, bs=bs, half=half, bk=bk, buf=buf: e.matmul(banks.f32(bk), lhsT=buf[:, 2 * gi + bs, a * 128:(a + 1) * 128],
                                                                                                        rhs=pooledT[:, 2 * gi + bs, half * 512:(half + 1) * 512], start=(bs == 0), stop=(bs == 1)),
                                    [fr_] + dops + pl_ops[2 * gi:2 * gi + 2] if bs == 0 else (), signal=(bs == 1))
                    o = k.op("act", lambda e, k8o=k8o, half=half, bk=bk: e.activation(out=qsT[:, k8o, half * 512:(half + 1) * 512], in_=banks.f32(bk), func=AF.Identity,
                                                                                   scale=pscale[:, k8o:k8o + 1]), [last, d_ps])
                    banks.rel(bk, o)
                    qs_ops.append(o)
        ws.release(bi, last)
        k.barrier()

        merged = carve(RB, 0, 32 * KB, BF16).rearrange("p (m n) -> p m n", m=16)
        tmpA = carve(BIG, 32 * KB, 16 * KB, F32).rearrange("p (m n) -> p m n", m=4)
        tmpB = carve(BIG, 48 * KB, 16 * KB, F32).rearrange("p (m n) -> p m n", m=4)
        s1 = [carve(BIG, 72 * KB, 2 * KB, F32), carve(BIG, 74 * KB, 2 * KB, F32)]
        slot = [carve(BIG, (64 + 8 * i) * KB, 8 * KB, F32) for i in range(3)]
        g1_ops = []
        o_sh2 = []
        o_sc2 = []
        s1_free = [None, None]
        sidx = [0]
        AB_free = [None]

        def mm_group(bk, fr_, n_k, lhs_fn, rhs_fn, deps):
            last = None
            for kc in range(n_k):
                last = k.op("pe", lambda e, kc=kc: e.matmul(banks.f32(bk), lhsT=lhs_fn(kc), rhs=rhs_fn(kc), start=(kc == 0), stop=(kc == n_k - 1)),
                            [fr_] + list(deps) if kc == 0 else (), signal=(kc == n_k - 1))
            return last

        p5_specs = []
        for mb_ in range(4):
            c0_ = mb_ * 512
            p5_specs.append([(lambda b_: b_[:, 0:8, :], wsrc(w_val, 0, 1024, c0_, 512)), (lambda b_: b_[:, 8:16, :], wsrc(w_gate, 0, 1024, c0_, 512))])
            p5_specs.append([(lambda b_: b_[:], wsrc(w_in, 0, D, 2048 + c0_, 512))])
            p5_specs.append([(lambda b_: b_[:, 0:8, :], wsrc(w_pool_out, 0, 1024, c0_, 512))])
            p5_specs.append([(lambda b_: b_[:], wsrc(w_in, 0, D, 4096 + c0_, 512))])
        p5_loaded = {}

        def p5_load(idx):
            for i_ in (idx, idx + 1):
                if i_ < 16 and i_ not in p5_loaded:
                    p5_loaded[i_] = ws.load(p5_specs[i_])
            return p5_loaded[idx]

        adaW = carve(BIG, 76 * KB, 8 * KB, BF16).rearrange("p (a b) -> p a b", a=16)
        adaW_free = [None]

        def ada_half(n0, dst):
            d_w = k.dma("pool", adaW, wsrc(w_ada, 0, D, n0, 256), deps=[adaW_free[0]])
            d_b = k.dma("sp", bada[:, 0:256], b_ada[:, n0:n0 + 256].broadcast_to([128, 256]), deps=[state["bada_free"]])
            bk, fr = banks.get()
            last = None
            for kc in range(16):
                last = k.op("pe", lambda e, kc=kc, bk=bk: e.matmul(banks.f32(bk)[:, 0:256], lhsT=crep[:, kc, :], rhs=adaW[:, kc, :], start=(kc == 0), stop=(kc == 15)),
                            [fr, o_crep, d_w] if kc == 0 else (), signal=(kc == 15))
            adaW_free[0] = last
            o = k.op("dve", lambda e, bk=bk, dst=dst: e.tensor_tensor(out=dst, in0=banks.f32(bk)[:, 0:256], in1=bada[:, 0:256], op=ALU.add), [last, d_b, state["modbc_free"]])
            banks.rel(bk, o)
            state["bada_free"] = o
            return o

        def ada_half_a(vi, mrow, hb):
            o = ada_half(vi * D + hb * 256, modbc[:, 0:256])
            o2 = k.op("dve", lambda e: e.tensor_tensor(out=dtmp[:, 0:256].rearrange("p (a b) -> p a b", a=2), in0=modbc[:, 0:256].rearrange("p (a b) -> p a b", a=2),
                                                      in1=identf[:].unsqueeze(1).broadcast_to([128, 2, 128]), op=ALU.mult), [o])
            o3 = k.op("dve", lambda e: e.tensor_reduce(out=modA[:, mrow, 2 * hb:2 * hb + 2], in_=dtmp[:, 0:256].rearrange("p (a b) -> p a b", a=2), axis=AX.X, op=ALU.add), [o2])
            state["modbc_free"] = o3
            return o3

        def ada_p5(mb, step):
            if step == 0:
                for hb in (2 * mb, 2 * mb + 1):
                    g1_ops.append(ada_half(2 * D + hb * 256, slot[0][:, hb * 256:(hb + 1) * 256]))
            elif step == 1:
                for hb in (2 * mb, 2 * mb + 1):
                    o_sh2.append(ada_half_a(3, 2, hb))
            elif step == 2:
                o_sc2.append(ada_half_a(4, 3, 2 * mb))
            else:
                o_sc2.append(ada_half_a(4, 3, 2 * mb + 1))

        for mb in range(4):
            c0 = mb * 512
            buf, dops, bi = p5_load(4 * mb + 0)
            last = None
            stepA = {}
            for m in range(4):
                for half in range(2):
                    hs = slice(half * 512, (half + 1) * 512)
                    bv, fv = banks.get()
                    lv = mm_group(bv, fv, 8, lambda kc, m=m, buf=buf: buf[:, kc, m * 128:(m + 1) * 128], lambda kc, hs=hs: zT[:, kc, hs], dops + zt_ops)
                    bg, fg = banks.get()
                    last = mm_group(bg, fg, 8, lambda kc, m=m, buf=buf: buf[:, 8 + kc, m * 128:(m + 1) * 128], lambda kc, hs=hs: zT[:, kc, hs], ())
                    si = sidx[0] % 2; sidx[0] += 1
                    o1 = k.op("act", lambda e, bg=bg, si=si: e.activation(out=s1[si], in_=banks.f32(bg), func=AF.Sigmoid), [last, s1_free[si]])
                    banks.rel(bg, o1)
                    o2 = k.op("dve", lambda e, bv=bv, si=si, m=m, hs=hs: e.tensor_tensor(out=tmpA[:, m, hs], in0=banks.f32(bv), in1=s1[si], op=ALU.mult), [lv, o1, AB_free[0]])
                    banks.rel(bv, o2)
                    s1_free[si] = o2
                    stepA[(m, half)] = o2
            ws.release(bi, last)
            ada_p5(mb, 0)
            buf, dops, bi = p5_load(4 * mb + 1)
            for m in range(4):
                for half in range(2):
                    hs = slice(half * 512, (half + 1) * 512)
                    bg, fg = banks.get()
                    last = mm_group(bg, fg, 16, lambda kc, m=m, buf=buf: buf[:, kc, m * 128:(m + 1) * 128], lambda kc, hs=hs: HT[:, kc, hs], dops)
                    si = sidx[0] % 2; sidx[0] += 1
                    o1 = k.op("act", lambda e, bg=bg, si=si: e.activation(out=s1[si], in_=banks.f32(bg), func=AF.Sigmoid), [last, s1_free[si]])
                    banks.rel(bg, o1)
                    o2 = k.op("dve", lambda e, si=si, m=m, hs=hs: e.tensor_tensor(out=tmpA[:, m, hs], in0=tmpA[:, m, hs], in1=s1[si], op=ALU.mult), [o1, stepA[(m, half)]])
                    s1_free[si] = o2
                    stepA[(m, half)] = o2
            ws.release(bi, last)
            ada_p5(mb, 1)
            buf, dops, bi = p5_load(4 * mb + 2)
            stepB = {}
            for m in range(4):
                for half in range(2):
                    hs = slice(half * 512, (half + 1) * 512)
                    bg, fg = banks.get()
                    last = mm_group(bg, fg, 8, lambda kc, m=m, buf=buf: buf[:, kc, m * 128:(m + 1) * 128], lambda kc, hs=hs: qsT[:, kc, hs], dops + qs_ops)
                    o1 = k.op("act", lambda e, bg=bg, m=m, hs=hs: e.activation(out=tmpB[:, m, hs], in_=banks.f32(bg), func=AF.Copy), [last, AB_free[0]])
                    banks.rel(bg, o1)
                    stepB[(m, half)] = o1
            ws.release(bi, last)
            ada_p5(mb, 2)
            buf, dops, bi = p5_load(4 * mb + 3)
            fin = []
            for m in range(4):
                for half in range(2):
                    hs = slice(half * 512, (half + 1) * 512)
                    bg, fg = banks.get()
                    last = mm_group(bg, fg, 16, lambda kc, m=m, buf=buf: buf[:, kc, m * 128:(m + 1) * 128], lambda kc, hs=hs: HT[:, kc, hs], dops)
                    si = sidx[0] % 2; sidx[0] += 1
                    o1 = k.op("act", lambda e, bg=bg, si=si: e.activation(out=s1[si], in_=banks.f32(bg), func=AF.Sigmoid), [last, s1_free[si]])
                    banks.rel(bg, o1)
                    o2 = k.op("dve", lambda e, si=si, m=m, hs=hs: e.tensor_tensor(out=s1[si], in0=s1[si], in1=tmpB[:, m, hs], op=ALU.mult), [o1, stepB[(m, half)]])
                    o3 = k.op("dve", lambda e, si=si, m=m, hs=hs, mb=mb: e.tensor_tensor(out=merged[:, 4 * mb + m, hs], in0=s1[si], in1=tmpA[:, m, hs], op=ALU.add), [o2, stepA[(m, half)]])
                    s1_free[si] = o3
                    fin.append(o3)
            ws.release(bi, last)
            AB_free[0] = fin
            ada_p5(mb, 3)
        k.barrier()

        if debug == "merged":
            dd = carve(BIG, 0, 64 * KB, F32)
            o1 = k.op("dve", lambda e: e.tensor_copy(out=dd, in_=merged.rearrange("p m n -> p (m n)")))
            dump(dd, 0, [o1])
            finish()
            return nc

        xres = carve(BIG, 0, 64 * KB, F32).rearrange("p (j d) -> p j d", j=8)
        d_xr = [k.dma("sp", xres[:, j, :], xov[:, j, :]) for j in range(8)]
        HTf = HT[:].rearrange("p a b -> p (a b)")
        tq = [HTf[:, i * 1024:(i + 1) * 1024].bitcast(F32) for i in range(4)]
        tq_free = [None] * 4
        tix = [0]
        for nb in range(4):
            ns = slice(nb * 512, (nb + 1) * 512)
            buf, dops, bi = ws.load([(lambda b_: b_[:], wsrc(w_out, 0, D, nb * 512, 512))])
            last = None
            for j in range(8):
                js = slice(j * 128, (j + 1) * 128)
                bk, fr_ = banks.get()
                last = mm_group(bk, fr_, 16, lambda m, js=js: merged[:, m, js], lambda m, buf=buf: buf[:, m, :], dops)
                ti = tix[0] % 4; tix[0] += 1
                o1 = k.op("dve", lambda e, bk=bk, ti=ti, ns=ns: e.tensor_tensor(out=tq[ti], in0=banks.f32(bk), in1=slot[0][:, ns], op=ALU.mult), [last, tq_free[ti]] + g1_ops)
                banks.rel(bk, o1)
                o2 = k.op("dve", lambda e, ti=ti, j=j, ns=ns: e.scalar_tensor_tensor(out=xres[:, j, ns], in0=xres[:, j, ns], scalar=ALPHA, in1=tq[ti], op0=ALU.mult, op1=ALU.add), [o1, d_xr[j]])
                tq_free[ti] = o2
            ws.release(bi, last)
        k.barrier()

        oa = k.op("dve", lambda e: e.tensor_scalar(out=G2[:], in0=modA[:, 3, :], scalar1=1.0, scalar2=None, op0=ALU.add), o_sc2)
        ob = k.op("dve", lambda e: e.tensor_tensor(out=B2[:], in0=lnA[:, 1, :], in1=G2[:], op=ALU.mult), [oa, d_lnA1])
        ob = k.op("dve", lambda e: e.tensor_tensor(out=B2[:], in0=B2[:], in1=modA[:, 2, :], op=ALU.add), [ob] + o_sh2)
        oa = k.op("dve", lambda e: e.tensor_tensor(out=G2[:], in0=G2[:], in1=lnA[:, 0, :], op=ALU.mult), [ob, d_lnA0])
        d_l1 = k.dma("sp", slot[1], ln1_g.broadcast_to([128, D]))
        d_l2 = k.dma("sp", slot[2], ln1_b.broadcast_to([128, D]))
        o_l1 = k.op("dve", lambda e: e.tensor_scalar(out=slot[1], in0=slot[1], scalar1=ALPHA, scalar2=None, op0=ALU.mult), [d_l1])
        o_l2 = k.op("dve", lambda e: e.tensor_scalar(out=slot[2], in0=slot[2], scalar1=ALPHA, scalar2=None, op0=ALU.mult), [d_l2])
        nb2 = [carve(RB, 0, 4 * KB, BF16), carve(RB, 4 * KB, 4 * KB, BF16)]
        nb2_free = [None, None]
        g2_ops = []
        def ln7_part(j):
            b2 = j % 2
            o_st = ln_stats(j, lambda q, j=j: xres[:, j, q * 512:(q + 1) * 512], [])
            o_n = k.op("dve", lambda e, j=j: e.tensor_scalar(out=xres[:, j, :], in0=xres[:, j, :], scalar1=rstd[:, j:j + 1], scalar2=nmr[:, j:j + 1],
                                                          op0=ALU.mult, op1=ALU.add), [o_st])
            return k.op("act", lambda e, j=j, b2=b2: e.activation(out=nb2[b2], in_=xres[:, j, :], func=AF.Copy), [o_n, nb2_free[b2]])

        o_cs = {0: ln7_part(0)}
        for j in range(8):
            b2 = j % 2
            if j + 1 < 8:
                o_cs[j + 1] = ln7_part(j + 1)
            o_c = o_cs[j]
            evs = transposes_to_hT(nb2[b2], lambda kc, j=j: HT[:, kc, j * 128:(j + 1) * 128], G2, B2, [o_c, oa, ob], all_dve=True)
            nb2_free[b2] = k.last("pe")
            if j % 2 == 1:
                blk = j // 2
                g2_ops.append(ada_block(5 * D + blk * 512, dst=slot[0][:, blk * 512:(blk + 1) * 512]))
        k.barrier()

        aT = carve(RB, 0, 32 * KB, BF16).rearrange("p (f n) -> p f n", f=16)
        tf = [slot[1][:, i * 512:(i + 1) * 512] for i in range(4)]
        tf_free = [None] * 4
        rl = [carve(RB, 32 * KB, 1024, BF16), carve(RB, 33 * KB, 1024, BF16)]
        rl_free = [None] * 2
        r2_ops = {}
        rix = [0]
        a_free = [None]
        for qd in range(4):
            a_ops = []
            for fb in range(4):
                buf, dops, bi = ws.load([(lambda b_: b_[:], wsrc(w_ff1, 0, D, qd * 2048 + fb * 512, 512))])
                last = None
                for f4 in range(4):
                    f = fb * 4 + f4
                    for half in range(2):
                        hs = slice(half * 512, (half + 1) * 512)
                        bk, fr_ = banks.get()
                        last = mm_group(bk, fr_, 16, lambda kc, f4=f4, buf=buf: buf[:, kc, f4 * 128:(f4 + 1) * 128], lambda kc, hs=hs: HT[:, kc, hs], dops)
                        ri = rix[0] % 2; rix[0] += 1
                        o1 = k.op("act", lambda e, bk=bk, ri=ri: e.activation(out=rl[ri], in_=banks.f32(bk), func=AF.Relu), [last, rl_free[ri]])
                        banks.rel(bk, o1)
                        o2 = k.op("dve", lambda e, ri=ri, f=f, hs=hs: e.tensor_tensor(out=aT[:, f, hs], in0=rl[ri], in1=rl[ri], op=ALU.mult), [o1, a_free[0]])
                        rl_free[ri] = o2
                        a_ops.append(o2)
                ws.release(bi, last)
                if qd == 0:
                    for j in (2 * fb, 2 * fb + 1):
                        o_r = k.op("dve", lambda e, j=j: e.tensor_tensor(out=xres[:, j, :], in0=xres[:, j, :], in1=slot[1], op=ALU.mult), [o_l1])
                        r2_ops[j] = k.op("dve", lambda e, j=j: e.tensor_tensor(out=xres[:, j, :], in0=xres[:, j, :], in1=slot[2], op=ALU.add), [o_r, o_l2])
                    if fb == 3:
                        tf_free = [list(r2_ops.values())] * 4
            lastq = []
            for nb in range(4):
                ns = slice(nb * 512, (nb + 1) * 512)
                buf, dops, bi = ws.load([(lambda b_: b_[:], wsrc(w_ff2, qd * 2048, 2048, nb * 512, 512))])
                last = None
                for j in range(8):
                    js = slice(j * 128, (j + 1) * 128)
                    bk, fr_ = banks.get()
                    last = mm_group(bk, fr_, 16, lambda f, js=js: aT[:, f, js], lambda f, buf=buf: buf[:, f, :], dops + a_ops)
                    ti = tix[0] % 4; tix[0] += 1
                    o1 = k.op("dve", lambda e, bk=bk, ti=ti, ns=ns: e.tensor_tensor(out=tf[ti], in0=banks.f32(bk), in1=slot[0][:, ns], op=ALU.mult), [last, tf_free[ti]] + g2_ops)
                    banks.rel(bk, o1)
                    o2 = k.op("dve", lambda e, ti=ti, j=j, ns=ns: e.tensor_tensor(out=xres[:, j, ns], in0=xres[:, j, ns], in1=tf[ti], op=ALU.add), [o1, r2_ops[j]])
                    tf_free[ti] = o2
                lastq.append(last)
                ws.release(bi, last)
            a_free[0] = lastq
        k.barrier()

        d_l1 = k.dma("sp", slot[1], ln2_g.broadcast_to([128, D]))
        d_l2 = k.dma("sp", slot[2], ln2_b.broadcast_to([128, D]))
        outv = out_d.rearrange("(c j) d -> c j d", j=8)
        outs = []
        for j in range(8):
            o_st = ln_stats(j, lambda q, j=j: xres[:, j, q * 512:(q + 1) * 512], [])
            o_n = k.op("act", lambda e, j=j: e.activation(out=xres[:, j, :], in_=xres[:, j, :], func=AF.Identity, scale=rstd[:, j:j + 1], bias=nmr[:, j:j + 1]), [o_st])
            o_g = k.op("pool", lambda e, j=j: e.tensor_tensor(out=xres[:, j, :], in0=xres[:, j, :], in1=slot[1], op=ALU.mult), [o_n, d_l1])
            o_b = k.op("dve", lambda e, j=j: e.tensor_tensor(out=xres[:, j, :], in0=xres[:, j, :], in1=slot[2], op=ALU.add), [o_g, d_l2])
            outs.append(k.dma("sp", outv[:, j, :], xres[:, j, :], deps=[o_b]))
        k.wait_all("sp", outs)
        dbg_ops.clear()
        finish()
        return nc
    return nc


def _consts(half):
    identf = np.eye(128, dtype=np.float32)
    ih = np.arange(128) // 16
    maskT = (ih[None, :] >= ih[:, None]).astype(np.float32)
    kq = np.arange(1, 9, dtype=np.float32)
    kr = np.arange(7, -1, -1).astype(np.float32)
    kp = -np.arange(1, 9, dtype=np.float32)
    kvec = np.tile(np.concatenate([kq, kr, kp])[None, :], (128, 1)).astype(np.float32)
    iota = np.tile(np.arange(129, dtype=np.float32)[None, :], (128, 1))
    jj, cc = np.meshgrid(np.arange(8), np.arange(128), indexing="ij")
    pos = (half * T + 8 * cc + jj).reshape(-1).astype(np.float32)
    rc = np.stack([1.0 / np.minimum(pos + 1.0, float(w)) for w in (2, 4, 8, 16)]).astype(np.float32)
    return dict(identf=identf, maskT=maskT, kvec=kvec, iota=iota, rc=rc)


def make_in_maps(inputs, cores=range(8)):
    f = lambda a: np.ascontiguousarray(np.asarray(a, dtype=np.float32))
    x = f(inputs["x"])
    shared = dict(
        w_ada=f(inputs["w_ada"][0]), b_ada=f(inputs["b_ada"][0]).reshape(1, -1), w_in=f(inputs["w_in"][0]),
        lam_re=f(inputs["lam_re"][0]), lam_im=f(inputs["lam_im"][0]), log_dt=f(inputs["log_dt"][0]),
        ssm_b_re=f(inputs["ssm_b_re"][0]), ssm_b_im=f(inputs["ssm_b_im"][0]),
        ssm_c_re=f(inputs["ssm_c_re"][0]), ssm_c_im=f(inputs["ssm_c_im"][0]), ssm_d=f(inputs["ssm_d"][0]),
        w_glu_val=f(inputs["w_glu_val"][0]), w_glu_gate=f(inputs["w_glu_gate"][0]),
        w_pool=f(inputs["w_pool"][0]).reshape(1024, 256), pool_scale=f(inputs["pool_scale"][0]),
        w_pool_out=f(inputs["w_pool_out"][0]), w_out=f(inputs["w_out"][0]),
        ln1_g=f(inputs["ln1_g"][0]).reshape(1, -1), ln1_b=f(inputs["ln1_b"][0]).reshape(1, -1),
        w_ff1=f(inputs["w_ff1"][0]), w_ff2=f(inputs["w_ff2"][0]),
        ln2_g=f(inputs["ln2_g"][0]).reshape(1, -1), ln2_b=f(inputs["ln2_b"][0]).reshape(1, -1),
    )
    consts = [_consts(0), _consts(1)]
    zeros = np.zeros((T, D), np.float32)
    maps = []
    for core in cores:
        b, half = core // 2, core % 2
        m = dict(shared)
        m.update(consts[half])
        m["x_own"] = np.ascontiguousarray(x[b, half * T:(half + 1) * T])
        m["x_prev"] = np.ascontiguousarray(x[b, 0:T]) if half == 1 else zeros
        m["c_row"] = f(inputs["c"][b])
        m["flag"] = np.full((128, 1), float(half), np.float32)
        maps.append(m)
    return maps


def kernel(**inputs):
    nc = build()
    maps = make_in_maps(inputs)
    res = run_bass_kernel_spmd(nc, maps, core_ids=list(range(8)))
    out = np.empty((4, 2 * T, D), np.float32)
    for core in range(8):
        b, half = core // 2, core % 2
        out[b, half * T:(half + 1) * T] = np.asarray(res.results[core]["out"], dtype=np.float32)
    return out
```
